# Optimizing a Trainium2 kernel written in Bass

```python
import math
import jax, jax.numpy as jnp
from jax import lax
import numpy as np

D_MODEL = 1024
BATCH = 4
SEQ = 8192
DEPTH = 2

HEAD_DIM = 64
BRANCH_WIDTH = 512
N_BRANCH = 3
CONV_GROUPS = BRANCH_WIDTH // HEAD_DIM
RWKV_HEADS = BRANCH_WIDTH // HEAD_DIM
ATTN_HEADS = BRANCH_WIDTH // HEAD_DIM
CONV_WIDTH = 3
DECAY_RANK = 64
ICLR_RANK = 64
GATE_RANK = 128
D_FF = 2816
Q_BLOCK = 128
NORM_EPS = 1e-6
GN_EPS = 64e-5
DECAY_SCALE = math.exp(-0.5)
FORGET_BIAS_INIT = 2.0

CONV_COLS = 3 * BRANCH_WIDTH
RWKV_COLS = 3 * BRANCH_WIDTH + DECAY_RANK + ICLR_RANK + GATE_RANK
ATTN_COLS = 3 * BRANCH_WIDTH + ATTN_HEADS
GATE_COLS = N_BRANCH * D_MODEL
N_IN = CONV_COLS + RWKV_COLS + ATTN_COLS + GATE_COLS

kernel_name = "hybrid_conv_rwkv7_fox_block"

F32 = jnp.float32


def rms_norm(x, g):
    xf = x.astype(F32)
    y = xf * lax.rsqrt(jnp.mean(xf * xf, axis=-1, keepdims=True) + NORM_EPS)
    return (y * g.astype(F32)).astype(x.dtype)


def causal_dwconv(x, w):
    k_w, c = w.shape
    return lax.conv_general_dilated(
        x, w[:, None, :].astype(x.dtype), window_strides=(1,),
        padding=[(k_w - 1, 0)], dimension_numbers=("NWC", "WIO", "NWC"),
        feature_group_count=c)


def token_shift(z):
    return jnp.concatenate([jnp.zeros_like(z[:, :1]), z[:, :-1]], axis=1)


def conv_mixer(z, w_conv):
    b_gate, c_gate, h = jnp.split(z, 3, axis=-1)
    return b_gate * causal_dwconv(c_gate * h, w_conv)


def rwkv7_mixer(z, mu, w0, w_up, a0, a_up, g_up, k_k, k_a, r_k, gn_g, gn_b):
    bsz, t_len, _ = z.shape
    w_ = BRANCH_WIDTH
    z = z + mu * (token_shift(z) - z)
    r, k, v, wd, ad, gd = jnp.split(
        z, [w_, 2 * w_, 3 * w_, 3 * w_ + DECAY_RANK, 3 * w_ + DECAY_RANK + ICLR_RANK], axis=-1)
    decay = jnp.exp(-DECAY_SCALE * jax.nn.sigmoid((w0 + jnp.tanh(wd) @ w_up).astype(F32)))
    a = jax.nn.sigmoid(a0 + ad @ a_up)
    g = jax.nn.sigmoid(gd) @ g_up
    heads = lambda t: t.astype(F32).reshape(bsz, t_len, RWKV_HEADS, HEAD_DIM)
    kappa = heads(k * k_k)
    kappa_hat = kappa * lax.rsqrt(jnp.sum(kappa * kappa, axis=-1, keepdims=True) + 1e-12)
    k_tilde = heads(k * (1.0 + (a - 1.0) * k_a))
    r_h, v_h, a_h, w_h = heads(r), heads(v), heads(a), heads(decay)

    def step(s, inp):
        r_t, w_t, k_t, v_t, kh_t, a_t = inp
        sk = jnp.einsum("bhvk,bhk->bhv", s, kh_t)
        s = (s * w_t[:, :, None, :] - sk[..., None] * (a_t * kh_t)[:, :, None, :]
             + v_t[..., None] * k_t[:, :, None, :])
        return s, jnp.einsum("bhvk,bhk->bhv", s, r_t)

    xs = tuple(t.transpose(1, 0, 2, 3) for t in (r_h, w_h, k_tilde, v_h, kappa_hat, a_h))
    s0 = jnp.zeros((bsz, RWKV_HEADS, HEAD_DIM, HEAD_DIM), F32)
    _, y = lax.scan(step, s0, xs)
    y = y.transpose(1, 0, 2, 3)
    mean = jnp.mean(y, axis=-1, keepdims=True)
    var = jnp.mean((y - mean) ** 2, axis=-1, keepdims=True)
    y = (y - mean) * lax.rsqrt(var + GN_EPS)
    y = y.reshape(bsz, t_len, w_) * gn_g.astype(F32) + gn_b.astype(F32)
    bonus = jnp.sum(r_h * k_tilde * r_k.astype(F32), axis=-1, keepdims=True) * v_h
    out = (y + bonus.reshape(bsz, t_len, w_)) * g.astype(F32)
    return out.astype(z.dtype)


def forgetting_attention(z, b_f):
    bsz, t_len, _ = z.shape
    w_ = BRANCH_WIDTH
    q, k, v, f_logit = jnp.split(z, [w_, 2 * w_, 3 * w_], axis=-1)
    to_heads = lambda t: t.reshape(bsz, t_len, ATTN_HEADS, HEAD_DIM).transpose(0, 2, 1, 3)
    q, k, v = to_heads(q), to_heads(k), to_heads(v)
    log_f = jax.nn.log_sigmoid((f_logit + b_f).astype(F32))
    c = jnp.cumsum(log_f, axis=1).transpose(0, 2, 1)
    scale = HEAD_DIM ** -0.5
    outs = []
    for i in range(t_len // Q_BLOCK):
        lo, hi = i * Q_BLOCK, (i + 1) * Q_BLOCK
        s = jnp.einsum("bhqd,bhkd->bhqk", q[:, :, lo:hi], k[:, :, :hi]).astype(F32) * scale
        s = s + c[:, :, lo:hi, None] - c[:, :, None, :hi]
        causal = jnp.arange(hi)[None, :] <= jnp.arange(lo, hi)[:, None]
        s = jnp.where(causal, s, -jnp.inf)
        p = jax.nn.softmax(s, axis=-1).astype(v.dtype)
        outs.append(jnp.einsum("bhqk,bhkd->bhqd", p, v[:, :, :hi]))
    o = jnp.concatenate(outs, axis=2)
    return o.transpose(0, 2, 1, 3).reshape(bsz, t_len, w_)


def hybrid_mixer(xn, w_in, gate_b, conv_mix_w, rwkv_mu, rwkv_w0, rwkv_w_up, rwkv_a0,
                 rwkv_a_up, rwkv_g_up, rwkv_k_k, rwkv_k_a, rwkv_r_k, rwkv_gn_g, rwkv_gn_b,
                 attn_forget_b, w_branch, w_o):
    z = xn @ w_in
    z_conv, z_rwkv, z_attn, z_gate = jnp.split(
        z, [CONV_COLS, CONV_COLS + RWKV_COLS, CONV_COLS + RWKV_COLS + ATTN_COLS], axis=-1)
    y_a = conv_mixer(z_conv, conv_mix_w)
    y_b = rwkv7_mixer(z_rwkv, rwkv_mu, rwkv_w0, rwkv_w_up, rwkv_a0, rwkv_a_up, rwkv_g_up,
                      rwkv_k_k, rwkv_k_a, rwkv_r_k, rwkv_gn_g, rwkv_gn_b)
    y_c = forgetting_attention(z_attn, attn_forget_b)
    g_a, g_b, g_c = jnp.split(jax.nn.sigmoid(z_gate + gate_b), 3, axis=-1)
    merged = g_a * (y_a @ w_branch[0]) + g_b * (y_b @ w_branch[1]) + g_c * (y_c @ w_branch[2])
    return merged @ w_o


def conv_ffn(xn, w_up, w_conv, w_down):
    h = causal_dwconv(xn @ w_up, w_conv)
    g, u = jnp.split(h, 2, axis=-1)
    return (jax.nn.silu(g) * u) @ w_down


def setup_inputs(seed: int = 0) -> dict:
    key = jax.random.key(seed)
    ks = jax.random.split(key, 24)
    L, D, W, F = DEPTH, D_MODEL, BRANCH_WIDTH, D_FF
    nrm = lambda k, shape, s: s * jax.random.normal(k, shape, F32)
    return {
        "x": nrm(ks[0], (BATCH, SEQ, D), 1.0),
        "norm1_g": 1.0 + nrm(ks[1], (L, D), 0.05),
        "w_in": nrm(ks[2], (L, D, N_IN), D ** -0.5),
        "gate_b": nrm(ks[3], (L, GATE_COLS), 0.1),
        "conv_mix_w": nrm(ks[4], (L, CONV_WIDTH, W), CONV_WIDTH ** -0.5),
        "rwkv_mu": jax.random.uniform(ks[5], (L, RWKV_COLS), F32),
        "rwkv_w0": nrm(ks[6], (L, W), 0.5),
        "rwkv_w_up": nrm(ks[7], (L, DECAY_RANK, W), DECAY_RANK ** -0.5),
        "rwkv_a0": nrm(ks[8], (L, W), 0.2),
        "rwkv_a_up": nrm(ks[9], (L, ICLR_RANK, W), ICLR_RANK ** -0.5),
        "rwkv_g_up": nrm(ks[10], (L, GATE_RANK, W), GATE_RANK ** -0.5),
        "rwkv_k_k": 0.85 + nrm(ks[11], (L, W), 0.05),
        "rwkv_k_a": 1.0 + nrm(ks[12], (L, W), 0.05),
        "rwkv_r_k": nrm(ks[13], (L, RWKV_HEADS, HEAD_DIM), 0.1),
        "rwkv_gn_g": 1.0 + nrm(ks[14], (L, W), 0.05),
        "rwkv_gn_b": nrm(ks[15], (L, W), 0.02),
        "attn_forget_b": FORGET_BIAS_INIT + nrm(ks[16], (L, ATTN_HEADS), 0.1),
        "w_branch": nrm(ks[17], (L, N_BRANCH, W, D), W ** -0.5),
        "w_o": nrm(ks[18], (L, D, D), D ** -0.5),
        "norm2_g": 1.0 + nrm(ks[19], (L, D), 0.05),
        "ffn_w_up": nrm(ks[20], (L, D, 2 * F), D ** -0.5),
        "ffn_conv_w": nrm(ks[21], (L, CONV_WIDTH, 2 * F), CONV_WIDTH ** -0.5),
        "ffn_w_down": nrm(ks[22], (L, F, D), F ** -0.5),
        "final_norm_g": 1.0 + nrm(ks[23], (D,), 0.05),
    }


def reference(x, norm1_g, w_in, gate_b, conv_mix_w, rwkv_mu, rwkv_w0, rwkv_w_up, rwkv_a0,
              rwkv_a_up, rwkv_g_up, rwkv_k_k, rwkv_k_a, rwkv_r_k, rwkv_gn_g, rwkv_gn_b,
              attn_forget_b, w_branch, w_o, norm2_g, ffn_w_up, ffn_conv_w, ffn_w_down,
              final_norm_g):
    for l in range(DEPTH):
        x = x + hybrid_mixer(
            rms_norm(x, norm1_g[l]), w_in[l], gate_b[l], conv_mix_w[l], rwkv_mu[l],
            rwkv_w0[l], rwkv_w_up[l], rwkv_a0[l], rwkv_a_up[l], rwkv_g_up[l], rwkv_k_k[l],
            rwkv_k_a[l], rwkv_r_k[l], rwkv_gn_g[l], rwkv_gn_b[l], attn_forget_b[l],
            w_branch[l], w_o[l])
        x = x + conv_ffn(rms_norm(x, norm2_g[l]), ffn_w_up[l], ffn_conv_w[l], ffn_w_down[l])
    return rms_norm(x, final_norm_g)
```

```python
import contextlib
import math
import os
CUT = int(os.environ.get("K_CUT", "0"))
SUB = int(os.environ.get("K_SUB", "0"))
import numpy as np
import concourse.bass as bass
import concourse.mybir as mybir
from concourse.bass_utils import run_bass_kernel_spmd

F32 = mybir.dt.float32
BF16 = mybir.dt.bfloat16
AF = mybir.ActivationFunctionType
ALU = mybir.AluOpType

ENGS = ("pe", "act", "dve", "pool", "sp")

D = 1024
NIN = 7944
FF = 2816
SEQ = 8192
NB = 4
DEPTH = 2
DS = math.exp(-0.5)

PCO = {}
_o = 0
for _n, _w in (("n1g", 8), ("n2g", 8), ("gateb", 24), ("cmw", 12), ("fcw", 132), ("mu", 14),
               ("w0", 4), ("a0", 4), ("kk", 4), ("ka", 4), ("rk", 4), ("fb", 1), ("gng8", 8), ("gnb8", 8), ("gng", 4), ("gnb", 4)):
    PCO[_n] = _o
    _o += _w
NPC = _o


class Buf:
    __slots__ = ("t", "name", "last_w", "readers", "chan", "last_dma")

    def __init__(self, t, name):
        self.t = t
        self.name = name
        self.last_w = []
        self.readers = []
        self.chan = None
        self.last_dma = None

    def __getitem__(self, k):
        return self.t[k]


class Node:
    __slots__ = ("id", "eng", "fns", "deps", "dur", "occ", "kind", "chan", "ev", "open", "kw")

    def __init__(self, id, eng, kind):
        self.id = id
        self.eng = eng
        self.kind = kind
        self.fns = []
        self.deps = set()
        self.dur = 0.0
        self.occ = 0.0
        self.chan = None
        self.ev = None
        self.open = False


def _nfree(ap):
    n = 1
    for d in ap.shape[1:]:
        n *= d
    return n


class Sched:
    NCHAN = 64
    XLAT = float(os.environ.get("K_XLAT", "0.3"))

    def __init__(self, nc, stack):
        self.nc = nc
        self.gstack = stack
        self.stack = stack
        self.q = {e: [] for e in ENGS}
        self.cnt = {e: 0 for e in ENGS}
        self.sems = {}
        for e in ENGS:
            self.sems[e] = stack.enter_context(nc.semaphore("s_" + e))
        self.chan_cnt = {}
        for i in range(self.NCHAN):
            k = f"d{i}"
            self.sems[k] = stack.enter_context(nc.semaphore("s_" + k))
            self.chan_cnt[k] = 0
        self.chan_next = 0
        self.known = {e: {} for e in ENGS}
        self.nbuf = 0
        self.ninstr = 0
        self.nodes = []
        self.pe_open = None
        self.sb_bytes = 0
        self.reorder = True

    def sbuf(self, shape, dtype, name=None):
        self.nbuf += 1
        name = f"{name or 'sb'}_{self.nbuf}"
        nb = int(np.prod(shape[1:])) * (2 if dtype == BF16 else 4)
        self.sb_bytes += ((nb + 31) // 32) * 32
        return Buf(self.stack.enter_context(self.nc.sbuf_tensor(name, list(shape), dtype)), name)

    def psum(self, shape, dtype, name=None):
        self.nbuf += 1
        name = f"{name or 'ps'}_{self.nbuf}"
        return Buf(self.stack.enter_context(self.nc.psum_tensor(name, list(shape), dtype)), name)

    def _chan(self, b):
        if b.chan is None:
            assert self.chan_next < self.NCHAN, "out of dma channels"
            b.chan = f"d{self.chan_next}"
            self.chan_next += 1
        return b.chan

    def _close_pe(self):
        if self.pe_open is not None:
            self.pe_open.open = False
            self.pe_open = None

    def _deps_of(self, reads, writes):
        deps = set()
        for r in reads:
            deps.update(r.last_w)
        for w in writes:
            deps.update(w.last_w)
            deps.update(w.readers)
        return deps

    def _touch(self, node, reads, writes):
        for r in reads:
            if not r.readers or r.readers[-1] is not node:
                r.readers.append(node)
        for w in writes:
            w.last_w = [node]
            w.readers = []

    def op(self, eng, fn, reads=(), writes=(), signal=True, same_ok=False, dur=0.3):
        self.ninstr += 1
        deps = self._deps_of(reads, writes)
        if eng == "pe":
            node = self.pe_open
            if node is None:
                node = Node(len(self.nodes), "pe", "op")
                self.nodes.append(node)
                node.open = True
                self.pe_open = node
            deps.discard(node)
            node.deps |= deps
            node.fns.append(fn)
            node.dur += dur
            node.occ += dur
            if signal:
                self._close_pe()
        else:
            self._close_pe()
            node = Node(len(self.nodes), eng, "op")
            self.nodes.append(node)
            node.deps = deps
            node.fns.append(fn)
            node.dur = dur
            node.occ = dur
        self._touch(node, reads, writes)
        return node

    def dma(self, eng, out_ap, in_ap, reads=(), writes=(), **kw):
        self._close_pe()
        self.ninstr += 1
        cb = writes[0] if writes else reads[0]
        key = self._chan(cb)
        node = Node(len(self.nodes), eng, "dma")
        self.nodes.append(node)
        node.chan = key
        node.deps = self._deps_of(reads, writes)
        if cb.last_dma is not None:
            node.deps.add(cb.last_dma)
        cb.last_dma = node
        nbytes = _nfree(out_ap) * out_ap.shape[0] * (2 if out_ap.dtype == BF16 else 4)
        node.dur = 2.0 + nbytes / 1.0e5
        node.occ = 0.1 if eng == "sp" else 0.6
        node.fns.append((out_ap, in_ap, kw))
        self._touch(node, reads, writes)
        return node

    def _schedule(self):
        import heapq
        nodes = self.nodes
        if not self.reorder:
            return list(nodes)
        succ = {n.id: [] for n in nodes}
        indeg = {}
        alive = {n.id for n in nodes}
        for n in nodes:
            n.deps = {d for d in n.deps if d.id in alive and d.ev is None}
            indeg[n.id] = len(n.deps)
            for d in n.deps:
                succ[d.id].append(n)
        tail = {}
        for n in reversed(nodes):
            t = 0.0
            for s_ in succ[n.id]:
                v = tail[s_.id] + self.XLAT
                if v > t:
                    t = v
            tail[n.id] = t + n.dur
        est = {n.id: 0.0 for n in nodes}
        wait = {e: [] for e in ENGS}
        avail = {e: [] for e in ENGS}
        for n in nodes:
            if indeg[n.id] == 0:
                heapq.heappush(wait[n.eng], (0.0, n.id, n))
        free = {e: 0.0 for e in ENGS}
        order = []
        left = len(nodes)
        while left:
            best = None
            for e in ENGS:
                if avail[e]:
                    tc = free[e]
                elif wait[e]:
                    tc = max(free[e], wait[e][0][0])
                else:
                    continue
                if best is None or tc < best[0]:
                    best = (tc, e)
            assert best is not None, "dependency cycle in schedule"
            tc, e = best
            w = wait[e]
            av = avail[e]
            while w and w[0][0] <= tc:
                _, i_, n_ = heapq.heappop(w)
                heapq.heappush(av, (-tail[i_], i_, n_))
            _, _, n = heapq.heappop(av)
            free[e] = tc + n.occ
            fin = tc + n.dur
            order.append(n)
            left -= 1
            for s_ in succ[n.id]:
                if est[s_.id] < fin + self.XLAT:
                    est[s_.id] = fin + self.XLAT
                indeg[s_.id] -= 1
                if indeg[s_.id] == 0:
                    heapq.heappush(wait[s_.eng], (est[s_.id], s_.id, s_))
        return order

    def _need(self, eng, evs):
        kn = self.known[eng]
        out = {}
        for k, v in evs:
            if kn.get(k, 0) < v and out.get(k, 0) < v:
                out[k] = v
        for k, v in out.items():
            kn[k] = v
        return list(out.items())

    def _emit_waits(self, eng, waits):
        for k, v in waits:
            self.q[eng].append(lambda e, s=self.sems[k], v=v: e.wait_ge(s, v))

    def flush(self):
        self._close_pe()
        order = self._schedule()
        for n in order:
            evs = [d.ev for d in n.deps if d.ev is not None]
            eng = n.eng
            self._emit_waits(eng, self._need(eng, evs))
            if n.kind == "dma":
                key = n.chan
                self.chan_cnt[key] += 16
                n.ev = (key, self.chan_cnt[key])
                o, i, kw = n.fns[0]
                self.q[eng].append(
                    lambda e, o=o, i=i, s=self.sems[key], kw=kw: e.dma_start(out=o, in_=i, **kw).then_inc(s, 16))
            else:
                self.cnt[eng] += 1
                n.ev = (eng, self.cnt[eng])
                s = self.sems[eng]
                for fn in n.fns[:-1]:
                    self.q[eng].append(lambda e, fn=fn: fn(e))
                self.q[eng].append(lambda e, fn=n.fns[-1], s=s: fn(e).then_inc(s, 1))
        self.nodes = []

    def barrier(self):
        self.flush()
        deps = [(e, self.cnt[e]) for e in ENGS if self.cnt[e] > 0]
        deps += [(k, v) for k, v in self.chan_cnt.items() if v > 0]
        for e in ENGS:
            self._emit_waits(e, self._need(e, deps))

    def emit(self):
        nc = self.nc
        q = self.q
        with nc.Block() as block:
            @block.tensor
            def _(e):
                for f in q["pe"]:
                    f(e)

            @block.scalar
            def _(e):
                for f in q["act"]:
                    f(e)

            @block.vector
            def _(e):
                for f in q["dve"]:
                    f(e)

            @block.gpsimd
            def _(e):
                for f in q["pool"]:
                    f(e)

            @block.sync
            def _(e):
                for f in q["sp"]:
                    f(e)
        self.q = {e: [] for e in ENGS}

    @contextlib.contextmanager
    def phase(self, reorder=True, xlat=None):
        with contextlib.ExitStack() as ph:
            self.stack = ph
            self.chan_next = 0
            self.reorder = reorder
            self.XLAT = xlat if xlat is not None else Sched.XLAT
            yield
            self.barrier()
            self.emit()
        self.stack = self.gstack

    @staticmethod
    def _d(eng, ap):
        n = _nfree(ap)
        if eng == "act":
            return 0.22 + n / 1400.0
        if eng == "dve":
            return 0.12 + n / 1000.0
        return 0.3 + n / 600.0

    def A(self, out, in_, func, r, w, eng="act", **kw):
        return self.op("act", lambda e: e.activation(out=out, in_=in_, func=func, **kw), r, w, dur=self._d("act", out))

    def TT(self, eng, out, in0, in1, op, r, w):
        return self.op(eng, lambda e: e.tensor_tensor(out=out, in0=in0, in1=in1, op=op), r, w, dur=self._d(eng, out))

    def TS(self, eng, out, in0, s1, s2, op0, op1, r, w):
        if s2 is None:
            return self.op(eng, lambda e: e.tensor_scalar(out=out, in0=in0, scalar1=s1, scalar2=None, op0=op0), r, w,
                           dur=self._d(eng, out))
        return self.op(eng, lambda e: e.tensor_scalar(out=out, in0=in0, scalar1=s1, scalar2=s2, op0=op0, op1=op1), r, w,
                       dur=self._d(eng, out))

    def STT(self, eng, out, in0, sc, in1, op0, op1, r, w):
        return self.op(eng, lambda e: e.scalar_tensor_tensor(out=out, in0=in0, scalar=sc, in1=in1, op0=op0, op1=op1), r, w,
                       dur=self._d(eng, out))

    def CP(self, eng, out, in_, r, w):
        if eng == "act":
            return self.op("act", lambda e: e.copy(out=out, in_=in_), r, w, dur=self._d("act", out))
        return self.op(eng, lambda e: e.tensor_copy(out=out, in_=in_), r, w, dur=self._d(eng, out))

    def MS(self, eng, ap, val, w):
        return self.op(eng, lambda e: e.memset(ap, val), (), w, dur=self._d(eng, ap))

    def MM(self, out, lhsT, rhs, start, stop, r, w, signal=True):
        n = _nfree(rhs)
        d = 0.03 + n / 2400.0
        if rhs.dtype == F32:
            d *= 4
        return self.op("pe", lambda e: e.matmul(out, lhsT=lhsT, rhs=rhs, start=start, stop=stop), r, w,
                       signal=signal, dur=d)

    def TR(self, out, in_, ident, r, w, signal=True):
        return self.op("pe", lambda e: e.transpose(out=out, in_=in_, identity=ident), r, w, signal=signal, dur=0.1)


def bc(ap2, n):
    return ap2.unsqueeze(2).to_broadcast([ap2.shape[0], ap2.shape[1], n])


def build(T, depth=DEPTH, dbg=(), nph=99):
    nc = bass.Bass("TRN2", target_bir_lowering=False)
    NT = T // 512
    NKT = T // 128

    def dram(name, shape, dt, kind="Internal"):
        if name in dbg:
            kind = "ExternalOutput"
        return nc.dram_tensor(name, list(shape), dt, kind=kind).ap()

    x_in = dram("x", [T, D], F32, "ExternalInput")
    w_in = dram("w_in", [depth, D, NIN], F32, "ExternalInput")
    w_br = dram("w_branch", [depth, 3, 512, D], F32, "ExternalInput")
    w_o = dram("w_o", [depth, D, D], F32, "ExternalInput")
    w_up = dram("ffn_w_up", [depth, D, 2 * FF], F32, "ExternalInput")
    w_dn = dram("ffn_w_down", [depth, FF, D], F32, "ExternalInput")
    r_wup = dram("rwkv_w_up", [depth, 64, 512], F32, "ExternalInput")
    r_aup = dram("rwkv_a_up", [depth, 64, 512], F32, "ExternalInput")
    r_gup = dram("rwkv_g_up", [depth, 128, 512], F32, "ExternalInput")
    pcd = dram("pc", [depth, 128, NPC], F32, "ExternalInput")
    gfin = dram("final_norm_g", [D], F32, "ExternalInput")
    cst = dram("cst", [128, 128 * 6], F32, "ExternalInput")
    mk = dram("mk", [64, 4, 8, 128], F32, "ExternalInput")
    y_out = dram("y", [T, D], F32, "ExternalOutput")

    xa = dram("xa", [T, D], F32)
    xb = dram("xb", [T, D], F32)
    zc = dram("zc", [1536, T], BF16)
    zr = dram("zr", [1792, T], BF16)
    qT = dram("qT", [512, T], BF16)
    kT = dram("kT", [512, T], BF16)
    zf = dram("zf", [8, T], F32)
    gT = dram("gT", [3072, T], BF16)
    vtm = dram("vtm", [T, 528], BF16)
    cqk = dram("cqk", [8, 6, T], BF16)
    yaT = dram("yaT", [512, T], BF16)
    ycT = dram("ycT", [512, T], BF16)
    ybn = dram("ybn", [512, T], BF16)
    bsc = dram("bsc", [512, T], BF16)
    gsc = dram("gsc", [512, T], BF16)

    with contextlib.ExitStack() as gst:
        S = Sched(nc, gst)
        pcs = [S.sbuf([128, NPC], F32, f"pc{l}") for l in range(depth)]
        cf = S.sbuf([128, 768], F32, "cstf")
        identb = S.sbuf([128, 128], BF16, "identb")
        trib = S.sbuf([128, 128], BF16, "trib")
        o64b = S.sbuf([64, 64], BF16, "o64b")
        bdmb = S.sbuf([128, 128], BF16, "bdmb")
        omka = [S.sbuf([128, 4], F32, f"omka{l}") for l in range(depth)]
        nfb = [S.sbuf([128, 1], F32, f"nfb{l}") for l in range(depth)]
        with S.phase():
            for l in range(depth):
                S.dma("sp", pcs[l][:], pcd[l], writes=[pcs[l]])
            S.dma("sp", cf[:], cst, writes=[cf])
            S.CP("dve", identb[:], cf[:, 0:128], [cf], [identb])
            S.CP("dve", trib[:], cf[:, 128:256], [cf], [trib])
            S.CP("dve", o64b[:], cf[0:64, 384:448], [cf], [o64b])
            S.TS("dve", bdmb[:], cf[:, 256:384], 1.0 / 64, None, ALU.mult, None, [cf], [bdmb])
            for l in range(depth):
                o = PCO["ka"]
                S.TS("dve", omka[l][:], pcs[l][:, o:o + 4], -1.0, 1.0, ALU.mult, ALU.add, [pcs[l]], [omka[l]])
                o = PCO["fb"]
                S.TS("dve", nfb[l][:], pcs[l][:, o:o + 1], -1.0, None, ALU.mult, None, [pcs[l]], [nfb[l]])
        bdones = cf[:, 256:384]
        ones64 = cf[0:64, 384:448]
        onesf = cf[:, 512:640]

        def pcol(l, name, j=0, n=1):
            o = PCO[name] + j
            return pcs[l][:, o:o + n]

        def load_w_bf16(dst, dst_ap_fn, src_ap_fn, pieces, stages, engs=("act", "dve")):
            for i, pc_ in enumerate(pieces):
                st = stages[i % len(stages)]
                sv = st_view(st, pc_)
                S.dma("sp", sv, src_ap_fn(pc_), writes=[st])
                S.CP(engs[i % len(engs)], dst_ap_fn(pc_), sv, [st], [dst])

        def st_view(st, pc_):
            kc, n = pc_[2], pc_[1]
            return st[:, 0:kc * n].rearrange("p (k n) -> p k n", k=kc)

        def rmsnorm_T(xt, gcol, xnT, s, nb):
            ss, rs, junk, xs, pT = nb
            S.MS("pool", ss[:], 0.0, [ss])
            S.A(junk[:], xt[:], AF.Square, [xt], [junk, ss], accum_out=ss[:])
            S.A(rs[:], ss[:], AF.Sqrt, [ss], [rs], scale=1.0 / D, bias=1e-6)
            S.op("dve", lambda e: e.reciprocal(out=rs[:], in_=rs[:]), [rs], [rs])
            S.TS("dve", xs[:], xt[:], rs[:, 0:1], None, ALU.mult, None, [xt, rs], [xs])
            for kc in range(8):
                S.TR(pT[:, kc, :], xs[:, kc * 128:(kc + 1) * 128], identb[:], [xs, identb], [pT], signal=(kc == 7))
            S.TT("dve", xnT[:, :, s * 128:(s + 1) * 128], pT[:], bc(gcol, 128), ALU.mult, [pT], [xnT])

        def phase_inproj(l, xsrc):
            with S.phase():
                wres = S.sbuf([128, 8, NIN], BF16, "wres")
                stages = [S.sbuf([128, 2048], F32, f"stg{i}") for i in range(2)]
                pieces = [(c0, min(256, NIN - c0), 8) for c0 in range(0, NIN, 256)]
                load_w_bf16(wres, lambda p: wres[:, :, p[0]:p[0] + p[1]],
                            lambda p: w_in[l, :, p[0]:p[0] + p[1]].rearrange("(k q) n -> q k n", q=128),
                            pieces, stages)
                if CUT == 1:
                    return
                xbufs = [S.sbuf([128, D], F32, f"xb{i}") for i in range(2)]
                nbs = [(S.sbuf([128, 1], F32), S.sbuf([128, 1], F32), S.sbuf([128, D], BF16), S.sbuf([128, D], BF16),
                        S.psum([128, 8, 128], BF16)) for _ in range(2)]
                xnTs = [S.sbuf([128, 8, 512], BF16, f"xnT{i}") for i in range(2)]
                obs = [S.sbuf([128, 512], BF16, f"ob{i}") for i in range(4)]
                fsts = [S.sbuf([8, 512], F32, f"fst{i}") for i in range(2)]
                vsts = [S.sbuf([128, 8, 66], BF16, f"vst{i}") for i in range(2)]
                pss = [S.psum([128, 512], F32, f"psm{i}") for i in range(4)]
                for v in vsts:
                    S.MS("pool", v[:], 1.0, [v])
                chunks = []
                for j in range(12):
                    chunks.append((j * 128, 128, zc, j * 128, "copy"))
                for j in range(14):
                    chunks.append((1536 + j * 128, 128, zr, j * 128, "copy"))
                for j in range(4):
                    chunks.append((3328 + j * 128, 128, qT, j * 128, "qscale"))
                for j in range(4):
                    chunks.append((3840 + j * 128, 128, kT, j * 128, "copyA"))
                chunks.append((4864, 8, zf, 0, "f"))
                for j in range(24):
                    chunks.append((4872 + j * 128, 128, gT, j * 128, "gate"))
                cnt = 0
                for i in range(NT):
                    xnT = xnTs[i % 2]
                    for s in range(4):
                        xt = xbufs[(i * 4 + s) % 2]
                        r0 = (i * 4 + s) * 128
                        S.dma("sp", xt[:], xsrc[r0:r0 + 128, :], writes=[xt])
                        rmsnorm_T(xt, pcol(l, "n1g", 0, 8), xnT, s, nbs[(i * 4 + s) % 2])
                    if CUT == 2:
                        continue
                    for (c0, M, dest, row0, mode) in (chunks[:2] if CUT == 3 else chunks):
                        ps = pss[cnt % 4]
                        for kc in range(8):
                            S.MM(ps[0:M, :], wres[:, kc, c0:c0 + M], xnT[:, kc, :], kc == 0, kc == 7,
                                 [wres, xnT], [ps], signal=(kc == 7))
                        if mode == "f":
                            fs = fsts[i % 2]
                            S.CP("dve", fs[:], ps[0:8, :], [ps], [fs])
                            S.dma("pool", zf[:, i * 512:(i + 1) * 512], fs[:], reads=[fs])
                        else:
                            ob = obs[cnt % 4]
                            if mode == "copy":
                                S.CP("dve", ob[:], ps[:], [ps], [ob])
                            elif mode == "copyA":
                                S.CP("act", ob[:], ps[:], [ps], [ob])
                            elif mode == "qscale":
                                S.A(ob[:], ps[:], AF.Copy, [ps], [ob], scale=0.125)
                            else:
                                j = (c0 - 4872) // 128
                                S.A(ob[:], ps[:], AF.Sigmoid, [ps], [ob], bias=pcol(l, "gateb", j))
                            S.dma("pool", dest[row0:row0 + 128, i * 512:(i + 1) * 512], ob[:], reads=[ob])
                        cnt += 1
                    for s in range(4 if CUT not in (3, 4) else 0):
                        ps = pss[cnt % 4]
                        for kc in range(8):
                            S.MM(ps[:], xnT[:, kc, s * 128:(s + 1) * 128], wres[:, kc, 4352:4864], kc == 0, kc == 7,
                                 [wres, xnT], [ps], signal=(kc == 7))
                        vs = vsts[s % 2]
                        S.CP("act", vs[:, :, 0:64], ps[:].rearrange("p (h d) -> p h d", h=8), [ps], [vs])
                        r0 = (i * 4 + s) * 128
                        S.dma("pool", vtm[r0:r0 + 128, :], vs[:].rearrange("p h d -> p (h d)"), reads=[vs])
                        cnt += 1

        def phase_conv(l):
            TB = min(T, 2048)
            with S.phase():
                ins = [[S.sbuf([128, TB], BF16) for _ in range(3)] for _ in range(2)]
                hbs = [S.sbuf([128, TB + 2], F32) for _ in range(2)]
                acc = [S.sbuf([128, TB], F32) for _ in range(2)]
                outs = [S.sbuf([128, TB], BF16) for _ in range(2)]
                n = 0
                for j in range(4):
                    for tb in range(T // TB):
                        Bt, Ct, ht = ins[n % 2]
                        hb = hbs[n % 2]
                        ac = acc[n % 2]
                        ot = outs[n % 2]
                        cs = slice(tb * TB, (tb + 1) * TB)
                        S.dma("sp", Bt[:], zc[j * 128:(j + 1) * 128, cs], writes=[Bt])
                        S.dma("sp", Ct[:], zc[512 + j * 128:512 + (j + 1) * 128, cs], writes=[Ct])
                        S.dma("sp", ht[:], zc[1024 + j * 128:1024 + (j + 1) * 128, cs], writes=[ht])
                        if tb == 0:
                            S.MS("pool", hb[:, 0:2], 0.0, [hb])
                        else:
                            hp_ = hbs[(n - 1) % 2]
                            S.CP("pool", hb[:, 0:2], hp_[:, TB:TB + 2], [hp_], [hb])
                        S.TT("dve", hb[:, 2:TB + 2], Ct[:], ht[:], ALU.mult, [Ct, ht], [hb])
                        S.A(ac[:], hb[:, 2:TB + 2], AF.Copy, [hb], [ac], scale=pcol(l, "cmw", j * 3 + 2))
                        S.STT("dve", ac[:], hb[:, 1:TB + 1], pcol(l, "cmw", j * 3 + 1), ac[:], ALU.mult, ALU.add, [hb, ac], [ac])
                        S.STT("dve", ac[:], hb[:, 0:TB], pcol(l, "cmw", j * 3 + 0), ac[:], ALU.mult, ALU.add, [hb, ac], [ac])
                        S.TT("dve", ot[:], ac[:], Bt[:], ALU.mult, [ac, Bt], [ot])
                        S.dma("pool", yaT[j * 128:(j + 1) * 128, cs], ot[:], reads=[ot])
                        n += 1

        def phase_fcum(l):
            with S.phase():
                z = S.sbuf([8, T], F32)
                e1 = S.sbuf([8, T], F32)
                cs_ = S.sbuf([8, T], F32)
                r1 = z
                ob = [S.sbuf([8, T], BF16) for _ in range(6)]
                S.dma("sp", z[:], zf, writes=[z])
                S.A(e1[:], z[:], AF.Exp, [z], [e1], scale=-1.0, bias=nfb[l][0:8, 0:1])
                S.A(z[:], e1[:], AF.Ln, [e1], [z], bias=1.0)
                S.op("dve", lambda e: e.tensor_tensor_scan(out=cs_[:], data0=z[:], data1=z[:], initial=0.0,
                                                            op0=ALU.add, op1=ALU.bypass), [z], [cs_])
                S.A(ob[0][:], cs_[:], AF.Copy, [cs_], [ob[0]], scale=-1.0)
                S.STT("dve", r1[:], cs_[:], -1.0, ob[0][:], ALU.mult, ALU.subtract, [cs_, ob[0]], [r1])
                S.CP("act", ob[1][:], r1[:], [r1], [ob[1]])
                S.TT("dve", e1[:], r1[:], ob[1][:], ALU.subtract, [r1, ob[1]], [e1])
                S.CP("act", ob[2][:], e1[:], [e1], [ob[2]])
                for k in range(3):
                    S.A(ob[3 + k][:], ob[k][:], AF.Copy, [ob[k]], [ob[3 + k]], scale=-1.0)
                for k in range(6):
                    S.dma("pool", cqk[:, k, :], ob[k][:], reads=[ob[k]])

        def phase_attn(l):
            NQB = T // 512
            with S.phase():
                vext = S.sbuf([128, NKT, 528], BF16, "vext")
                VS = min(8, NKT)
                for a in range(0, NKT, VS):
                    S.dma("sp", vext[:, a:a + VS, :], vtm[a * 128:(a + VS) * 128, :].rearrange("(n p) c -> p n c", p=128),
                          writes=[vext])
                qas = [S.sbuf([70, T], BF16, f"qa{i}") for i in range(2)]
                kas = [S.sbuf([70, T], BF16, f"ka{i}") for i in range(2)]
                NSL = int(os.environ.get("K_NSL", "5"))
                LOOK = int(os.environ.get("K_LOOK", "3"))
                pts = [S.sbuf([128, 512], BF16, f"pt{i}") for i in range(NSL)]
                osb = [S.sbuf([65, 512], F32, f"osb{i}") for i in range(2)]
                rdn = [S.sbuf([65, 512], F32, f"rdn{i}") for i in range(2)]
                yos = [S.sbuf([64, 512], BF16, f"yo{i}") for i in range(2)]
                psS = [S.psum([128, 512], F32, f"psS{i}") for i in range(NSL)]
                psO = [S.psum([128, 512], F32, f"psO{i}") for i in range(2)]
                psB = [S.psum([128, 512], F32, f"psB{i}") for i in range(1)]
                steps = [(qb, kt) for qb in range(NQB) for kt in range(4 * qb + 4)]
                nst = len(steps)
                gcnt = 0
                nq = 0
                for h in range(8):
                    qa = qas[h % 2]
                    ka = kas[h % 2]
                    S.dma("sp", qa[0:64, :], qT[h * 64:(h + 1) * 64, :], writes=[qa])
                    S.MS("pool", qa[64:70, :], 1.0, [qa])
                    S.dma("sp", qa[64:67, :], cqk[h, 0:3, :], writes=[qa])
                    S.dma("sp", ka[0:64, :], kT[h * 64:(h + 1) * 64, :], writes=[ka])
                    S.MS("pool", ka[64:70, :], 1.0, [ka])
                    S.dma("sp", ka[67:70, :], cqk[h, 3:6, :], writes=[ka])

                    def geom(i):
                        qb, kt = steps[i]
                        j = kt - 4 * qb
                        q0 = j * 128 if j > 0 else 0
                        return qb, kt, j, q0

                    def issue_S(i, qa=qa, ka=ka):
                        qb, kt, j, q0 = geom(i)
                        sp_ = psS[(gcnt + i) % NSL]
                        S.MM(sp_[:, q0:512], ka[:, kt * 128:(kt + 1) * 128], qa[:, qb * 512 + q0:(qb + 1) * 512],
                             True, True, [ka, qa], [sp_])

                    pending = []

                    def epi2(qb, ob, rd, yo, h=h):
                        pb = psB[0]
                        S.MM(pb[0:64, :], cf[64:65, 512:576], rd[64:65, :], True, True, [cf, rd], [pb])
                        S.TT("dve", yo[:], ob[0:64, :], pb[0:64, :], ALU.mult, [ob, pb], [yo])
                        S.dma("pool", ycT[h * 64:(h + 1) * 64, qb * 512:(qb + 1) * 512], yo[:], reads=[yo])

                    for i in range(min(LOOK, nst)):
                        issue_S(i)
                    for i0 in range(0, nst, 2):
                        pair = [i for i in (i0, i0 + 1) if i < nst]
                        for i in pair:
                            qb, kt, j, q0 = geom(i)
                            sp_ = psS[(gcnt + i) % NSL]
                            pt = pts[(gcnt + i) % NSL]
                            S.A(pt[:, q0:512], sp_[:, q0:512], AF.Exp, [sp_], [pt])
                            if j >= 0:
                                S.TT("dve", pt[:, q0:q0 + 128], pt[:, q0:q0 + 128], trib[:], ALU.mult, [pt, trib], [pt])
                        for i in pair:
                            if i + LOOK < nst:
                                issue_S(i + LOOK)
                        for i in pair:
                            qb, kt, j, q0 = geom(i)
                            nkt = 4 * qb + 4
                            pt = pts[(gcnt + i) % NSL]
                            ops_ = psO[(nq + qb) % 2]
                            extra = [pts[(gcnt + k) % NSL] for k in pair]
                            S.MM(ops_[0:65, q0:512], vext[:, kt, h * 66:h * 66 + 65], pt[:, q0:512],
                                 kt == 0, kt == nkt - 1, [vext] + extra, [ops_], signal=(kt == nkt - 1))
                            while pending and pending[0][0] <= i:
                                pending.pop(0)[1]()
                            if kt == nkt - 1:
                                ob = osb[(nq + qb) % 2]
                                rd = rdn[(nq + qb) % 2]
                                yo = yos[(nq + qb) % 2]
                                S.CP("dve", ob[:], ops_[0:65, :], [ops_], [ob])
                                S.op("dve", lambda e, rd=rd, ob=ob: e.reciprocal(out=rd[64:65, :], in_=ob[64:65, :]), [ob], [rd])
                                pending.append((i + 3, lambda qb=qb, ob=ob, rd=rd, yo=yo: epi2(qb, ob, rd, yo)))
                    for _, fn in pending:
                        fn()
                    gcnt += nst
                    nq += NQB

        def phase_rwkv(l):
            NTB = T // 512
            with S.phase(xlat=1.0):
                stg = S.sbuf([128, 512], F32, "rstg")
                waw = S.sbuf([128, 512], BF16, "waw")
                gup = S.sbuf([128, 512], BF16, "gup")
                S.dma("sp", stg[0:64, :], r_wup[l], writes=[stg])
                S.dma("sp", stg[64:128, :], r_aup[l], writes=[stg])
                S.CP("dve", waw[:], stg[:], [stg], [waw])
                S.dma("sp", stg[:], r_gup[l], writes=[stg])
                S.CP("dve", gup[:], stg[:], [stg], [gup])
                mkf = S.sbuf([64, 3, 8, 128], F32, "mkf")
                S.dma("sp", mkf[:], mk[:, 0:3], writes=[mkf])
                mG1 = mkf[:, 0]
                mG2 = mkf[:, 1]
                mL = mkf[:, 2, :, 0:64]
                id8 = mkf[:, 2, :, 64:128]
                St = S.sbuf([64, 8, 64], F32, "St")
                Sb = S.sbuf([64, 8, 64], BF16, "Sb")
                Stmp = S.sbuf([64, 8, 64], F32, "Stmp")
                S.MS("pool", St[:], 0.0, [St])
                S.MS("pool", Sb[:], 0.0, [Sb])
                EQ8 = S.sbuf([64, 8, 8, 2, 64], BF16, "EQ8")
                FB8 = S.sbuf([64, 8, 2, 512], BF16, "FB8")
                pC8 = S.sbuf([64, 8, 8], F32, "pC8")
                EQ = S.sbuf([128, 4, 8, 2, 64], BF16, "EQ")
                FB = S.sbuf([128, 4, 2, 512], BF16, "FB")
                Vb = S.sbuf([128, 4, 512], BF16, "Vb")
                pC = S.sbuf([128, 4, 8], F32, "pC")
                Ftm = S.sbuf([64, 8, 512], BF16, "Ftm")
                nBtm = S.sbuf([64, 8, 512], BF16, "nBtm")
                Vtm = S.sbuf([64, 8, 512], BF16, "Vtm")
                zls = [S.sbuf([128, 514], BF16, f"zl{i}") for i in range(3)]
                f32t = lambda n: S.sbuf([128, 512], F32, n)
                dtmp, twa, sgf, rr, kk_, vv, sig, csf = [f32t(n) for n in ("dtmp", "twa", "sgf", "rr", "kk", "vv", "sig", "csf")]
                csm, pp, pinv, pprev, aa, kap, ksq, rinv, kh, ktl, bet = [f32t(n) for n in (
                    "csm", "pp", "pinv", "pprev", "aa", "kap", "ksq", "rinv", "kh", "ktl", "bet")]
                t1 = dtmp
                rk = ksq
                twab = S.sbuf([128, 512], BF16, "twab")
                sgb = S.sbuf([128, 512], BF16, "sgb")
                offs = S.sbuf([128, 8], F32, "offs")
                gob = [S.sbuf([128, 512], BF16, f"gob{i}") for i in range(2)]
                bob = [S.sbuf([128, 512], BF16, f"bob{i}") for i in range(2)]
                G1bs = [S.sbuf([64, 8, 128], BF16, f"G1b{i}") for i in range(2)]
                G2bs = [S.sbuf([64, 8, 128], BF16, f"G2b{i}") for i in range(2)]
                Pfin = [S.sbuf([64, 8, 64], BF16, f"Pfin{i}") for i in range(2)]
                Ab = [S.sbuf([64, 8, 64], BF16, f"Ab{i}") for i in range(2)]
                Bb = [S.sbuf([64, 8, 64], BF16, f"Bb{i}") for i in range(2)]
                Pb = [S.sbuf([64, 8, 64], BF16, f"Pb{i}") for i in range(2)]
                IBn = S.sbuf([64, 8, 64], BF16, "IBn")
                Xb = S.sbuf([64, 8, 64], BF16, "Xb")
                Ub = S.sbuf([64, 8, 64], BF16, "Ub")
                Ycs = [S.sbuf([64, 512], F32, f"Yc{i}") for i in range(2)]
                ynbs = [S.sbuf([64, 8, 8, 64], BF16, f"ynb{i}") for i in range(2)]
                gsqb = S.sbuf([64, 512], BF16, "gsqb")
                Ycb = S.sbuf([64, 512], BF16, "Ycb")
                gd_ = S.sbuf([64, 512], F32, "gd")
                gm2 = S.sbuf([64, 512], F32, "gm2")
                gmean = S.sbuf([64, 512], F32, "gmean")
                gvar = S.sbuf([64, 512], F32, "gvar")
                QA, QB, QC, RA, RB, RC, W = [S.psum([128, 512], F32, f"rp{i}") for i in range(7)]
                PT = S.psum([64, 8, 128], BF16, "rpT")

                def v3(ap, a):
                    return ap.rearrange("p (a b) -> p a b", a=a)

                nzc = [0]

                def mix(dst, row0, mucol, tb):
                    zl = zls[nzc[0] % 3]
                    nzc[0] += 1
                    c0 = tb * 512
                    if tb == 0:
                        S.MS("pool", zl[:, 0:2], 0.0, [zl])
                        S.dma("sp", zl[:, 2:514], zr[row0:row0 + 128, 0:512], writes=[zl])
                    else:
                        S.dma("sp", zl[:, 1:514], zr[row0:row0 + 128, c0 - 1:c0 + 512], writes=[zl])
                    S.TT("pool", dtmp[:], zl[:, 1:513], zl[:, 2:514], ALU.subtract, [zl], [dtmp])
                    S.STT("dve", dst[:], dtmp[:], mucol, zl[:, 2:514], ALU.mult, ALU.add, [dtmp, zl], [dst])

                def stage1(tb):
                    tcs = slice(tb * 512, (tb + 1) * 512)
                    mix(twa, 1536, pcol(l, "mu", 12), tb)
                    S.A(twab[0:64, :], twa[0:64, :], AF.Tanh, [twa], [twab])
                    S.CP("act", twab[64:128, :], twa[64:128, :], [twa], [twab])
                    yield
                    mix(sgf, 1664, pcol(l, "mu", 13), tb)
                    S.A(sgb[:], sgf[:], AF.Sigmoid, [sgf], [sgb])
                    yield
                    for hp in range(4):
                        hs = slice(hp * 128, (hp + 1) * 128)
                        mix(rr, hp * 128, pcol(l, "mu", hp), tb)
                        yield
                        mix(kk_, 512 + hp * 128, pcol(l, "mu", 4 + hp), tb)
                        yield
                        mix(vv, 1024 + hp * 128, pcol(l, "mu", 8 + hp), tb)
                        yield
                        S.MM(W[:], waw[0:64, hs], twab[0:64, :], True, True, [waw, twab], [W])
                        S.A(sig[:], W[:], AF.Sigmoid, [W], [sig], bias=pcol(l, "w0", hp))
                        yield
                        S.MM(W[:], waw[64:128, hs], twab[64:128, :], True, True, [waw, twab], [W])
                        S.A(aa[:], W[:], AF.Sigmoid, [W], [aa], bias=pcol(l, "a0", hp))
                        yield
                        S.MM(W[:], gup[:, hs], sgb[:], True, True, [gup, sgb], [W])
                        go = gob[(tb * 4 + hp) % 2]
                        S.CP("act", go[:], W[:], [W], [go])
                        S.dma("pool", gsc[hs, tcs], go[:], reads=[go])
                        yield
                        S.op("dve", lambda e: e.tensor_tensor_scan(out=csf[:], data0=sig[:], data1=sig[:], initial=0.0,
                                                                    op0=ALU.add, op1=ALU.bypass), [sig], [csf])
                        S.MS("pool", offs[:, 0:1], 0.0, [offs])
                        yield
                        S.CP("pool", offs[:, 1:8], v3(csf[:], 8)[:, 0:7, 63], [csf], [offs])
                        yield
                        S.TT("dve", v3(csf[:], 8), v3(csf[:], 8), bc(offs[:, 0:8], 64), ALU.subtract, [csf, offs], [csf])
                        yield
                        S.TT("pool", csm[:], csf[:], sig[:], ALU.subtract, [csf, sig], [csm])
                        S.A(pp[:], csf[:], AF.Exp, [csf], [pp], scale=-DS)
                        yield
                        S.A(pinv[:], csf[:], AF.Exp, [csf], [pinv], scale=DS)
                        S.A(pprev[:], csm[:], AF.Exp, [csm], [pprev], scale=-DS)
                        S.CP("pool", pC[:, hp, :], v3(pp[:], 8)[:, :, 63], [pp], [pC])
                        yield
                        S.A(kap[:], kk_[:], AF.Copy, [kk_], [kap], scale=pcol(l, "kk", hp))
                        yield
                        S.A(ksq[:], kap[:], AF.Square, [kap], [ksq])
                        yield
                        S.MM(W[:], bdones, ksq[:], True, True, [cf, ksq], [W])
                        S.A(rinv[:], W[:], AF.Ln, [W], [rinv], bias=1e-12)
                        S.A(rinv[:], rinv[:], AF.Exp, [rinv], [rinv], scale=-0.5)
                        yield
                        S.TS("pool", t1[:], aa[:], pcol(l, "ka", hp), omka[l][:, hp:hp + 1], ALU.mult, ALU.add, [aa, omka[l]], [t1])
                        yield
                        S.TT("dve", kh[:], kap[:], rinv[:], ALU.mult, [kap, rinv], [kh])
                        S.TT("pool", ktl[:], kk_[:], t1[:], ALU.mult, [kk_, t1], [ktl])
                        yield
                        S.TT("pool", bet[:], aa[:], kh[:], ALU.mult, [aa, kh], [bet])
                        S.TT("dve", EQ[:, hp, :, 1, :], v3(rr[:], 8), v3(pp[:], 8), ALU.mult, [rr, pp], [EQ])
                        yield
                        S.TT("pool", EQ[:, hp, :, 0, :], v3(kh[:], 8), v3(pprev[:], 8), ALU.mult, [kh, pprev], [EQ])
                        S.TT("pool", FB[:, hp, 0, :], ktl[:], pinv[:], ALU.mult, [ktl, pinv], [FB])
                        yield
                        S.TT("pool", FB[:, hp, 1, :], bet[:], pinv[:], ALU.mult, [bet, pinv], [FB])
                        S.CP("act", Vb[:, hp, :], vv[:], [vv], [Vb])
                        S.STT("dve", rk[:], rr[:], pcol(l, "rk", hp), ktl[:], ALU.mult, ALU.mult, [rr, ktl], [rk])
                        yield
                        S.MM(W[:], bdones, rk[:], True, True, [cf, rk], [W])
                        bo = bob[(tb * 4 + hp) % 2]
                        S.TT("dve", bo[:], W[:], vv[:], ALU.mult, [W, vv], [bo])
                        S.dma("pool", bsc[hs, tcs], bo[:], reads=[bo])
                        yield

                def stage2():
                    for hp in range(4):
                        hs = slice(hp * 128, (hp + 1) * 128)
                        for (src, dstT, neg) in ((FB[:, hp, 0, :], Ftm, False), (FB[:, hp, 1, :], nBtm, True),
                                                 (Vb[:, hp, :], Vtm, False)):
                            for c in range(8):
                                S.TR(PT[:, c, :], src[:, c * 64:(c + 1) * 64], identb[:], [FB, Vb, identb], [PT],
                                     signal=(c == 7))
                            if neg:
                                S.A(dstT[:, :, hs], PT[:], AF.Copy, [PT], [dstT], scale=-1.0)
                            else:
                                S.CP("dve", dstT[:, :, hs], PT[:], [PT], [dstT])
                    for par in range(2):
                        ps_ = slice(par * 64, par * 64 + 64)
                        S.dma("sp", EQ8[:].rearrange("p (a two) c e t -> p a two (c e t)", two=2)[:, :, par, :],
                              EQ[ps_].rearrange("p a c e t -> p a (c e t)"), reads=[EQ], writes=[EQ8])
                        S.dma("sp", FB8[:].rearrange("p (a two) e t -> p a two (e t)", two=2)[:, :, par, :],
                              FB[ps_].rearrange("p a e t -> p a (e t)"), reads=[FB], writes=[FB8])
                        S.dma("sp", pC8[:].rearrange("p (a two) c -> p a two c", two=2)[:, :, par, :],
                              pC[ps_], reads=[pC], writes=[pC8])

                def streamA(c, par):
                    ccs = slice(c * 64, (c + 1) * 64)
                    G1b, G2b = G1bs[par], G2bs[par]
                    ga = [v3(QA[0:64, :], 4), v3(QB[0:64, :], 4)]
                    bank = [QA, QB]
                    for h in range(8):
                        eq = EQ8[:, h, c, :, :].rearrange("p a b -> p (a b)")
                        S.MM(ga[h // 4][:, h % 4, :], FB8[:, h, 0, ccs], eq, True, True, [FB8, EQ8], [bank[h // 4]],
                             signal=(h % 4 == 3))
                    yield
                    S.TT("dve", G1b[:, 0:4, :], ga[0], mG1[:, 0:4, :], ALU.mult, [QA, mkf], [G1b])
                    S.TT("dve", G1b[:, 4:8, :], ga[1], mG1[:, 4:8, :], ALU.mult, [QB, mkf], [G1b])
                    g3 = v3(QC[0:64, :], 8)
                    for h in range(8):
                        S.MM(g3[:, h, :], EQ8[:, h, c, 0, :], FB8[:, h, 1, ccs], True, True, [FB8, EQ8], [QC], signal=(h == 7))
                    yield
                    for h in range(8):
                        eq = EQ8[:, h, c, :, :].rearrange("p a b -> p (a b)")
                        S.MM(ga[h // 4][:, h % 4, :], FB8[:, h, 1, ccs], eq, True, True, [FB8, EQ8], [bank[h // 4]],
                             signal=(h % 4 == 3))
                    S.TT("dve", Bb[0][:], g3, mL, ALU.mult, [QC, mkf], [Bb[0]])
                    yield
                    S.TT("dve", G2b[:, 0:4, :], ga[0], mG2[:, 0:4, :], ALU.mult, [QA, mkf], [G2b])
                    S.TT("dve", G2b[:, 4:8, :], ga[1], mG2[:, 4:8, :], ALU.mult, [QB, mkf], [G2b])
                    yield
                    S.TT("dve", Pb[0][:], id8, G2b[:, :, 0:64], ALU.subtract, [mkf, G2b], [Pb[0]])
                    yield
                    pa, pb_, pq = v3(QA[0:64, :], 8), v3(QB[0:64, :], 8), v3(QC[0:64, :], 8)
                    for j in range(5):
                        Ac, Bc, Pc = Ab[j % 2], Bb[j % 2], Pb[j % 2]
                        An, Bn = Ab[(j + 1) % 2], Bb[(j + 1) % 2]
                        Pn = Pb[(j + 1) % 2] if j < 4 else Pfin[par]
                        if j == 0:
                            Acb, Acv = G2b, (lambda h: G2b[:, h, 0:64])
                        else:
                            Acb, Acv = Ac, (lambda h, Ac=Ac: Ac[:, h, :])
                        for h in range(8):
                            S.MM(pb_[:, h, :], Acv(h), Bc[:, h, :], True, True, [Acb, Bc], [QB], signal=(h == 7))
                        if j < 4:
                            for h in range(8):
                                S.MM(pa[:, h, :], Bc[:, h, :], Acv(h), True, True, [Acb, Bc], [QA], signal=(h == 7))
                        yield
                        S.CP("act", Bn[:], pb_, [QB], [Bn])
                        if j < 4:
                            S.CP("act", An[:], pa, [QA], [An])
                        yield
                        S.TT("dve", IBn[:], Bn[:], id8, ALU.add, [Bn, mkf], [IBn])
                        yield
                        for h in range(8):
                            S.MM(pq[:, h, :], IBn[:, h, :], Pc[:, h, :], True, True, [IBn, Pc], [QC], signal=(h == 7))
                        yield
                        S.CP("act", Pn[:], pq, [QC], [Pn])
                        yield

                def streamB(c, par, yn):
                    G1b, G2b, Pf = G1bs[par], G2bs[par], Pfin[par]
                    px = v3(RA[0:64, :], 8)
                    for h in range(8):
                        hc = slice(h * 64, (h + 1) * 64)
                        S.MM(px[:, h, :], EQ8[:, h, c, 0, :], Sb[:, h, :], True, False, [EQ8, Sb], [RA], signal=False)
                        S.MM(px[:, h, :], G1b[:, h, 0:64], Vtm[:, c, hc], False, True, [G1b, Vtm], [RA], signal=(h == 7))
                    yield
                    S.CP("act", Xb[:], px, [RA], [Xb])
                    yield
                    pu = v3(RB[0:64, :], 8)
                    for h in range(8):
                        S.MM(pu[:, h, :], Pf[:, h, :], Xb[:, h, :], True, True, [Pf, Xb], [RB], signal=(h == 7))
                    yield
                    S.CP("act", Ub[:], pu, [RB], [Ub])
                    yield
                    pS = v3(RA[0:64, :], 8)
                    for h in range(8):
                        hc = slice(h * 64, (h + 1) * 64)
                        S.MM(pS[:, h, :], Ftm[:, c, hc], Vtm[:, c, hc], True, False, [Ftm, Vtm], [RA], signal=False)
                        S.MM(pS[:, h, :], nBtm[:, c, hc], Ub[:, h, :], False, True, [nBtm, Ub], [RA], signal=(h == 7))
                    py = v3(RC[0:64, :], 8)
                    for h in range(8):
                        hc = slice(h * 64, (h + 1) * 64)
                        S.MM(py[:, h, :], Sb[:, h, :], EQ8[:, h, c, 1, :], True, False, [EQ8, Sb], [RC], signal=False)
                        S.MM(py[:, h, :], Vtm[:, c, hc], G1b[:, h, 64:128], False, False, [G1b, Vtm], [RC], signal=False)
                        S.MM(py[:, h, :], Ub[:, h, :], G2b[:, h, 64:128], False, True, [G2b, Ub], [RC], signal=(h == 7))
                    yield
                    S.TT("dve", Stmp[:], St[:], pS, ALU.add, [St, RA], [Stmp])
                    S.CP("act", yn[:, :, c, :], py, [RC], [yn])
                    yield
                    S.TT("dve", Sb[:], Stmp[:], bc(pC8[:, :, c], 64), ALU.mult, [Stmp, pC8], [Sb])
                    yield
                    S.TT("pool", St[:], Stmp[:], bc(pC8[:, :, c], 64), ALU.mult, [Stmp, pC8], [St])
                    yield

                def drive(gens):
                    gens = [g for g in gens if g is not None]
                    while gens:
                        for g in list(gens):
                            try:
                                next(g)
                            except StopIteration:
                                gens.remove(g)

                drive([stage1(0)])
                stage2()
                gch = 0
                for tb in range(NTB):
                    tcs = slice(tb * 512, (tb + 1) * 512)
                    yn = ynbs[tb % 2]
                    s1 = stage1(tb + 1) if tb + 1 < NTB else None
                    RW = int(os.environ.get("K_RW", "0"))
                    if RW != 1:
                        drive([streamA(0, gch % 2)])
                    for c in range(8 if RW != 1 else 0):
                        ga_ = streamA(c + 1, (gch + 1) % 2) if c < 7 else None
                        gb_ = streamB(c, gch % 2, yn) if RW != 2 else None
                        gens = [g for g in (gb_, ga_) if g is not None]
                        while gens:
                            for g in list(gens):
                                try:
                                    next(g)
                                except StopIteration:
                                    gens.remove(g)
                            if s1 is not None:
                                try:
                                    next(s1)
                                except StopIteration:
                                    s1 = None
                        gch += 1
                    if s1 is not None:
                        drive([s1])
                    S.dma("pool", ybn[:, tcs].rearrange("(h p) t -> p h t", p=64),
                          yn[:].rearrange("p h c t -> p h (c t)"), reads=[yn])
                    if tb + 1 < NTB:
                        stage2()

        def phase_merge(l, xsrc, xdst):
            with S.phase():
                wb = S.sbuf([128, 3, 4, D], BF16, "wb")
                wo = S.sbuf([128, 8, D], BF16, "wo")
                stages = [S.sbuf([128, 2048], F32, f"mstg{i}") for i in range(2)]
                for br in range(3):
                    pieces = [(c0, 256, 4) for c0 in range(0, D, 256)]
                    load_w_bf16(wb, lambda p, br=br: wb[:, br, :, p[0]:p[0] + p[1]],
                                lambda p, br=br: w_br[l, br, :, p[0]:p[0] + p[1]].rearrange("(k q) n -> q k n", q=128),
                                pieces, stages)
                pieces = [(c0, 256, 8) for c0 in range(0, D, 256)]
                load_w_bf16(wo, lambda p: wo[:, :, p[0]:p[0] + p[1]],
                            lambda p: w_o[l, :, p[0]:p[0] + p[1]].rearrange("(k q) n -> q k n", q=128), pieces, stages)
                yas = [S.sbuf([128, 4, 512], BF16) for _ in range(2)]
                ycs = [S.sbuf([128, 4, 512], BF16) for _ in range(2)]
                ybs = [S.sbuf([128, 4, 512], BF16) for _ in range(2)]
                bos = [S.sbuf([128, 4, 512], BF16) for _ in range(2)]
                gos = [S.sbuf([128, 4, 512], BF16) for _ in range(2)]
                ybf = [S.sbuf([128, 4, 512], BF16) for _ in range(2)]
                gsets = [(S.sbuf([128, 512], BF16), S.sbuf([128, 512], F32), S.sbuf([128, 512], F32),
                          S.sbuf([128, 512], F32), S.sbuf([128, 512], F32)) for _ in range(2)]
                Gs = [S.sbuf([128, 24, 512], BF16) for _ in range(2)]
                mT = [S.sbuf([128, 8, 512], BF16) for _ in range(2)]
                m1 = [S.sbuf([128, 512], F32) for _ in range(2)]
                _m2 = S.sbuf([128, 512], F32)
                m2 = [_m2, _m2]
                xbufs = [S.sbuf([128, D], F32) for _ in range(2)]
                pbr = [S.psum([128, 512], F32) for _ in range(3)]
                pso = [S.psum([128, 512], F32) for _ in range(2)]
                pgn = [S.psum([128, 512], F32) for _ in range(3)]
                no = 0
                for i in range(NT):
                    tcs = slice(i * 512, (i + 1) * 512)
                    k = i % 2
                    ld = lambda dst, src: S.dma("sp", dst[:], src[:, tcs].rearrange("(c p) t -> p c t", p=128), writes=[dst])
                    ld(yas[k], yaT); ld(ycs[k], ycT); ld(ybs[k], ybn); ld(bos[k], bsc); ld(gos[k], gsc)
                    S.dma("sp", Gs[k][:], gT[:, tcs].rearrange("(c p) t -> p c t", p=128), writes=[Gs[k]])
                    for hp in range(4):
                        yv = ybs[k][:, hp, :]
                        gsqm, gmn, gdm, gm2m, gvm = gsets[hp % 2]
                        pg0 = pgn[(2 * hp) % 3]
                        pg1 = pgn[(2 * hp + 1) % 3]
                        S.A(gsqm[:], yv, AF.Square, [ybs[k]], [gsqm])
                        S.MM(pg0[:], bdmb[:], yv, True, True, [bdmb, ybs[k]], [pg0])
                        S.MM(pg1[:], bdmb[:], gsqm[:], True, True, [bdmb, gsqm], [pg1])
                        S.CP("dve", gmn[:], pg0[:], [pg0], [gmn])
                        S.TT("pool", gdm[:], yv, gmn[:], ALU.subtract, [ybs[k], gmn], [gdm])
                        S.A(gm2m[:], gmn[:], AF.Square, [gmn], [gm2m])
                        S.TT("dve", gvm[:], pg1[:], gm2m[:], ALU.subtract, [pg1, gm2m], [gvm])
                        S.A(gvm[:], gvm[:], AF.Sqrt, [gvm], [gvm], bias=64e-5)
                        S.op("dve", lambda e, gvm=gvm: e.reciprocal(out=gvm[:], in_=gvm[:]), [gvm], [gvm], dur=0.65)
                        S.TT("pool", gdm[:], gdm[:], gvm[:], ALU.mult, [gdm, gvm], [gdm])
                        S.TS("dve", gdm[:], gdm[:], pcol(l, "gng", hp), pcol(l, "gnb", hp), ALU.mult, ALU.add, [gdm], [gdm])
                        S.TT("pool", gdm[:], gdm[:], bos[k][:, hp, :], ALU.add, [gdm, bos[k]], [gdm])
                        S.TT("dve", ybf[k][:, hp, :], gdm[:], gos[k][:, hp, :], ALU.mult, [gdm, gos[k]], [ybf[k]])
                    ysrc = (yas[k], ybf[k], ycs[k])
                    for oc in range(8):
                        pp3 = [pbr[br] for br in range(3)]
                        for br in range(3):
                            for kc in range(4):
                                S.MM(pp3[br][:], wb[:, br, kc, oc * 128:(oc + 1) * 128], ysrc[br][:, kc, :], kc == 0, kc == 3,
                                     [wb, ysrc[br]], [pp3[br]], signal=(kc == 3))
                        a1, a2 = m1[oc % 2], m2[oc % 2]
                        S.TT("dve", a1[:], pp3[0][:], Gs[k][:, oc, :], ALU.mult, [pp3[0], Gs[k]], [a1])
                        S.TT("dve", a2[:], pp3[1][:], Gs[k][:, 8 + oc, :], ALU.mult, [pp3[1], Gs[k]], [a2])
                        S.TT("pool", a1[:], a1[:], a2[:], ALU.add, [a1, a2], [a1])
                        S.TT("dve", a2[:], pp3[2][:], Gs[k][:, 16 + oc, :], ALU.mult, [pp3[2], Gs[k]], [a2])
                        S.TT("pool", mT[k][:, oc, :], a1[:], a2[:], ALU.add, [a1, a2], [mT[k]])
                    for s in range(4):
                        xt = xbufs[no % 2]
                        r0 = (i * 4 + s) * 128
                        S.dma("sp", xt[:], xsrc[r0:r0 + 128, :], writes=[xt])
                        for half in range(2):
                            ps = pso[half]
                            for kc in range(8):
                                S.MM(ps[:], mT[k][:, kc, s * 128:(s + 1) * 128], wo[:, kc, half * 512:(half + 1) * 512],
                                     kc == 0, kc == 7, [mT[k], wo], [ps], signal=(kc == 7))
                            S.TT("dve", xt[:, half * 512:(half + 1) * 512], xt[:, half * 512:(half + 1) * 512], ps[:], ALU.add,
                                 [xt, ps], [xt])
                        S.dma("pool", xdst[r0:r0 + 128, :], xt[:], reads=[xt])
                        no += 1

        def phase_ffn(l, xsrc, xdst, final):
            with S.phase():
                wup = S.sbuf([128, 8, 2 * FF], BF16, "wup")
                wdn = S.sbuf([128, 22, D], BF16, "wdn")
                xbufs = [S.sbuf([128, D], F32, f"fx{i}") for i in range(3)]
                pieces = [(c0, 128, 8) for c0 in range(0, 2 * FF, 128)]
                load_w_bf16(wup, lambda p: wup[:, :, p[0]:p[0] + p[1]],
                            lambda p: w_up[l, :, p[0]:p[0] + p[1]].rearrange("(k q) n -> q k n", q=128), pieces, xbufs)
                ip = 0
                for kc0 in range(0, 22, 2):
                    for half in range(2):
                        st = xbufs[ip % 3]
                        sv = st[:, 0:1024].rearrange("p (k n) -> p k n", k=2)
                        S.dma("sp", sv, w_dn[l, kc0 * 128:(kc0 + 2) * 128, half * 512:(half + 1) * 512].rearrange(
                            "(k q) n -> q k n", q=128), writes=[st])
                        S.CP(("act", "dve")[ip % 2], wdn[:, kc0:kc0 + 2, half * 512:(half + 1) * 512], sv, [st], [wdn])
                        ip += 1
                _xs = S.sbuf([128, D], BF16)
                nbs = [(S.sbuf([128, 1], F32), S.sbuf([128, 1], F32), _xs, _xs, S.psum([128, 8, 128], BF16))]
                fxnTs = [S.sbuf([128, 8, 512], BF16, f"fxnT{i}") for i in range(2)]
                actT = S.sbuf([128, 22, 512], BF16, "actT")
                hg = [S.sbuf([128, 514], F32) for _ in range(2)]
                _hu = S.sbuf([128, 514], F32)
                hu = [_hu, _hu]
                cg = [S.sbuf([128, 512], F32) for _ in range(2)]
                _cu = S.sbuf([128, 512], F32)
                cu = [_cu, _cu]
                carry = S.sbuf([128, 44, 2], F32, "carry")
                S.MS("pool", carry[:], 0.0, [carry])
                gf = None
                if final:
                    gf = S.sbuf([128, D], F32, "gfin")
                    S.dma("sp", gf[:], gfin.partition_broadcast(128), writes=[gf])
                    fss = S.sbuf([128, 1], F32)
                    frs = S.sbuf([128, 1], F32)
                    fj = _xs
                pg = [S.psum([128, 512], F32) for _ in range(2)]
                pu = [S.psum([128, 512], F32) for _ in range(2)]
                pdn = [S.psum([128, 512], F32) for _ in range(2)]
                nx = 0
                for i in range(NT):
                    xnT = fxnTs[i % 2]
                    for s in range(4):
                        xt = xbufs[nx % 3]; nx += 1
                        r0 = (i * 4 + s) * 128
                        S.dma("sp", xt[:], xsrc[r0:r0 + 128, :], writes=[xt])
                        rmsnorm_T(xt, pcol(l, "n2g", 0, 8), xnT, s, nbs[0])
                    for j in range(22):
                        k = j % 2
                        for kc in range(8):
                            S.MM(pg[k][:], wup[:, kc, j * 128:(j + 1) * 128], xnT[:, kc, :], kc == 0, kc == 7, [wup, xnT], [pg[k]],
                                 signal=(kc == 7))
                        for kc in range(8):
                            S.MM(pu[k][:], wup[:, kc, FF + j * 128:FF + (j + 1) * 128], xnT[:, kc, :], kc == 0, kc == 7,
                                 [wup, xnT], [pu[k]], signal=(kc == 7))
                        for (hb, ps, cc, slot) in ((hg[k], pg[k], cg[k], j), (hu[k], pu[k], cu[k], 22 + j)):
                            S.CP("pool", hb[:, 0:2], carry[:, slot, :], [carry], [hb])
                            S.CP("act", hb[:, 2:514], ps[:], [ps], [hb])
                            S.CP("pool", carry[:, slot, :], hb[:, 512:514], [hb], [carry])
                            S.A(cc[:], hb[:, 2:514], AF.Copy, [hb], [cc], scale=pcol(l, "fcw", slot * 3 + 2))
                            S.STT("dve", cc[:], hb[:, 1:513], pcol(l, "fcw", slot * 3 + 1), cc[:], ALU.mult, ALU.add, [hb, cc], [cc])
                            S.STT("dve", cc[:], hb[:, 0:512], pcol(l, "fcw", slot * 3 + 0), cc[:], ALU.mult, ALU.add, [hb, cc], [cc])
                        S.A(cg[k][:], cg[k][:], AF.Silu, [cg[k]], [cg[k]])
                        S.TT("dve", actT[:, j, :], cg[k][:], cu[k][:], ALU.mult, [cg[k], cu[k]], [actT])
                    for s in range(4):
                        xt = xbufs[nx % 3]; nx += 1
                        r0 = (i * 4 + s) * 128
                        S.dma("sp", xt[:], xsrc[r0:r0 + 128, :], writes=[xt])
                        for half in range(2):
                            ps = pdn[half]
                            for j in range(22):
                                S.MM(ps[:], actT[:, j, s * 128:(s + 1) * 128], wdn[:, j, half * 512:(half + 1) * 512],
                                     j == 0, j == 21, [actT, wdn], [ps], signal=(j == 21))
                            S.TT("dve", xt[:, half * 512:(half + 1) * 512], xt[:, half * 512:(half + 1) * 512], ps[:], ALU.add,
                                 [xt, ps], [xt])
                        if final:
                            S.MS("pool", fss[:], 0.0, [fss])
                            S.A(fj[:], xt[:], AF.Square, [xt], [fj, fss], accum_out=fss[:])
                            S.A(frs[:], fss[:], AF.Sqrt, [fss], [frs], scale=1.0 / D, bias=1e-6)
                            S.op("dve", lambda e: e.reciprocal(out=frs[:], in_=frs[:]), [frs], [frs])
                            S.STT("dve", xt[:], xt[:], frs[:, 0:1], gf[:], ALU.mult, ALU.mult, [xt, frs, gf], [xt])
                        S.dma("pool", xdst[r0:r0 + 128, :], xt[:], reads=[xt])

        src = x_in
        phl = []
        for l in range(depth):
            last = (l == depth - 1)
            phl += [lambda l=l, src=src: phase_inproj(l, src), lambda l=l: phase_conv(l), lambda l=l: phase_fcum(l),
                    lambda l=l: phase_attn(l), lambda l=l: phase_rwkv(l), lambda l=l, src=src: phase_merge(l, src, xa),
                    lambda l=l, last=last: phase_ffn(l, xa, y_out if last else xb, last)]
            src = xb
        for f in phl[:nph]:
            f()
        print("instructions:", S.ninstr)
    return nc


def host_consts():
    cst = np.zeros((128, 768), np.float32)
    cst[:, 0:128] = np.eye(128)
    k = np.arange(128)
    cst[:, 128:256] = (k[:, None] <= k[None, :])
    bd = np.zeros((128, 128), np.float32)
    bd[:64, :64] = 1
    bd[64:, 64:] = 1
    cst[:, 256:384] = bd
    cst[:, 384:512] = 1.0 / 64
    cst[:, 512:640] = 1.0
    s = np.arange(64)
    su = (s[:, None] < s[None, :]).astype(np.float32)
    ui = (s[:, None] <= s[None, :]).astype(np.float32)
    sl = (s[:, None] > s[None, :]).astype(np.float32)
    mk = np.zeros((64, 4, 8, 128), np.float32)
    mk[:, 0, :, 0:64] = su[:, None, :]
    mk[:, 0, :, 64:128] = ui[:, None, :]
    mk[:, 1, :, 0:64] = su[:, None, :]
    mk[:, 1, :, 64:128] = -ui[:, None, :]
    mk[:, 2, :, 0:64] = sl[:, None, :]
    mk[:, 2, :, 64:128] = np.eye(64, dtype=np.float32)[:, None, :]
    return cst, mk


def pack_params(inp, depth):
    pc = np.zeros((depth, 128, NPC), np.float32)

    def col(v):
        return np.ascontiguousarray(v.reshape(-1, 128).T)

    for l in range(depth):
        def put(name, arr):
            o = PCO[name]
            pc[l, :arr.shape[0], o:o + arr.shape[1]] = arr
        put("n1g", col(inp["norm1_g"][l]))
        put("n2g", col(inp["norm2_g"][l]))
        put("gateb", col(inp["gate_b"][l]))
        cm = inp["conv_mix_w"][l]
        put("cmw", np.ascontiguousarray(cm.reshape(3, 4, 128).transpose(2, 1, 0).reshape(128, 12)))
        fc = inp["ffn_conv_w"][l]
        put("fcw", np.ascontiguousarray(fc.reshape(3, 44, 128).transpose(2, 1, 0).reshape(128, 132)))
        put("mu", col(inp["rwkv_mu"][l]))
        put("w0", col(inp["rwkv_w0"][l]))
        put("a0", col(inp["rwkv_a0"][l]))
        put("kk", col(inp["rwkv_k_k"][l]))
        put("ka", col(inp["rwkv_k_a"][l]))
        put("rk", col(inp["rwkv_r_k"][l].reshape(-1)))
        put("fb", inp["attn_forget_b"][l].reshape(8, 1))
        put("gng8", np.ascontiguousarray(inp["rwkv_gn_g"][l].reshape(8, 64).T))
        put("gnb8", np.ascontiguousarray(inp["rwkv_gn_b"][l].reshape(8, 64).T))
        put("gng", col(inp["rwkv_gn_g"][l]))
        put("gnb", col(inp["rwkv_gn_b"][l]))
    return pc


_NC_CACHE = {}


def run(inputs, T, nb, depth=DEPTH, dbg=(), nph=99):
    inputs = {k: np.asarray(v) for k, v in inputs.items()}
    key = (T, depth, tuple(dbg), nph)
    if key not in _NC_CACHE:
        _NC_CACHE[key] = build(T, depth, dbg, nph)
    nc = _NC_CACHE[key]
    cst, mk = host_consts()
    pc = pack_params(inputs, depth)
    shared = {
        "w_in": inputs["w_in"][:depth], "w_branch": inputs["w_branch"][:depth], "w_o": inputs["w_o"][:depth],
        "ffn_w_up": inputs["ffn_w_up"][:depth], "ffn_w_down": inputs["ffn_w_down"][:depth],
        "rwkv_w_up": inputs["rwkv_w_up"][:depth], "rwkv_a_up": inputs["rwkv_a_up"][:depth],
        "rwkv_g_up": inputs["rwkv_g_up"][:depth], "pc": pc, "final_norm_g": inputs["final_norm_g"],
        "cst": cst, "mk": mk,
    }
    shared = {k: np.ascontiguousarray(v, dtype=np.float32) for k, v in shared.items()}
    in_maps = []
    for b in range(nb):
        m = dict(shared)
        m["x"] = np.ascontiguousarray(inputs["x"][b], dtype=np.float32)
        in_maps.append(m)
    res = run_bass_kernel_spmd(nc, in_maps, core_ids=list(range(nb)))
    return res.results


def kernel(**inputs):
    res = run(inputs, SEQ, NB)
    return np.stack([np.asarray(r["y"], dtype=np.float32) for r in res], axis=0)
```

```python
import contextlib
import math
import os
CUT = int(os.environ.get("K_CUT", "0"))
SUB = int(os.environ.get("K_SUB", "0"))
import numpy as np
import concourse.bass as bass
import concourse.mybir as mybir
from concourse.bass_utils import run_bass_kernel_spmd

F32 = mybir.dt.float32
BF16 = mybir.dt.bfloat16
AF = mybir.ActivationFunctionType
ALU = mybir.AluOpType

ENGS = ("pe", "act", "dve", "pool", "sp")

D = 1024
NIN = 7944
FF = 2816
SEQ = 8192
NB = 4
DEPTH = 2
DS = math.exp(-0.5)

PCO = {}
_o = 0
for _n, _w in (("n1g", 8), ("n2g", 8), ("gateb", 24), ("cmw", 12), ("fcw", 132), ("mu", 14),
               ("w0", 4), ("a0", 4), ("kk", 4), ("ka", 4), ("rk", 4), ("fb", 1), ("gng8", 8), ("gnb8", 8), ("gng", 4), ("gnb", 4)):
    PCO[_n] = _o
    _o += _w
NPC = _o


class Buf:
    __slots__ = ("t", "name", "last_w", "readers", "chan", "last_dma")

    def __init__(self, t, name):
        self.t = t
        self.name = name
        self.last_w = []
        self.readers = []
        self.chan = None
        self.last_dma = None

    def __getitem__(self, k):
        return self.t[k]


class Node:
    __slots__ = ("id", "eng", "fns", "deps", "dur", "occ", "kind", "chan", "ev", "open", "kw")

    def __init__(self, id, eng, kind):
        self.id = id
        self.eng = eng
        self.kind = kind
        self.fns = []
        self.deps = set()
        self.dur = 0.0
        self.occ = 0.0
        self.chan = None
        self.ev = None
        self.open = False


def _nfree(ap):
    n = 1
    for d in ap.shape[1:]:
        n *= d
    return n


class Sched:
    NCHAN = 64
    XLAT = float(os.environ.get("K_XLAT", "0.3"))

    def __init__(self, nc, stack):
        self.nc = nc
        self.gstack = stack
        self.stack = stack
        self.q = {e: [] for e in ENGS}
        self.cnt = {e: 0 for e in ENGS}
        self.sems = {}
        for e in ENGS:
            self.sems[e] = stack.enter_context(nc.semaphore("s_" + e))
        self.chan_cnt = {}
        for i in range(self.NCHAN):
            k = f"d{i}"
            self.sems[k] = stack.enter_context(nc.semaphore("s_" + k))
            self.chan_cnt[k] = 0
        self.chan_next = 0
        self.known = {e: {} for e in ENGS}
        self.nbuf = 0
        self.ninstr = 0
        self.nodes = []
        self.pe_open = None
        self.sb_bytes = 0
        self.reorder = True

    def sbuf(self, shape, dtype, name=None):
        self.nbuf += 1
        name = f"{name or 'sb'}_{self.nbuf}"
        nb = int(np.prod(shape[1:])) * (2 if dtype == BF16 else 4)
        self.sb_bytes += ((nb + 31) // 32) * 32
        return Buf(self.stack.enter_context(self.nc.sbuf_tensor(name, list(shape), dtype)), name)

    def psum(self, shape, dtype, name=None):
        self.nbuf += 1
        name = f"{name or 'ps'}_{self.nbuf}"
        return Buf(self.stack.enter_context(self.nc.psum_tensor(name, list(shape), dtype)), name)

    def _chan(self, b):
        if b.chan is None:
            assert self.chan_next < self.NCHAN, "out of dma channels"
            b.chan = f"d{self.chan_next}"
            self.chan_next += 1
        return b.chan

    def _close_pe(self):
        if self.pe_open is not None:
            self.pe_open.open = False
            self.pe_open = None

    def _deps_of(self, reads, writes):
        deps = set()
        for r in reads:
            deps.update(r.last_w)
        for w in writes:
            deps.update(w.last_w)
            deps.update(w.readers)
        return deps

    def _touch(self, node, reads, writes):
        for r in reads:
            if not r.readers or r.readers[-1] is not node:
                r.readers.append(node)
        for w in writes:
            w.last_w = [node]
            w.readers = []

    def op(self, eng, fn, reads=(), writes=(), signal=True, same_ok=False, dur=0.3):
        self.ninstr += 1
        deps = self._deps_of(reads, writes)
        if eng == "pe":
            node = self.pe_open
            if node is None:
                node = Node(len(self.nodes), "pe", "op")
                self.nodes.append(node)
                node.open = True
                self.pe_open = node
            deps.discard(node)
            node.deps |= deps
            node.fns.append(fn)
            node.dur += dur
            node.occ += dur
            if signal:
                self._close_pe()
        else:
            self._close_pe()
            node = Node(len(self.nodes), eng, "op")
            self.nodes.append(node)
            node.deps = deps
            node.fns.append(fn)
            node.dur = dur
            node.occ = dur
        self._touch(node, reads, writes)
        return node

    def dma(self, eng, out_ap, in_ap, reads=(), writes=(), **kw):
        self._close_pe()
        self.ninstr += 1
        cb = writes[0] if writes else reads[0]
        key = self._chan(cb)
        node = Node(len(self.nodes), eng, "dma")
        self.nodes.append(node)
        node.chan = key
        node.deps = self._deps_of(reads, writes)
        if cb.last_dma is not None:
            node.deps.add(cb.last_dma)
        cb.last_dma = node
        nbytes = _nfree(out_ap) * out_ap.shape[0] * (2 if out_ap.dtype == BF16 else 4)
        node.dur = 2.0 + nbytes / 1.0e5
        node.occ = 0.1 if eng == "sp" else 0.6
        node.fns.append((out_ap, in_ap, kw))
        self._touch(node, reads, writes)
        return node

    def _schedule(self):
        import heapq
        nodes = self.nodes
        if not self.reorder:
            return list(nodes)
        succ = {n.id: [] for n in nodes}
        indeg = {}
        alive = {n.id for n in nodes}
        for n in nodes:
            n.deps = {d for d in n.deps if d.id in alive and d.ev is None}
            indeg[n.id] = len(n.deps)
            for d in n.deps:
                succ[d.id].append(n)
        tail = {}
        for n in reversed(nodes):
            t = 0.0
            for s_ in succ[n.id]:
                v = tail[s_.id] + self.XLAT
                if v > t:
                    t = v
            tail[n.id] = t + n.dur
        est = {n.id: 0.0 for n in nodes}
        wait = {e: [] for e in ENGS}
        avail = {e: [] for e in ENGS}
        for n in nodes:
            if indeg[n.id] == 0:
                heapq.heappush(wait[n.eng], (0.0, n.id, n))
        free = {e: 0.0 for e in ENGS}
        order = []
        left = len(nodes)
        while left:
            best = None
            for e in ENGS:
                if avail[e]:
                    tc = free[e]
                elif wait[e]:
                    tc = max(free[e], wait[e][0][0])
                else:
                    continue
                if best is None or tc < best[0]:
                    best = (tc, e)
            assert best is not None, "dependency cycle in schedule"
            tc, e = best
            w = wait[e]
            av = avail[e]
            while w and w[0][0] <= tc:
                _, i_, n_ = heapq.heappop(w)
                heapq.heappush(av, (-tail[i_], i_, n_))
            _, _, n = heapq.heappop(av)
            free[e] = tc + n.occ
            fin = tc + n.dur
            order.append(n)
            left -= 1
            for s_ in succ[n.id]:
                if est[s_.id] < fin + self.XLAT:
                    est[s_.id] = fin + self.XLAT
                indeg[s_.id] -= 1
                if indeg[s_.id] == 0:
                    heapq.heappush(wait[s_.eng], (est[s_.id], s_.id, s_))
        return order

    def _need(self, eng, evs):
        kn = self.known[eng]
        out = {}
        for k, v in evs:
            if kn.get(k, 0) < v and out.get(k, 0) < v:
                out[k] = v
        for k, v in out.items():
            kn[k] = v
        return list(out.items())

    def _emit_waits(self, eng, waits):
        for k, v in waits:
            self.q[eng].append(lambda e, s=self.sems[k], v=v: e.wait_ge(s, v))

    def flush(self):
        self._close_pe()
        order = self._schedule()
        for n in order:
            evs = [d.ev for d in n.deps if d.ev is not None]
            eng = n.eng
            self._emit_waits(eng, self._need(eng, evs))
            if n.kind == "dma":
                key = n.chan
                self.chan_cnt[key] += 16
                n.ev = (key, self.chan_cnt[key])
                o, i, kw = n.fns[0]
                self.q[eng].append(
                    lambda e, o=o, i=i, s=self.sems[key], kw=kw: e.dma_start(out=o, in_=i, **kw).then_inc(s, 16))
            else:
                self.cnt[eng] += 1
                n.ev = (eng, self.cnt[eng])
                s = self.sems[eng]
                for fn in n.fns[:-1]:
                    self.q[eng].append(lambda e, fn=fn: fn(e))
                self.q[eng].append(lambda e, fn=n.fns[-1], s=s: fn(e).then_inc(s, 1))
        self.nodes = []

    def barrier(self):
        self.flush()
        deps = [(e, self.cnt[e]) for e in ENGS if self.cnt[e] > 0]
        deps += [(k, v) for k, v in self.chan_cnt.items() if v > 0]
        for e in ENGS:
            self._emit_waits(e, self._need(e, deps))

    def emit(self):
        nc = self.nc
        q = self.q
        with nc.Block() as block:
            @block.tensor
            def _(e):
                for f in q["pe"]:
                    f(e)

            @block.scalar
            def _(e):
                for f in q["act"]:
                    f(e)

            @block.vector
            def _(e):
                for f in q["dve"]:
                    f(e)

            @block.gpsimd
            def _(e):
                for f in q["pool"]:
                    f(e)

            @block.sync
            def _(e):
                for f in q["sp"]:
                    f(e)
        self.q = {e: [] for e in ENGS}

    @contextlib.contextmanager
    def phase(self, reorder=True, xlat=None):
        with contextlib.ExitStack() as ph:
            self.stack = ph
            self.chan_next = 0
            self.reorder = reorder
            self.XLAT = xlat if xlat is not None else Sched.XLAT
            yield
            self.barrier()
            self.emit()
        self.stack = self.gstack

    @staticmethod
    def _d(eng, ap):
        n = _nfree(ap)
        if eng == "act":
            return 0.22 + n / 1400.0
        if eng == "dve":
            return 0.12 + n / 1000.0
        return 0.3 + n / 600.0

    def A(self, out, in_, func, r, w, eng="act", **kw):
        return self.op("act", lambda e: e.activation(out=out, in_=in_, func=func, **kw), r, w, dur=self._d("act", out))

    def TT(self, eng, out, in0, in1, op, r, w):
        return self.op(eng, lambda e: e.tensor_tensor(out=out, in0=in0, in1=in1, op=op), r, w, dur=self._d(eng, out))

    def TS(self, eng, out, in0, s1, s2, op0, op1, r, w):
        if s2 is None:
            return self.op(eng, lambda e: e.tensor_scalar(out=out, in0=in0, scalar1=s1, scalar2=None, op0=op0), r, w,
                           dur=self._d(eng, out))
        return self.op(eng, lambda e: e.tensor_scalar(out=out, in0=in0, scalar1=s1, scalar2=s2, op0=op0, op1=op1), r, w,
                       dur=self._d(eng, out))

    def STT(self, eng, out, in0, sc, in1, op0, op1, r, w):
        return self.op(eng, lambda e: e.scalar_tensor_tensor(out=out, in0=in0, scalar=sc, in1=in1, op0=op0, op1=op1), r, w,
                       dur=self._d(eng, out))

    def CP(self, eng, out, in_, r, w):
        if eng == "act":
            return self.op("act", lambda e: e.copy(out=out, in_=in_), r, w, dur=self._d("act", out))
        return self.op(eng, lambda e: e.tensor_copy(out=out, in_=in_), r, w, dur=self._d(eng, out))

    def MS(self, eng, ap, val, w):
        return self.op(eng, lambda e: e.memset(ap, val), (), w, dur=self._d(eng, ap))

    def MM(self, out, lhsT, rhs, start, stop, r, w, signal=True):
        n = _nfree(rhs)
        d = 0.03 + n / 2400.0
        if rhs.dtype == F32:
            d *= 4
        return self.op("pe", lambda e: e.matmul(out, lhsT=lhsT, rhs=rhs, start=start, stop=stop), r, w,
                       signal=signal, dur=d)

    def TR(self, out, in_, ident, r, w, signal=True):
        return self.op("pe", lambda e: e.transpose(out=out, in_=in_, identity=ident), r, w, signal=signal, dur=0.1)


def bc(ap2, n):
    return ap2.unsqueeze(2).to_broadcast([ap2.shape[0], ap2.shape[1], n])


def build(T, depth=DEPTH, dbg=(), nph=99):
    nc = bass.Bass("TRN2", target_bir_lowering=False)
    NT = T // 512
    NKT = T // 128

    def dram(name, shape, dt, kind="Internal"):
        if name in dbg:
            kind = "ExternalOutput"
        return nc.dram_tensor(name, list(shape), dt, kind=kind).ap()

    x_in = dram("x", [T, D], F32, "ExternalInput")
    w_in = dram("w_in", [depth, D, NIN], F32, "ExternalInput")
    w_br = dram("w_branch", [depth, 3, 512, D], F32, "ExternalInput")
    w_o = dram("w_o", [depth, D, D], F32, "ExternalInput")
    w_up = dram("ffn_w_up", [depth, D, 2 * FF], F32, "ExternalInput")
    w_dn = dram("ffn_w_down", [depth, FF, D], F32, "ExternalInput")
    r_wup = dram("rwkv_w_up", [depth, 64, 512], F32, "ExternalInput")
    r_aup = dram("rwkv_a_up", [depth, 64, 512], F32, "ExternalInput")
    r_gup = dram("rwkv_g_up", [depth, 128, 512], F32, "ExternalInput")
    pcd = dram("pc", [depth, 128, NPC], F32, "ExternalInput")
    gfin = dram("final_norm_g", [D], F32, "ExternalInput")
    cst = dram("cst", [128, 128 * 6], F32, "ExternalInput")
    mk = dram("mk", [64, 4, 8, 128], F32, "ExternalInput")
    y_out = dram("y", [T, D], F32, "ExternalOutput")

    xa = dram("xa", [T, D], F32)
    xb = dram("xb", [T, D], F32)
    zc = dram("zc", [1536, T], BF16)
    zr = dram("zr", [1792, T], BF16)
    qT = dram("qT", [512, T], BF16)
    kT = dram("kT", [512, T], BF16)
    zf = dram("zf", [8, T], F32)
    gT = dram("gT", [3072, T], BF16)
    vtm = dram("vtm", [T, 528], BF16)
    cqk = dram("cqk", [8, 6, T], BF16)
    yaT = dram("yaT", [512, T], BF16)
    ycT = dram("ycT", [512, T], BF16)
    ybn = dram("ybn", [512, T], BF16)
    bsc = dram("bsc", [512, T], BF16)
    gsc = dram("gsc", [512, T], BF16)

    with contextlib.ExitStack() as gst:
        S = Sched(nc, gst)
        pcs = [S.sbuf([128, NPC], F32, f"pc{l}") for l in range(depth)]
        cf = S.sbuf([128, 768], F32, "cstf")
        identb = S.sbuf([128, 128], BF16, "identb")
        trib = S.sbuf([128, 128], BF16, "trib")
        o64b = S.sbuf([64, 64], BF16, "o64b")
        bdmb = S.sbuf([128, 128], BF16, "bdmb")
        omka = [S.sbuf([128, 4], F32, f"omka{l}") for l in range(depth)]
        nfb = [S.sbuf([128, 1], F32, f"nfb{l}") for l in range(depth)]
        with S.phase():
            for l in range(depth):
                S.dma("sp", pcs[l][:], pcd[l], writes=[pcs[l]])
            S.dma("sp", cf[:], cst, writes=[cf])
            S.CP("dve", identb[:], cf[:, 0:128], [cf], [identb])
            S.CP("dve", trib[:], cf[:, 128:256], [cf], [trib])
            S.CP("dve", o64b[:], cf[0:64, 384:448], [cf], [o64b])
            S.TS("dve", bdmb[:], cf[:, 256:384], 1.0 / 64, None, ALU.mult, None, [cf], [bdmb])
            for l in range(depth):
                o = PCO["ka"]
                S.TS("dve", omka[l][:], pcs[l][:, o:o + 4], -1.0, 1.0, ALU.mult, ALU.add, [pcs[l]], [omka[l]])
                o = PCO["fb"]
                S.TS("dve", nfb[l][:], pcs[l][:, o:o + 1], -1.0, None, ALU.mult, None, [pcs[l]], [nfb[l]])
        bdones = cf[:, 256:384]
        ones64 = cf[0:64, 384:448]
        onesf = cf[:, 512:640]

        def pcol(l, name, j=0, n=1):
            o = PCO[name] + j
            return pcs[l][:, o:o + n]

        def load_w_bf16(dst, dst_ap_fn, src_ap_fn, pieces, stages, engs=("act", "dve")):
            for i, pc_ in enumerate(pieces):
                st = stages[i % len(stages)]
                sv = st_view(st, pc_)
                S.dma("sp", sv, src_ap_fn(pc_), writes=[st])
                S.CP(engs[i % len(engs)], dst_ap_fn(pc_), sv, [st], [dst(pc_) if callable(dst) else dst])

        def st_view(st, pc_):
            kc, n = pc_[2], pc_[1]
            return st[:, 0:kc * n].rearrange("p (k n) -> p k n", k=kc)

        def rmsnorm_T(xt, gcol, xnT, s, nb):
            ss, rs, junk, xs, pT = nb
            S.MS("pool", ss[:], 0.0, [ss])
            S.A(junk[:], xt[:], AF.Square, [xt], [junk, ss], accum_out=ss[:])
            S.A(rs[:], ss[:], AF.Sqrt, [ss], [rs], scale=1.0 / D, bias=1e-6)
            S.op("dve", lambda e: e.reciprocal(out=rs[:], in_=rs[:]), [rs], [rs])
            S.TS("dve", xs[:], xt[:], rs[:, 0:1], None, ALU.mult, None, [xt, rs], [xs])
            for kc in range(8):
                S.TR(pT[:, kc, :], xs[:, kc * 128:(kc + 1) * 128], identb[:], [xs, identb], [pT], signal=(kc == 7))
            S.TT("dve", xnT[:, :, s * 128:(s + 1) * 128], pT[:], bc(gcol, 128), ALU.mult, [pT], [xnT])

        def phase_inproj(l, xsrc):
            with S.phase():
                wres = S.sbuf([128, 8, NIN], BF16, "wres")
                stages = [S.sbuf([128, 2048], F32, f"stg{i}") for i in range(2)]
                pieces = [(c0, min(256, NIN - c0), 8) for c0 in range(0, NIN, 256)]
                wv = {p[0]: Buf(wres[:, :, p[0]:p[0] + p[1]], f"wv{p[0]}") for p in pieces}
                load_w_bf16(lambda p: wv[p[0]], lambda p: wres[:, :, p[0]:p[0] + p[1]],
                            lambda p: w_in[l, :, p[0]:p[0] + p[1]].rearrange("(k q) n -> q k n", q=128),
                            pieces, stages)
                if CUT == 1:
                    return
                xbufs = [S.sbuf([128, D], F32, f"xb{i}") for i in range(2)]
                nbs = [(S.sbuf([128, 1], F32), S.sbuf([128, 1], F32), S.sbuf([128, D], BF16), S.sbuf([128, D], BF16),
                        S.psum([128, 8, 128], BF16)) for _ in range(2)]
                xnTs = [S.sbuf([128, 8, 512], BF16, f"xnT{i}") for i in range(2)]
                obs = [S.sbuf([128, 512], BF16, f"ob{i}") for i in range(4)]
                fsts = [S.sbuf([8, 512], F32, f"fst{i}") for i in range(2)]
                vsts = [S.sbuf([128, 8, 66], BF16, f"vst{i}") for i in range(2)]
                pss = [S.psum([128, 512], F32, f"psm{i}") for i in range(4)]
                for v in vsts:
                    S.MS("pool", v[:], 1.0, [v])
                chunks = []
                for j in range(12):
                    chunks.append((j * 128, 128, zc, j * 128, "copy"))
                for j in range(14):
                    chunks.append((1536 + j * 128, 128, zr, j * 128, "copy"))
                for j in range(4):
                    chunks.append((3328 + j * 128, 128, qT, j * 128, "qscale"))
                for j in range(4):
                    chunks.append((3840 + j * 128, 128, kT, j * 128, "copyA"))
                chunks.append((4864, 8, zf, 0, "f"))
                for j in range(24):
                    chunks.append((4872 + j * 128, 128, gT, j * 128, "gate"))
                cnt = 0
                for i in range(NT):
                    xnT = xnTs[i % 2]
                    for s in range(4):
                        xt = xbufs[(i * 4 + s) % 2]
                        r0 = (i * 4 + s) * 128
                        S.dma("sp", xt[:], xsrc[r0:r0 + 128, :], writes=[xt])
                        rmsnorm_T(xt, pcol(l, "n1g", 0, 8), xnT, s, nbs[(i * 4 + s) % 2])
                    if CUT == 2:
                        continue
                    for (c0, M, dest, row0, mode) in (chunks[:2] if CUT == 3 else chunks):
                        ps = pss[cnt % 4]
                        for kc in range(8):
                            S.MM(ps[0:M, :], wres[:, kc, c0:c0 + M], xnT[:, kc, :], kc == 0, kc == 7,
                                 [wv[q_] for q_ in range((c0 // 256) * 256, c0 + M, 256)] + [xnT], [ps], signal=(kc == 7))
                        if mode == "f":
                            fs = fsts[i % 2]
                            S.CP("dve", fs[:], ps[0:8, :], [ps], [fs])
                            S.dma("pool", zf[:, i * 512:(i + 1) * 512], fs[:], reads=[fs])
                        else:
                            ob = obs[cnt % 4]
                            if mode == "copy":
                                S.CP("dve", ob[:], ps[:], [ps], [ob])
                            elif mode == "copyA":
                                S.CP("act", ob[:], ps[:], [ps], [ob])
                            elif mode == "qscale":
                                S.A(ob[:], ps[:], AF.Copy, [ps], [ob], scale=0.125)
                            else:
                                j = (c0 - 4872) // 128
                                S.A(ob[:], ps[:], AF.Sigmoid, [ps], [ob], bias=pcol(l, "gateb", j))
                            S.dma("pool", dest[row0:row0 + 128, i * 512:(i + 1) * 512], ob[:], reads=[ob])
                        cnt += 1
                    for s in range(4 if CUT not in (3, 4) else 0):
                        ps = pss[cnt % 4]
                        for kc in range(8):
                            S.MM(ps[:], xnT[:, kc, s * 128:(s + 1) * 128], wres[:, kc, 4352:4864], kc == 0, kc == 7,
                                 [wv[4352], wv[4608], xnT], [ps], signal=(kc == 7))
                        vs = vsts[s % 2]
                        S.CP("act", vs[:, :, 0:64], ps[:].rearrange("p (h d) -> p h d", h=8), [ps], [vs])
                        r0 = (i * 4 + s) * 128
                        S.dma("pool", vtm[r0:r0 + 128, :], vs[:].rearrange("p h d -> p (h d)"), reads=[vs])
                        cnt += 1

        def phase_conv(l):
            TB = min(T, 2048)
            with S.phase():
                ins = [[S.sbuf([128, TB], BF16) for _ in range(3)] for _ in range(2)]
                hbs = [S.sbuf([128, TB + 2], F32) for _ in range(2)]
                acc = [S.sbuf([128, TB], F32) for _ in range(2)]
                outs = [S.sbuf([128, TB], BF16) for _ in range(2)]
                n = 0
                for j in range(4):
                    for tb in range(T // TB):
                        Bt, Ct, ht = ins[n % 2]
                        hb = hbs[n % 2]
                        ac = acc[n % 2]
                        ot = outs[n % 2]
                        cs = slice(tb * TB, (tb + 1) * TB)
                        S.dma("sp", Bt[:], zc[j * 128:(j + 1) * 128, cs], writes=[Bt])
                        S.dma("sp", Ct[:], zc[512 + j * 128:512 + (j + 1) * 128, cs], writes=[Ct])
                        S.dma("sp", ht[:], zc[1024 + j * 128:1024 + (j + 1) * 128, cs], writes=[ht])
                        if tb == 0:
                            S.MS("pool", hb[:, 0:2], 0.0, [hb])
                        else:
                            hp_ = hbs[(n - 1) % 2]
                            S.CP("pool", hb[:, 0:2], hp_[:, TB:TB + 2], [hp_], [hb])
                        S.TT("dve", hb[:, 2:TB + 2], Ct[:], ht[:], ALU.mult, [Ct, ht], [hb])
                        S.A(ac[:], hb[:, 2:TB + 2], AF.Copy, [hb], [ac], scale=pcol(l, "cmw", j * 3 + 2))
                        S.STT("dve", ac[:], hb[:, 1:TB + 1], pcol(l, "cmw", j * 3 + 1), ac[:], ALU.mult, ALU.add, [hb, ac], [ac])
                        S.STT("dve", ac[:], hb[:, 0:TB], pcol(l, "cmw", j * 3 + 0), ac[:], ALU.mult, ALU.add, [hb, ac], [ac])
                        S.TT("dve", ot[:], ac[:], Bt[:], ALU.mult, [ac, Bt], [ot])
                        S.dma("pool", yaT[j * 128:(j + 1) * 128, cs], ot[:], reads=[ot])
                        n += 1

        def phase_fcum(l):
            with S.phase():
                z = S.sbuf([8, T], F32)
                e1 = S.sbuf([8, T], F32)
                cs_ = S.sbuf([8, T], F32)
                r1 = z
                ob = [S.sbuf([8, T], BF16) for _ in range(6)]
                S.dma("sp", z[:], zf, writes=[z])
                S.A(e1[:], z[:], AF.Exp, [z], [e1], scale=-1.0, bias=nfb[l][0:8, 0:1])
                S.A(z[:], e1[:], AF.Ln, [e1], [z], bias=1.0)
                S.op("dve", lambda e: e.tensor_tensor_scan(out=cs_[:], data0=z[:], data1=z[:], initial=0.0,
                                                            op0=ALU.add, op1=ALU.bypass), [z], [cs_])
                S.A(ob[0][:], cs_[:], AF.Copy, [cs_], [ob[0]], scale=-1.0)
                S.STT("dve", r1[:], cs_[:], -1.0, ob[0][:], ALU.mult, ALU.subtract, [cs_, ob[0]], [r1])
                S.CP("act", ob[1][:], r1[:], [r1], [ob[1]])
                S.TT("dve", e1[:], r1[:], ob[1][:], ALU.subtract, [r1, ob[1]], [e1])
                S.CP("act", ob[2][:], e1[:], [e1], [ob[2]])
                for k in range(3):
                    S.A(ob[3 + k][:], ob[k][:], AF.Copy, [ob[k]], [ob[3 + k]], scale=-1.0)
                for k in range(6):
                    S.dma("pool", cqk[:, k, :], ob[k][:], reads=[ob[k]])

        def phase_attn(l):
            NQB = T // 512
            with S.phase():
                vext = S.sbuf([128, NKT, 528], BF16, "vext")
                VS = min(8, NKT)
                for a in range(0, NKT, VS):
                    S.dma("sp", vext[:, a:a + VS, :], vtm[a * 128:(a + VS) * 128, :].rearrange("(n p) c -> p n c", p=128),
                          writes=[vext])
                qas = [S.sbuf([70, T], BF16, f"qa{i}") for i in range(2)]
                kas = [S.sbuf([70, T], BF16, f"ka{i}") for i in range(2)]
                NSL = int(os.environ.get("K_NSL", "5"))
                LOOK = int(os.environ.get("K_LOOK", "3"))
                pts = [S.sbuf([128, 512], BF16, f"pt{i}") for i in range(NSL)]
                osb = [S.sbuf([65, 512], F32, f"osb{i}") for i in range(2)]
                rdn = [S.sbuf([65, 512], F32, f"rdn{i}") for i in range(2)]
                yos = [S.sbuf([64, 512], BF16, f"yo{i}") for i in range(2)]
                psS = [S.psum([128, 512], F32, f"psS{i}") for i in range(NSL)]
                psO = [S.psum([128, 512], F32, f"psO{i}") for i in range(2)]
                psB = [S.psum([128, 512], F32, f"psB{i}") for i in range(1)]
                steps = [(qb, kt) for qb in range(NQB) for kt in range(4 * qb + 4)]
                nst = len(steps)
                gcnt = 0
                nq = 0
                for h in range(8):
                    qa = qas[h % 2]
                    ka = kas[h % 2]
                    S.dma("sp", qa[0:64, :], qT[h * 64:(h + 1) * 64, :], writes=[qa])
                    S.MS("pool", qa[64:70, :], 1.0, [qa])
                    S.dma("sp", qa[64:67, :], cqk[h, 0:3, :], writes=[qa])
                    S.dma("sp", ka[0:64, :], kT[h * 64:(h + 1) * 64, :], writes=[ka])
                    S.MS("pool", ka[64:70, :], 1.0, [ka])
                    S.dma("sp", ka[67:70, :], cqk[h, 3:6, :], writes=[ka])

                    def geom(i):
                        qb, kt = steps[i]
                        j = kt - 4 * qb
                        q0 = j * 128 if j > 0 else 0
                        return qb, kt, j, q0

                    def issue_S(i, qa=qa, ka=ka):
                        qb, kt, j, q0 = geom(i)
                        sp_ = psS[(gcnt + i) % NSL]
                        S.MM(sp_[:, q0:512], ka[:, kt * 128:(kt + 1) * 128], qa[:, qb * 512 + q0:(qb + 1) * 512],
                             True, True, [ka, qa], [sp_])

                    pending = []

                    def epi2(qb, ob, rd, yo, h=h):
                        pb = psB[0]
                        S.MM(pb[0:64, :], cf[64:65, 512:576], rd[64:65, :], True, True, [cf, rd], [pb])
                        S.TT("dve", yo[:], ob[0:64, :], pb[0:64, :], ALU.mult, [ob, pb], [yo])
                        S.dma("pool", ycT[h * 64:(h + 1) * 64, qb * 512:(qb + 1) * 512], yo[:], reads=[yo])

                    for i in range(min(LOOK, nst)):
                        issue_S(i)
                    for i0 in range(0, nst, 2):
                        pair = [i for i in (i0, i0 + 1) if i < nst]
                        for i in pair:
                            qb, kt, j, q0 = geom(i)
                            sp_ = psS[(gcnt + i) % NSL]
                            pt = pts[(gcnt + i) % NSL]
                            S.A(pt[:, q0:512], sp_[:, q0:512], AF.Exp, [sp_], [pt])
                            if j >= 0:
                                S.TT("dve", pt[:, q0:q0 + 128], pt[:, q0:q0 + 128], trib[:], ALU.mult, [pt, trib], [pt])
                        for i in pair:
                            if i + LOOK < nst:
                                issue_S(i + LOOK)
                        for i in pair:
                            qb, kt, j, q0 = geom(i)
                            nkt = 4 * qb + 4
                            pt = pts[(gcnt + i) % NSL]
                            ops_ = psO[(nq + qb) % 2]
                            extra = [pts[(gcnt + k) % NSL] for k in pair]
                            S.MM(ops_[0:65, q0:512], vext[:, kt, h * 66:h * 66 + 65], pt[:, q0:512],
                                 kt == 0, kt == nkt - 1, [vext] + extra, [ops_], signal=(kt == nkt - 1))
                            while pending and pending[0][0] <= i:
                                pending.pop(0)[1]()
                            if kt == nkt - 1:
                                ob = osb[(nq + qb) % 2]
                                rd = rdn[(nq + qb) % 2]
                                yo = yos[(nq + qb) % 2]
                                S.CP("dve", ob[:], ops_[0:65, :], [ops_], [ob])
                                S.op("dve", lambda e, rd=rd, ob=ob: e.reciprocal(out=rd[64:65, :], in_=ob[64:65, :]), [ob], [rd])
                                pending.append((i + 3, lambda qb=qb, ob=ob, rd=rd, yo=yo: epi2(qb, ob, rd, yo)))
                    for _, fn in pending:
                        fn()
                    gcnt += nst
                    nq += NQB

        def phase_rwkv(l):
            NTB = T // 512
            with S.phase(xlat=1.0):
                stg = S.sbuf([128, 512], F32, "rstg")
                waw = S.sbuf([128, 512], BF16, "waw")
                gup = S.sbuf([128, 512], BF16, "gup")
                S.dma("sp", stg[0:64, :], r_wup[l], writes=[stg])
                S.dma("sp", stg[64:128, :], r_aup[l], writes=[stg])
                S.CP("dve", waw[:], stg[:], [stg], [waw])
                S.dma("sp", stg[:], r_gup[l], writes=[stg])
                S.CP("dve", gup[:], stg[:], [stg], [gup])
                mkf = S.sbuf([64, 3, 8, 128], F32, "mkf")
                S.dma("sp", mkf[:], mk[:, 0:3], writes=[mkf])
                mG1 = mkf[:, 0]
                mG2 = mkf[:, 1]
                mL = mkf[:, 2, :, 0:64]
                id8 = mkf[:, 2, :, 64:128]
                St = S.sbuf([64, 8, 64], F32, "St")
                Sb = S.sbuf([64, 8, 64], BF16, "Sb")
                Stmp = S.sbuf([64, 8, 64], F32, "Stmp")
                S.MS("pool", St[:], 0.0, [St])
                S.MS("pool", Sb[:], 0.0, [Sb])
                EQ8 = S.sbuf([64, 8, 8, 2, 64], BF16, "EQ8")
                FB8 = S.sbuf([64, 8, 2, 512], BF16, "FB8")
                pC8 = S.sbuf([64, 8, 8], F32, "pC8")
                EQ = S.sbuf([128, 4, 8, 2, 64], BF16, "EQ")
                FB = S.sbuf([128, 4, 2, 512], BF16, "FB")
                Vb = S.sbuf([128, 4, 512], BF16, "Vb")
                pC = S.sbuf([128, 4, 8], F32, "pC")
                Ftm = S.sbuf([64, 8, 512], BF16, "Ftm")
                nBtm = S.sbuf([64, 8, 512], BF16, "nBtm")
                Vtm = S.sbuf([64, 8, 512], BF16, "Vtm")
                zls = [S.sbuf([128, 514], BF16, f"zl{i}") for i in range(3)]
                f32t = lambda n: S.sbuf([128, 512], F32, n)
                dtmp, twa, sgf, rr, kk_, vv, sig, csf = [f32t(n) for n in ("dtmp", "twa", "sgf", "rr", "kk", "vv", "sig", "csf")]
                csm, pp, pinv, pprev, aa, kap, ksq, rinv, kh, ktl, bet = [f32t(n) for n in (
                    "csm", "pp", "pinv", "pprev", "aa", "kap", "ksq", "rinv", "kh", "ktl", "bet")]
                t1 = dtmp
                rk = ksq
                twab = S.sbuf([128, 512], BF16, "twab")
                sgb = S.sbuf([128, 512], BF16, "sgb")
                offs = S.sbuf([128, 8], F32, "offs")
                gob = [S.sbuf([128, 512], BF16, f"gob{i}") for i in range(2)]
                bob = [S.sbuf([128, 512], BF16, f"bob{i}") for i in range(2)]
                G1bs = [S.sbuf([64, 8, 128], BF16, f"G1b{i}") for i in range(2)]
                G2bs = [S.sbuf([64, 8, 128], BF16, f"G2b{i}") for i in range(2)]
                Pfin = [S.sbuf([64, 8, 64], BF16, f"Pfin{i}") for i in range(2)]
                Ab = [S.sbuf([64, 8, 64], BF16, f"Ab{i}") for i in range(2)]
                Bb = [S.sbuf([64, 8, 64], BF16, f"Bb{i}") for i in range(2)]
                Pb = [S.sbuf([64, 8, 64], BF16, f"Pb{i}") for i in range(2)]
                IBn = S.sbuf([64, 8, 64], BF16, "IBn")
                Xb = S.sbuf([64, 8, 64], BF16, "Xb")
                Ub = S.sbuf([64, 8, 64], BF16, "Ub")
                Ycs = [S.sbuf([64, 512], F32, f"Yc{i}") for i in range(2)]
                ynbs = [S.sbuf([64, 8, 8, 64], BF16, f"ynb{i}") for i in range(2)]
                gsqb = S.sbuf([64, 512], BF16, "gsqb")
                Ycb = S.sbuf([64, 512], BF16, "Ycb")
                gd_ = S.sbuf([64, 512], F32, "gd")
                gm2 = S.sbuf([64, 512], F32, "gm2")
                gmean = S.sbuf([64, 512], F32, "gmean")
                gvar = S.sbuf([64, 512], F32, "gvar")
                QA, QB, QC, RA, RB, RC, W = [S.psum([128, 512], F32, f"rp{i}") for i in range(7)]
                PT = S.psum([64, 8, 128], BF16, "rpT")

                def v3(ap, a):
                    return ap.rearrange("p (a b) -> p a b", a=a)

                nzc = [0]

                def mix(dst, row0, mucol, tb):
                    zl = zls[nzc[0] % 3]
                    nzc[0] += 1
                    c0 = tb * 512
                    if tb == 0:
                        S.MS("pool", zl[:, 0:2], 0.0, [zl])
                        S.dma("sp", zl[:, 2:514], zr[row0:row0 + 128, 0:512], writes=[zl])
                    else:
                        S.dma("sp", zl[:, 1:514], zr[row0:row0 + 128, c0 - 1:c0 + 512], writes=[zl])
                    S.TT("pool", dtmp[:], zl[:, 1:513], zl[:, 2:514], ALU.subtract, [zl], [dtmp])
                    S.STT("dve", dst[:], dtmp[:], mucol, zl[:, 2:514], ALU.mult, ALU.add, [dtmp, zl], [dst])

                def stage1(tb):
                    tcs = slice(tb * 512, (tb + 1) * 512)
                    mix(twa, 1536, pcol(l, "mu", 12), tb)
                    S.A(twab[0:64, :], twa[0:64, :], AF.Tanh, [twa], [twab])
                    S.CP("act", twab[64:128, :], twa[64:128, :], [twa], [twab])
                    yield
                    mix(sgf, 1664, pcol(l, "mu", 13), tb)
                    S.A(sgb[:], sgf[:], AF.Sigmoid, [sgf], [sgb])
                    yield
                    for hp in range(4):
                        hs = slice(hp * 128, (hp + 1) * 128)
                        mix(rr, hp * 128, pcol(l, "mu", hp), tb)
                        yield
                        mix(kk_, 512 + hp * 128, pcol(l, "mu", 4 + hp), tb)
                        yield
                        mix(vv, 1024 + hp * 128, pcol(l, "mu", 8 + hp), tb)
                        yield
                        S.MM(W[:], waw[0:64, hs], twab[0:64, :], True, True, [waw, twab], [W])
                        S.A(sig[:], W[:], AF.Sigmoid, [W], [sig], bias=pcol(l, "w0", hp))
                        yield
                        S.MM(W[:], waw[64:128, hs], twab[64:128, :], True, True, [waw, twab], [W])
                        S.A(aa[:], W[:], AF.Sigmoid, [W], [aa], bias=pcol(l, "a0", hp))
                        yield
                        S.MM(W[:], gup[:, hs], sgb[:], True, True, [gup, sgb], [W])
                        go = gob[(tb * 4 + hp) % 2]
                        S.CP("act", go[:], W[:], [W], [go])
                        S.dma("pool", gsc[hs, tcs], go[:], reads=[go])
                        yield
                        S.op("dve", lambda e: e.tensor_tensor_scan(out=csf[:], data0=sig[:], data1=sig[:], initial=0.0,
                                                                    op0=ALU.add, op1=ALU.bypass), [sig], [csf])
                        S.MS("pool", offs[:, 0:1], 0.0, [offs])
                        yield
                        S.CP("pool", offs[:, 1:8], v3(csf[:], 8)[:, 0:7, 63], [csf], [offs])
                        yield
                        S.TT("dve", v3(csf[:], 8), v3(csf[:], 8), bc(offs[:, 0:8], 64), ALU.subtract, [csf, offs], [csf])
                        yield
                        S.TT("pool", csm[:], csf[:], sig[:], ALU.subtract, [csf, sig], [csm])
                        S.A(pp[:], csf[:], AF.Exp, [csf], [pp], scale=-DS)
                        yield
                        S.A(pinv[:], csf[:], AF.Exp, [csf], [pinv], scale=DS)
                        S.A(pprev[:], csm[:], AF.Exp, [csm], [pprev], scale=-DS)
                        S.CP("pool", pC[:, hp, :], v3(pp[:], 8)[:, :, 63], [pp], [pC])
                        yield
                        S.A(kap[:], kk_[:], AF.Copy, [kk_], [kap], scale=pcol(l, "kk", hp))
                        yield
                        S.A(ksq[:], kap[:], AF.Square, [kap], [ksq])
                        yield
                        S.MM(W[:], bdones, ksq[:], True, True, [cf, ksq], [W])
                        S.A(rinv[:], W[:], AF.Ln, [W], [rinv], bias=1e-12)
                        S.A(rinv[:], rinv[:], AF.Exp, [rinv], [rinv], scale=-0.5)
                        yield
                        S.TS("pool", t1[:], aa[:], pcol(l, "ka", hp), omka[l][:, hp:hp + 1], ALU.mult, ALU.add, [aa, omka[l]], [t1])
                        yield
                        S.TT("dve", kh[:], kap[:], rinv[:], ALU.mult, [kap, rinv], [kh])
                        S.TT("pool", ktl[:], kk_[:], t1[:], ALU.mult, [kk_, t1], [ktl])
                        yield
                        S.TT("pool", bet[:], aa[:], kh[:], ALU.mult, [aa, kh], [bet])
                        S.TT("dve", EQ[:, hp, :, 1, :], v3(rr[:], 8), v3(pp[:], 8), ALU.mult, [rr, pp], [EQ])
                        yield
                        S.TT("pool", EQ[:, hp, :, 0, :], v3(kh[:], 8), v3(pprev[:], 8), ALU.mult, [kh, pprev], [EQ])
                        S.TT("pool", FB[:, hp, 0, :], ktl[:], pinv[:], ALU.mult, [ktl, pinv], [FB])
                        yield
                        S.TT("pool", FB[:, hp, 1, :], bet[:], pinv[:], ALU.mult, [bet, pinv], [FB])
                        S.CP("act", Vb[:, hp, :], vv[:], [vv], [Vb])
                        S.STT("dve", rk[:], rr[:], pcol(l, "rk", hp), ktl[:], ALU.mult, ALU.mult, [rr, ktl], [rk])
                        yield
                        S.MM(W[:], bdones, rk[:], True, True, [cf, rk], [W])
                        bo = bob[(tb * 4 + hp) % 2]
                        S.TT("dve", bo[:], W[:], vv[:], ALU.mult, [W, vv], [bo])
                        S.dma("pool", bsc[hs, tcs], bo[:], reads=[bo])
                        yield

                def stage2():
                    for hp in range(4):
                        hs = slice(hp * 128, (hp + 1) * 128)
                        for (src, dstT, neg) in ((FB[:, hp, 0, :], Ftm, False), (FB[:, hp, 1, :], nBtm, True),
                                                 (Vb[:, hp, :], Vtm, False)):
                            for c in range(8):
                                S.TR(PT[:, c, :], src[:, c * 64:(c + 1) * 64], identb[:], [FB, Vb, identb], [PT],
                                     signal=(c == 7))
                            if neg:
                                S.A(dstT[:, :, hs], PT[:], AF.Copy, [PT], [dstT], scale=-1.0)
                            else:
                                S.CP("dve", dstT[:, :, hs], PT[:], [PT], [dstT])
                    for par in range(2):
                        ps_ = slice(par * 64, par * 64 + 64)
                        S.dma("sp", EQ8[:].rearrange("p (a two) c e t -> p a two (c e t)", two=2)[:, :, par, :],
                              EQ[ps_].rearrange("p a c e t -> p a (c e t)"), reads=[EQ], writes=[EQ8])
                        S.dma("sp", FB8[:].rearrange("p (a two) e t -> p a two (e t)", two=2)[:, :, par, :],
                              FB[ps_].rearrange("p a e t -> p a (e t)"), reads=[FB], writes=[FB8])
                        S.dma("sp", pC8[:].rearrange("p (a two) c -> p a two c", two=2)[:, :, par, :],
                              pC[ps_], reads=[pC], writes=[pC8])

                def streamA(c, par):
                    ccs = slice(c * 64, (c + 1) * 64)
                    G1b, G2b = G1bs[par], G2bs[par]
                    ga = [v3(QA[0:64, :], 4), v3(QB[0:64, :], 4)]
                    bank = [QA, QB]
                    for h in range(8):
                        eq = EQ8[:, h, c, :, :].rearrange("p a b -> p (a b)")
                        S.MM(ga[h // 4][:, h % 4, :], FB8[:, h, 0, ccs], eq, True, True, [FB8, EQ8], [bank[h // 4]],
                             signal=(h % 4 == 3))
                    yield
                    S.TT("dve", G1b[:, 0:4, :], ga[0], mG1[:, 0:4, :], ALU.mult, [QA, mkf], [G1b])
                    S.TT("dve", G1b[:, 4:8, :], ga[1], mG1[:, 4:8, :], ALU.mult, [QB, mkf], [G1b])
                    g3 = v3(QC[0:64, :], 8)
                    for h in range(8):
                        S.MM(g3[:, h, :], EQ8[:, h, c, 0, :], FB8[:, h, 1, ccs], True, True, [FB8, EQ8], [QC], signal=(h == 7))
                    yield
                    for h in range(8):
                        eq = EQ8[:, h, c, :, :].rearrange("p a b -> p (a b)")
                        S.MM(ga[h // 4][:, h % 4, :], FB8[:, h, 1, ccs], eq, True, True, [FB8, EQ8], [bank[h // 4]],
                             signal=(h % 4 == 3))
                    S.TT("dve", Bb[0][:], g3, mL, ALU.mult, [QC, mkf], [Bb[0]])
                    yield
                    S.TT("dve", G2b[:, 0:4, :], ga[0], mG2[:, 0:4, :], ALU.mult, [QA, mkf], [G2b])
                    S.TT("dve", G2b[:, 4:8, :], ga[1], mG2[:, 4:8, :], ALU.mult, [QB, mkf], [G2b])
                    yield
                    S.TT("dve", Pb[0][:], id8, G2b[:, :, 0:64], ALU.subtract, [mkf, G2b], [Pb[0]])
                    yield
                    pa, pb_, pq = v3(QA[0:64, :], 8), v3(QB[0:64, :], 8), v3(QC[0:64, :], 8)
                    for j in range(5):
                        Ac, Bc, Pc = Ab[j % 2], Bb[j % 2], Pb[j % 2]
                        An, Bn = Ab[(j + 1) % 2], Bb[(j + 1) % 2]
                        Pn = Pb[(j + 1) % 2] if j < 4 else Pfin[par]
                        if j == 0:
                            Acb, Acv = G2b, (lambda h: G2b[:, h, 0:64])
                        else:
                            Acb, Acv = Ac, (lambda h, Ac=Ac: Ac[:, h, :])
                        for h in range(8):
                            S.MM(pb_[:, h, :], Acv(h), Bc[:, h, :], True, True, [Acb, Bc], [QB], signal=(h == 7))
                        if j < 4:
                            for h in range(8):
                                S.MM(pa[:, h, :], Bc[:, h, :], Acv(h), True, True, [Acb, Bc], [QA], signal=(h == 7))
                        yield
                        S.CP("act", Bn[:], pb_, [QB], [Bn])
                        if j < 4:
                            S.CP("act", An[:], pa, [QA], [An])
                        yield
                        S.TT("dve", IBn[:], Bn[:], id8, ALU.add, [Bn, mkf], [IBn])
                        yield
                        for h in range(8):
                            S.MM(pq[:, h, :], IBn[:, h, :], Pc[:, h, :], True, True, [IBn, Pc], [QC], signal=(h == 7))
                        yield
                        S.CP("act", Pn[:], pq, [QC], [Pn])
                        yield

                def streamB(c, par, yn):
                    G1b, G2b, Pf = G1bs[par], G2bs[par], Pfin[par]
                    px = v3(RA[0:64, :], 8)
                    for h in range(8):
                        hc = slice(h * 64, (h + 1) * 64)
                        S.MM(px[:, h, :], EQ8[:, h, c, 0, :], Sb[:, h, :], True, False, [EQ8, Sb], [RA], signal=False)
                        S.MM(px[:, h, :], G1b[:, h, 0:64], Vtm[:, c, hc], False, True, [G1b, Vtm], [RA], signal=(h == 7))
                    yield
                    S.CP("act", Xb[:], px, [RA], [Xb])
                    yield
                    pu = v3(RB[0:64, :], 8)
                    for h in range(8):
                        S.MM(pu[:, h, :], Pf[:, h, :], Xb[:, h, :], True, True, [Pf, Xb], [RB], signal=(h == 7))
                    yield
                    S.CP("act", Ub[:], pu, [RB], [Ub])
                    yield
                    pS = v3(RA[0:64, :], 8)
                    for h in range(8):
                        hc = slice(h * 64, (h + 1) * 64)
                        S.MM(pS[:, h, :], Ftm[:, c, hc], Vtm[:, c, hc], True, False, [Ftm, Vtm], [RA], signal=False)
                        S.MM(pS[:, h, :], nBtm[:, c, hc], Ub[:, h, :], False, True, [nBtm, Ub], [RA], signal=(h == 7))
                    py = v3(RC[0:64, :], 8)
                    for h in range(8):
                        hc = slice(h * 64, (h + 1) * 64)
                        S.MM(py[:, h, :], Sb[:, h, :], EQ8[:, h, c, 1, :], True, False, [EQ8, Sb], [RC], signal=False)
                        S.MM(py[:, h, :], Vtm[:, c, hc], G1b[:, h, 64:128], False, False, [G1b, Vtm], [RC], signal=False)
                        S.MM(py[:, h, :], Ub[:, h, :], G2b[:, h, 64:128], False, True, [G2b, Ub], [RC], signal=(h == 7))
                    yield
                    S.TT("dve", Stmp[:], St[:], pS, ALU.add, [St, RA], [Stmp])
                    S.CP("act", yn[:, :, c, :], py, [RC], [yn])
                    yield
                    S.TT("dve", Sb[:], Stmp[:], bc(pC8[:, :, c], 64), ALU.mult, [Stmp, pC8], [Sb])
                    yield
                    S.TT("pool", St[:], Stmp[:], bc(pC8[:, :, c], 64), ALU.mult, [Stmp, pC8], [St])
                    yield

                def drive(gens):
                    gens = [g for g in gens if g is not None]
                    while gens:
                        for g in list(gens):
                            try:
                                next(g)
                            except StopIteration:
                                gens.remove(g)

                drive([stage1(0)])
                stage2()
                gch = 0
                for tb in range(NTB):
                    tcs = slice(tb * 512, (tb + 1) * 512)
                    yn = ynbs[tb % 2]
                    s1 = stage1(tb + 1) if tb + 1 < NTB else None
                    RW = int(os.environ.get("K_RW", "0"))
                    if RW != 1:
                        drive([streamA(0, gch % 2)])
                    for c in range(8 if RW != 1 else 0):
                        ga_ = streamA(c + 1, (gch + 1) % 2) if c < 7 else None
                        gb_ = streamB(c, gch % 2, yn) if RW != 2 else None
                        gens = [g for g in (gb_, ga_) if g is not None]
                        while gens:
                            for g in list(gens):
                                try:
                                    next(g)
                                except StopIteration:
                                    gens.remove(g)
                            if s1 is not None:
                                try:
                                    next(s1)
                                except StopIteration:
                                    s1 = None
                        gch += 1
                    if s1 is not None:
                        drive([s1])
                    S.dma("pool", ybn[:, tcs].rearrange("(h p) t -> p h t", p=64),
                          yn[:].rearrange("p h c t -> p h (c t)"), reads=[yn])
                    if tb + 1 < NTB:
                        stage2()

        def phase_merge(l, xsrc, xdst):
            with S.phase():
                wb = S.sbuf([128, 3, 4, D], BF16, "wb")
                wo = S.sbuf([128, 8, D], BF16, "wo")
                stages = [S.sbuf([128, 2048], F32, f"mstg{i}") for i in range(2)]
                wvb = {}
                for br in range(3):
                    pieces = [(c0, 256, 4) for c0 in range(0, D, 256)]
                    for p in pieces:
                        wvb[(br, p[0])] = Buf(wb[:, br, :, p[0]:p[0] + p[1]], f"wvb{br}_{p[0]}")
                    load_w_bf16(lambda p, br=br: wvb[(br, p[0])], lambda p, br=br: wb[:, br, :, p[0]:p[0] + p[1]],
                                lambda p, br=br: w_br[l, br, :, p[0]:p[0] + p[1]].rearrange("(k q) n -> q k n", q=128),
                                pieces, stages)
                pieces = [(c0, 256, 8) for c0 in range(0, D, 256)]
                wvo = {p[0]: Buf(wo[:, :, p[0]:p[0] + p[1]], f"wvo{p[0]}") for p in pieces}
                load_w_bf16(lambda p: wvo[p[0]], lambda p: wo[:, :, p[0]:p[0] + p[1]],
                            lambda p: w_o[l, :, p[0]:p[0] + p[1]].rearrange("(k q) n -> q k n", q=128), pieces, stages)
                yas = [S.sbuf([128, 4, 512], BF16) for _ in range(2)]
                ycs = [S.sbuf([128, 4, 512], BF16) for _ in range(2)]
                ybs = [S.sbuf([128, 4, 512], BF16) for _ in range(2)]
                bos = [S.sbuf([128, 4, 512], BF16) for _ in range(2)]
                gos = [S.sbuf([128, 4, 512], BF16) for _ in range(2)]
                ybf = [S.sbuf([128, 4, 512], BF16) for _ in range(2)]
                gsets = [(S.sbuf([128, 512], BF16), S.sbuf([128, 512], F32), S.sbuf([128, 512], F32),
                          S.sbuf([128, 512], F32), S.sbuf([128, 512], F32)) for _ in range(2)]
                Gs = [S.sbuf([128, 24, 512], BF16) for _ in range(2)]
                mT = [S.sbuf([128, 8, 512], BF16) for _ in range(2)]
                mTs = [[Buf(mT[k_][:, oc_, :], f"mTs{k_}_{oc_}") for oc_ in range(8)] for k_ in range(2)]
                m1 = [S.sbuf([128, 512], F32) for _ in range(2)]
                _m2 = S.sbuf([128, 512], F32)
                m2 = [_m2, _m2]
                xbufs = [S.sbuf([128, D], F32) for _ in range(2)]
                pbr = [S.psum([128, 512], F32) for _ in range(3)]
                pso = [S.psum([128, 512], F32) for _ in range(2)]
                pgn = [S.psum([128, 512], F32) for _ in range(3)]
                no = 0
                for i in range(NT):
                    tcs = slice(i * 512, (i + 1) * 512)
                    k = i % 2
                    ld = lambda dst, src: S.dma("sp", dst[:], src[:, tcs].rearrange("(c p) t -> p c t", p=128), writes=[dst])
                    ld(yas[k], yaT); ld(ycs[k], ycT); ld(ybs[k], ybn); ld(bos[k], bsc); ld(gos[k], gsc)
                    S.dma("sp", Gs[k][:], gT[:, tcs].rearrange("(c p) t -> p c t", p=128), writes=[Gs[k]])
                    for hp in range(4):
                        yv = ybs[k][:, hp, :]
                        gsqm, gmn, gdm, gm2m, gvm = gsets[hp % 2]
                        pg0 = pgn[(2 * hp) % 3]
                        pg1 = pgn[(2 * hp + 1) % 3]
                        S.A(gsqm[:], yv, AF.Square, [ybs[k]], [gsqm])
                        S.MM(pg0[:], bdmb[:], yv, True, True, [bdmb, ybs[k]], [pg0])
                        S.MM(pg1[:], bdmb[:], gsqm[:], True, True, [bdmb, gsqm], [pg1])
                        S.CP("dve", gmn[:], pg0[:], [pg0], [gmn])
                        S.TT("pool", gdm[:], yv, gmn[:], ALU.subtract, [ybs[k], gmn], [gdm])
                        S.A(gm2m[:], gmn[:], AF.Square, [gmn], [gm2m])
                        S.TT("dve", gvm[:], pg1[:], gm2m[:], ALU.subtract, [pg1, gm2m], [gvm])
                        S.A(gvm[:], gvm[:], AF.Sqrt, [gvm], [gvm], bias=64e-5)
                        S.op("dve", lambda e, gvm=gvm: e.reciprocal(out=gvm[:], in_=gvm[:]), [gvm], [gvm], dur=0.65)
                        S.TT("pool", gdm[:], gdm[:], gvm[:], ALU.mult, [gdm, gvm], [gdm])
                        S.TS("dve", gdm[:], gdm[:], pcol(l, "gng", hp), pcol(l, "gnb", hp), ALU.mult, ALU.add, [gdm], [gdm])
                        S.TT("pool", gdm[:], gdm[:], bos[k][:, hp, :], ALU.add, [gdm, bos[k]], [gdm])
                        S.TT("dve", ybf[k][:, hp, :], gdm[:], gos[k][:, hp, :], ALU.mult, [gdm, gos[k]], [ybf[k]])
                    ysrc = (yas[k], ybf[k], ycs[k])
                    for oc in range(8):
                        pp3 = [pbr[br] for br in range(3)]
                        for br in range(3):
                            for kc in range(4):
                                S.MM(pp3[br][:], wb[:, br, kc, oc * 128:(oc + 1) * 128], ysrc[br][:, kc, :], kc == 0, kc == 3,
                                     [wvb[(br, (oc // 2) * 256)], ysrc[br]], [pp3[br]], signal=(kc == 3))
                        a1, a2 = m1[oc % 2], m2[oc % 2]
                        S.TT("dve", a1[:], pp3[0][:], Gs[k][:, oc, :], ALU.mult, [pp3[0], Gs[k]], [a1])
                        S.TT("dve", a2[:], pp3[1][:], Gs[k][:, 8 + oc, :], ALU.mult, [pp3[1], Gs[k]], [a2])
                        S.TT("pool", a1[:], a1[:], a2[:], ALU.add, [a1, a2], [a1])
                        S.TT("dve", a2[:], pp3[2][:], Gs[k][:, 16 + oc, :], ALU.mult, [pp3[2], Gs[k]], [a2])
                        S.TT("pool", mT[k][:, oc, :], a1[:], a2[:], ALU.add, [a1, a2], [mTs[k][oc]])
                    for s in range(4):
                        xt = xbufs[no % 2]
                        r0 = (i * 4 + s) * 128
                        S.dma("sp", xt[:], xsrc[r0:r0 + 128, :], writes=[xt])
                        for half in range(2):
                            ps = pso[half]
                            for kc in range(8):
                                S.MM(ps[:], mT[k][:, kc, s * 128:(s + 1) * 128], wo[:, kc, half * 512:(half + 1) * 512],
                                     kc == 0, kc == 7, [mTs[k][kc], wvo[half * 512], wvo[half * 512 + 256]], [ps], signal=(kc == 7))
                            S.TT("dve", xt[:, half * 512:(half + 1) * 512], xt[:, half * 512:(half + 1) * 512], ps[:], ALU.add,
                                 [xt, ps], [xt])
                        S.dma("pool", xdst[r0:r0 + 128, :], xt[:], reads=[xt])
                        no += 1

        def phase_ffn(l, xsrc, xdst, final):
            with S.phase():
                wup = S.sbuf([128, 8, 2 * FF], BF16, "wup")
                wdn = S.sbuf([128, 22, D], BF16, "wdn")
                xbufs = [S.sbuf([128, D], F32, f"fx{i}") for i in range(3)]
                pieces = [(c0, 128, 8) for c0 in range(0, 2 * FF, 128)]
                wvu = {p[0]: Buf(wup[:, :, p[0]:p[0] + p[1]], f"wvu{p[0]}") for p in pieces}
                wvd = {}
                load_w_bf16(lambda p: wvu[p[0]], lambda p: wup[:, :, p[0]:p[0] + p[1]],
                            lambda p: w_up[l, :, p[0]:p[0] + p[1]].rearrange("(k q) n -> q k n", q=128), pieces, xbufs)
                ip = 0
                for kc0 in range(0, 22, 2):
                    for half in range(2):
                        st = xbufs[ip % 3]
                        sv = st[:, 0:1024].rearrange("p (k n) -> p k n", k=2)
                        S.dma("sp", sv, w_dn[l, kc0 * 128:(kc0 + 2) * 128, half * 512:(half + 1) * 512].rearrange(
                            "(k q) n -> q k n", q=128), writes=[st])
                        wvd[(kc0, half)] = Buf(wdn[:, kc0:kc0 + 2, half * 512:(half + 1) * 512], f"wvd{kc0}_{half}")
                        S.CP(("act", "dve")[ip % 2], wdn[:, kc0:kc0 + 2, half * 512:(half + 1) * 512], sv, [st], [wvd[(kc0, half)]])
                        ip += 1
                _xs = S.sbuf([128, D], BF16)
                nbs = [(S.sbuf([128, 1], F32), S.sbuf([128, 1], F32), _xs, _xs, S.psum([128, 8, 128], BF16))]
                fxnTs = [S.sbuf([128, 8, 512], BF16, f"fxnT{i}") for i in range(2)]
                actT = S.sbuf([128, 22, 512], BF16, "actT")
                actTs = [Buf(actT[:, j_, :], f"actT{j_}") for j_ in range(22)]
                hg = [S.sbuf([128, 514], F32) for _ in range(2)]
                _hu = S.sbuf([128, 514], F32)
                hu = [_hu, _hu]
                cg = [S.sbuf([128, 512], F32) for _ in range(2)]
                _cu = S.sbuf([128, 512], F32)
                cu = [_cu, _cu]
                carry = S.sbuf([128, 44, 2], F32, "carry")
                S.MS("pool", carry[:], 0.0, [carry])
                gf = None
                if final:
                    gf = S.sbuf([128, D], F32, "gfin")
                    S.dma("sp", gf[:], gfin.partition_broadcast(128), writes=[gf])
                    fss = S.sbuf([128, 1], F32)
                    frs = S.sbuf([128, 1], F32)
                    fj = _xs
                pg = [S.psum([128, 512], F32) for _ in range(2)]
                pu = [S.psum([128, 512], F32) for _ in range(2)]
                pdn = [S.psum([128, 512], F32) for _ in range(2)]
                nx = 0
                for i in range(NT):
                    xnT = fxnTs[i % 2]
                    for s in range(4):
                        xt = xbufs[nx % 3]; nx += 1
                        r0 = (i * 4 + s) * 128
                        S.dma("sp", xt[:], xsrc[r0:r0 + 128, :], writes=[xt])
                        rmsnorm_T(xt, pcol(l, "n2g", 0, 8), xnT, s, nbs[0])
                    for j in range(22):
                        k = j % 2
                        for kc in range(8):
                            S.MM(pg[k][:], wup[:, kc, j * 128:(j + 1) * 128], xnT[:, kc, :], kc == 0, kc == 7, [wvu[j * 128], xnT], [pg[k]],
                                 signal=(kc == 7))
                        for kc in range(8):
                            S.MM(pu[k][:], wup[:, kc, FF + j * 128:FF + (j + 1) * 128], xnT[:, kc, :], kc == 0, kc == 7,
                                 [wvu[FF + j * 128], xnT], [pu[k]], signal=(kc == 7))
                        for (hb, ps, cc, slot) in ((hg[k], pg[k], cg[k], j), (hu[k], pu[k], cu[k], 22 + j)):
                            S.CP("pool", hb[:, 0:2], carry[:, slot, :], [carry], [hb])
                            S.CP("act", hb[:, 2:514], ps[:], [ps], [hb])
                            S.CP("pool", carry[:, slot, :], hb[:, 512:514], [hb], [carry])
                            S.A(cc[:], hb[:, 2:514], AF.Copy, [hb], [cc], scale=pcol(l, "fcw", slot * 3 + 2))
                            S.STT("dve", cc[:], hb[:, 1:513], pcol(l, "fcw", slot * 3 + 1), cc[:], ALU.mult, ALU.add, [hb, cc], [cc])
                            S.STT("dve", cc[:], hb[:, 0:512], pcol(l, "fcw", slot * 3 + 0), cc[:], ALU.mult, ALU.add, [hb, cc], [cc])
                        S.A(cg[k][:], cg[k][:], AF.Silu, [cg[k]], [cg[k]])
                        S.TT("dve", actT[:, j, :], cg[k][:], cu[k][:], ALU.mult, [cg[k], cu[k]], [actTs[j]])
                    for s in range(4):
                        xt = xbufs[nx % 3]; nx += 1
                        r0 = (i * 4 + s) * 128
                        S.dma("sp", xt[:], xsrc[r0:r0 + 128, :], writes=[xt])
                        for half in range(2):
                            ps = pdn[half]
                            for j in range(22):
                                S.MM(ps[:], actT[:, j, s * 128:(s + 1) * 128], wdn[:, j, half * 512:(half + 1) * 512],
                                     j == 0, j == 21, [actTs[j], wvd[((j // 2) * 2, half)]], [ps], signal=(j == 21))
                            S.TT("dve", xt[:, half * 512:(half + 1) * 512], xt[:, half * 512:(half + 1) * 512], ps[:], ALU.add,
                                 [xt, ps], [xt])
                        if final:
                            S.MS("pool", fss[:], 0.0, [fss])
                            S.A(fj[:], xt[:], AF.Square, [xt], [fj, fss], accum_out=fss[:])
                            S.A(frs[:], fss[:], AF.Sqrt, [fss], [frs], scale=1.0 / D, bias=1e-6)
                            S.op("dve", lambda e: e.reciprocal(out=frs[:], in_=frs[:]), [frs], [frs])
                            S.STT("dve", xt[:], xt[:], frs[:, 0:1], gf[:], ALU.mult, ALU.mult, [xt, frs, gf], [xt])
                        S.dma("pool", xdst[r0:r0 + 128, :], xt[:], reads=[xt])

        src = x_in
        phl = []
        for l in range(depth):
            last = (l == depth - 1)
            phl += [lambda l=l, src=src: phase_inproj(l, src), lambda l=l: phase_conv(l), lambda l=l: phase_fcum(l),
                    lambda l=l: phase_attn(l), lambda l=l: phase_rwkv(l), lambda l=l, src=src: phase_merge(l, src, xa),
                    lambda l=l, last=last: phase_ffn(l, xa, y_out if last else xb, last)]
            src = xb
        for f in phl[:nph]:
            f()
        print("instructions:", S.ninstr)
    return nc


def host_consts():
    cst = np.zeros((128, 768), np.float32)
    cst[:, 0:128] = np.eye(128)
    k = np.arange(128)
    cst[:, 128:256] = (k[:, None] <= k[None, :])
    bd = np.zeros((128, 128), np.float32)
    bd[:64, :64] = 1
    bd[64:, 64:] = 1
    cst[:, 256:384] = bd
    cst[:, 384:512] = 1.0 / 64
    cst[:, 512:640] = 1.0
    s = np.arange(64)
    su = (s[:, None] < s[None, :]).astype(np.float32)
    ui = (s[:, None] <= s[None, :]).astype(np.float32)
    sl = (s[:, None] > s[None, :]).astype(np.float32)
    mk = np.zeros((64, 4, 8, 128), np.float32)
    mk[:, 0, :, 0:64] = su[:, None, :]
    mk[:, 0, :, 64:128] = ui[:, None, :]
    mk[:, 1, :, 0:64] = su[:, None, :]
    mk[:, 1, :, 64:128] = -ui[:, None, :]
    mk[:, 2, :, 0:64] = sl[:, None, :]
    mk[:, 2, :, 64:128] = np.eye(64, dtype=np.float32)[:, None, :]
    return cst, mk


def pack_params(inp, depth):
    pc = np.zeros((depth, 128, NPC), np.float32)

    def col(v):
        return np.ascontiguousarray(v.reshape(-1, 128).T)

    for l in range(depth):
        def put(name, arr):
            o = PCO[name]
            pc[l, :arr.shape[0], o:o + arr.shape[1]] = arr
        put("n1g", col(inp["norm1_g"][l]))
        put("n2g", col(inp["norm2_g"][l]))
        put("gateb", col(inp["gate_b"][l]))
        cm = inp["conv_mix_w"][l]
        put("cmw", np.ascontiguousarray(cm.reshape(3, 4, 128).transpose(2, 1, 0).reshape(128, 12)))
        fc = inp["ffn_conv_w"][l]
        put("fcw", np.ascontiguousarray(fc.reshape(3, 44, 128).transpose(2, 1, 0).reshape(128, 132)))
        put("mu", col(inp["rwkv_mu"][l]))
        put("w0", col(inp["rwkv_w0"][l]))
        put("a0", col(inp["rwkv_a0"][l]))
        put("kk", col(inp["rwkv_k_k"][l]))
        put("ka", col(inp["rwkv_k_a"][l]))
        put("rk", col(inp["rwkv_r_k"][l].reshape(-1)))
        put("fb", inp["attn_forget_b"][l].reshape(8, 1))
        put("gng8", np.ascontiguousarray(inp["rwkv_gn_g"][l].reshape(8, 64).T))
        put("gnb8", np.ascontiguousarray(inp["rwkv_gn_b"][l].reshape(8, 64).T))
        put("gng", col(inp["rwkv_gn_g"][l]))
        put("gnb", col(inp["rwkv_gn_b"][l]))
    return pc


_NC_CACHE = {}


def run(inputs, T, nb, depth=DEPTH, dbg=(), nph=99):
    inputs = {k: np.asarray(v) for k, v in inputs.items()}
    key = (T, depth, tuple(dbg), nph)
    if key not in _NC_CACHE:
        _NC_CACHE[key] = build(T, depth, dbg, nph)
    nc = _NC_CACHE[key]
    cst, mk = host_consts()
    pc = pack_params(inputs, depth)
    shared = {
        "w_in": inputs["w_in"][:depth], "w_branch": inputs["w_branch"][:depth], "w_o": inputs["w_o"][:depth],
        "ffn_w_up": inputs["ffn_w_up"][:depth], "ffn_w_down": inputs["ffn_w_down"][:depth],
        "rwkv_w_up": inputs["rwkv_w_up"][:depth], "rwkv_a_up": inputs["rwkv_a_up"][:depth],
        "rwkv_g_up": inputs["rwkv_g_up"][:depth], "pc": pc, "final_norm_g": inputs["final_norm_g"],
        "cst": cst, "mk": mk,
    }
    shared = {k: np.ascontiguousarray(v, dtype=np.float32) for k, v in shared.items()}
    in_maps = []
    for b in range(nb):
        m = dict(shared)
        m["x"] = np.ascontiguousarray(inputs["x"][b], dtype=np.float32)
        in_maps.append(m)
    res = run_bass_kernel_spmd(nc, in_maps, core_ids=list(range(nb)))
    return res.results


def kernel(**inputs):
    res = run(inputs, SEQ, NB)
    return np.stack([np.asarray(r["y"], dtype=np.float32) for r in res], axis=0)
```

```python
import contextlib
import math
import os
CUT = int(os.environ.get("K_CUT", "0"))
SUB = int(os.environ.get("K_SUB", "0"))
import numpy as np
import concourse.bass as bass
import concourse.mybir as mybir
from concourse.bass_utils import run_bass_kernel_spmd

F32 = mybir.dt.float32
BF16 = mybir.dt.bfloat16
AF = mybir.ActivationFunctionType
ALU = mybir.AluOpType

ENGS = ("pe", "act", "dve", "pool", "sp")

D = 1024
NIN = 7944
FF = 2816
SEQ = 8192
NB = 4
DEPTH = 2
DS = math.exp(-0.5)

PCO = {}
_o = 0
for _n, _w in (("n1g", 8), ("n2g", 8), ("gateb", 24), ("cmw", 12), ("fcw", 132), ("mu", 14),
               ("w0", 4), ("a0", 4), ("kk", 4), ("ka", 4), ("rk", 4), ("fb", 1), ("gng8", 8), ("gnb8", 8), ("gng", 4), ("gnb", 4)):
    PCO[_n] = _o
    _o += _w
NPC = _o


class Buf:
    __slots__ = ("t", "name", "last_w", "readers", "chan", "last_dma")

    def __init__(self, t, name):
        self.t = t
        self.name = name
        self.last_w = []
        self.readers = []
        self.chan = None
        self.last_dma = None

    def __getitem__(self, k):
        return self.t[k]


class Node:
    __slots__ = ("id", "eng", "fns", "deps", "dur", "occ", "kind", "chan", "ev", "open", "kw")

    def __init__(self, id, eng, kind):
        self.id = id
        self.eng = eng
        self.kind = kind
        self.fns = []
        self.deps = set()
        self.dur = 0.0
        self.occ = 0.0
        self.chan = None
        self.ev = None
        self.open = False


def _nfree(ap):
    n = 1
    for d in ap.shape[1:]:
        n *= d
    return n


class Sched:
    NCHAN = 64
    XLAT = float(os.environ.get("K_XLAT", "0.3"))

    def __init__(self, nc, stack):
        self.nc = nc
        self.gstack = stack
        self.stack = stack
        self.q = {e: [] for e in ENGS}
        self.cnt = {e: 0 for e in ENGS}
        self.sems = {}
        for e in ENGS:
            self.sems[e] = stack.enter_context(nc.semaphore("s_" + e))
        self.chan_cnt = {}
        for i in range(self.NCHAN):
            k = f"d{i}"
            self.sems[k] = stack.enter_context(nc.semaphore("s_" + k))
            self.chan_cnt[k] = 0
        self.chan_next = 0
        self.known = {e: {} for e in ENGS}
        self.nbuf = 0
        self.ninstr = 0
        self.nodes = []
        self.pe_open = None
        self.sb_bytes = 0
        self.reorder = True

    def sbuf(self, shape, dtype, name=None):
        self.nbuf += 1
        name = f"{name or 'sb'}_{self.nbuf}"
        nb = int(np.prod(shape[1:])) * (2 if dtype == BF16 else 4)
        self.sb_bytes += ((nb + 31) // 32) * 32
        return Buf(self.stack.enter_context(self.nc.sbuf_tensor(name, list(shape), dtype)), name)

    def psum(self, shape, dtype, name=None):
        self.nbuf += 1
        name = f"{name or 'ps'}_{self.nbuf}"
        return Buf(self.stack.enter_context(self.nc.psum_tensor(name, list(shape), dtype)), name)

    def _chan(self, b):
        if b.chan is None:
            assert self.chan_next < self.NCHAN, "out of dma channels"
            b.chan = f"d{self.chan_next}"
            self.chan_next += 1
        return b.chan

    def _close_pe(self):
        if self.pe_open is not None:
            self.pe_open.open = False
            self.pe_open = None

    def _deps_of(self, reads, writes):
        deps = set()
        for r in reads:
            deps.update(r.last_w)
        for w in writes:
            deps.update(w.last_w)
            deps.update(w.readers)
        return deps

    def _touch(self, node, reads, writes):
        for r in reads:
            if not r.readers or r.readers[-1] is not node:
                r.readers.append(node)
        for w in writes:
            w.last_w = [node]
            w.readers = []

    def op(self, eng, fn, reads=(), writes=(), signal=True, same_ok=False, dur=0.3):
        self.ninstr += 1
        deps = self._deps_of(reads, writes)
        if eng == "pe":
            node = self.pe_open
            if node is None:
                node = Node(len(self.nodes), "pe", "op")
                self.nodes.append(node)
                node.open = True
                self.pe_open = node
            deps.discard(node)
            node.deps |= deps
            node.fns.append(fn)
            node.dur += dur
            node.occ += dur
            if signal:
                self._close_pe()
        else:
            self._close_pe()
            node = Node(len(self.nodes), eng, "op")
            self.nodes.append(node)
            node.deps = deps
            node.fns.append(fn)
            node.dur = dur
            node.occ = dur
        self._touch(node, reads, writes)
        return node

    def dma(self, eng, out_ap, in_ap, reads=(), writes=(), **kw):
        self._close_pe()
        self.ninstr += 1
        cb = writes[0] if writes else reads[0]
        key = self._chan(cb)
        node = Node(len(self.nodes), eng, "dma")
        self.nodes.append(node)
        node.chan = key
        node.deps = self._deps_of(reads, writes)
        if cb.last_dma is not None:
            node.deps.add(cb.last_dma)
        cb.last_dma = node
        nbytes = _nfree(out_ap) * out_ap.shape[0] * (2 if out_ap.dtype == BF16 else 4)
        node.dur = 2.0 + nbytes / 1.0e5
        node.occ = 0.1 if eng == "sp" else 0.6
        node.fns.append((out_ap, in_ap, kw))
        self._touch(node, reads, writes)
        return node

    def _schedule(self):
        import heapq
        nodes = self.nodes
        if not self.reorder:
            return list(nodes)
        succ = {n.id: [] for n in nodes}
        indeg = {}
        alive = {n.id for n in nodes}
        for n in nodes:
            n.deps = {d for d in n.deps if d.id in alive and d.ev is None}
            indeg[n.id] = len(n.deps)
            for d in n.deps:
                succ[d.id].append(n)
        tail = {}
        for n in reversed(nodes):
            t = 0.0
            for s_ in succ[n.id]:
                v = tail[s_.id] + self.XLAT
                if v > t:
                    t = v
            tail[n.id] = t + n.dur
        est = {n.id: 0.0 for n in nodes}
        wait = {e: [] for e in ENGS}
        avail = {e: [] for e in ENGS}
        for n in nodes:
            if indeg[n.id] == 0:
                heapq.heappush(wait[n.eng], (0.0, n.id, n))
        free = {e: 0.0 for e in ENGS}
        order = []
        left = len(nodes)
        while left:
            best = None
            for e in ENGS:
                if avail[e]:
                    tc = free[e]
                elif wait[e]:
                    tc = max(free[e], wait[e][0][0])
                else:
                    continue
                if best is None or tc < best[0]:
                    best = (tc, e)
            assert best is not None, "dependency cycle in schedule"
            tc, e = best
            w = wait[e]
            av = avail[e]
            while w and w[0][0] <= tc:
                _, i_, n_ = heapq.heappop(w)
                heapq.heappush(av, (-tail[i_], i_, n_))
            _, _, n = heapq.heappop(av)
            free[e] = tc + n.occ
            fin = tc + n.dur
            order.append(n)
            left -= 1
            for s_ in succ[n.id]:
                if est[s_.id] < fin + self.XLAT:
                    est[s_.id] = fin + self.XLAT
                indeg[s_.id] -= 1
                if indeg[s_.id] == 0:
                    heapq.heappush(wait[s_.eng], (est[s_.id], s_.id, s_))
        return order

    def _need(self, eng, evs):
        kn = self.known[eng]
        out = {}
        for k, v in evs:
            if kn.get(k, 0) < v and out.get(k, 0) < v:
                out[k] = v
        for k, v in out.items():
            kn[k] = v
        return list(out.items())

    def _emit_waits(self, eng, waits):
        for k, v in waits:
            self.q[eng].append(lambda e, s=self.sems[k], v=v: e.wait_ge(s, v))

    def flush(self):
        self._close_pe()
        order = self._schedule()
        for n in order:
            evs = [d.ev for d in n.deps if d.ev is not None]
            eng = n.eng
            self._emit_waits(eng, self._need(eng, evs))
            if n.kind == "dma":
                key = n.chan
                self.chan_cnt[key] += 16
                n.ev = (key, self.chan_cnt[key])
                o, i, kw = n.fns[0]
                self.q[eng].append(
                    lambda e, o=o, i=i, s=self.sems[key], kw=kw: e.dma_start(out=o, in_=i, **kw).then_inc(s, 16))
            else:
                self.cnt[eng] += 1
                n.ev = (eng, self.cnt[eng])
                s = self.sems[eng]
                for fn in n.fns[:-1]:
                    self.q[eng].append(lambda e, fn=fn: fn(e))
                self.q[eng].append(lambda e, fn=n.fns[-1], s=s: fn(e).then_inc(s, 1))
        self.nodes = []

    def barrier(self):
        self.flush()
        deps = [(e, self.cnt[e]) for e in ENGS if self.cnt[e] > 0]
        deps += [(k, v) for k, v in self.chan_cnt.items() if v > 0]
        for e in ENGS:
            self._emit_waits(e, self._need(e, deps))

    def emit(self):
        nc = self.nc
        q = self.q
        with nc.Block() as block:
            @block.tensor
            def _(e):
                for f in q["pe"]:
                    f(e)

            @block.scalar
            def _(e):
                for f in q["act"]:
                    f(e)

            @block.vector
            def _(e):
                for f in q["dve"]:
                    f(e)

            @block.gpsimd
            def _(e):
                for f in q["pool"]:
                    f(e)

            @block.sync
            def _(e):
                for f in q["sp"]:
                    f(e)
        self.q = {e: [] for e in ENGS}

    @contextlib.contextmanager
    def phase(self, reorder=True, xlat=None):
        with contextlib.ExitStack() as ph:
            self.stack = ph
            self.chan_next = 0
            self.reorder = reorder
            self.XLAT = xlat if xlat is not None else Sched.XLAT
            yield
            self.barrier()
            self.emit()
        self.stack = self.gstack

    @staticmethod
    def _d(eng, ap):
        n = _nfree(ap)
        if eng == "act":
            return 0.22 + n / 1400.0
        if eng == "dve":
            return 0.12 + n / 1000.0
        return 0.3 + n / 600.0

    def A(self, out, in_, func, r, w, eng="act", **kw):
        return self.op("act", lambda e: e.activation(out=out, in_=in_, func=func, **kw), r, w, dur=self._d("act", out))

    def TT(self, eng, out, in0, in1, op, r, w):
        return self.op(eng, lambda e: e.tensor_tensor(out=out, in0=in0, in1=in1, op=op), r, w, dur=self._d(eng, out))

    def TS(self, eng, out, in0, s1, s2, op0, op1, r, w):
        if s2 is None:
            return self.op(eng, lambda e: e.tensor_scalar(out=out, in0=in0, scalar1=s1, scalar2=None, op0=op0), r, w,
                           dur=self._d(eng, out))
        return self.op(eng, lambda e: e.tensor_scalar(out=out, in0=in0, scalar1=s1, scalar2=s2, op0=op0, op1=op1), r, w,
                       dur=self._d(eng, out))

    def STT(self, eng, out, in0, sc, in1, op0, op1, r, w):
        return self.op(eng, lambda e: e.scalar_tensor_tensor(out=out, in0=in0, scalar=sc, in1=in1, op0=op0, op1=op1), r, w,
                       dur=self._d(eng, out))

    def CP(self, eng, out, in_, r, w):
        if eng == "act":
            return self.op("act", lambda e: e.copy(out=out, in_=in_), r, w, dur=self._d("act", out))
        return self.op(eng, lambda e: e.tensor_copy(out=out, in_=in_), r, w, dur=self._d(eng, out))

    def MS(self, eng, ap, val, w):
        return self.op(eng, lambda e: e.memset(ap, val), (), w, dur=self._d(eng, ap))

    def MM(self, out, lhsT, rhs, start, stop, r, w, signal=True):
        n = _nfree(rhs)
        d = 0.03 + n / 2400.0
        if rhs.dtype == F32:
            d *= 4
        return self.op("pe", lambda e: e.matmul(out, lhsT=lhsT, rhs=rhs, start=start, stop=stop), r, w,
                       signal=signal, dur=d)

    def TR(self, out, in_, ident, r, w, signal=True):
        return self.op("pe", lambda e: e.transpose(out=out, in_=in_, identity=ident), r, w, signal=signal, dur=0.1)


def bc(ap2, n):
    return ap2.unsqueeze(2).to_broadcast([ap2.shape[0], ap2.shape[1], n])


def build(T, depth=DEPTH, dbg=(), nph=99):
    nc = bass.Bass("TRN2", target_bir_lowering=False)
    NT = T // 512
    NKT = T // 128

    def dram(name, shape, dt, kind="Internal"):
        if name in dbg:
            kind = "ExternalOutput"
        return nc.dram_tensor(name, list(shape), dt, kind=kind).ap()

    x_in = dram("x", [T, D], F32, "ExternalInput")
    w_in = dram("w_in", [depth, D, NIN], F32, "ExternalInput")
    w_br = dram("w_branch", [depth, 3, 512, D], F32, "ExternalInput")
    w_o = dram("w_o", [depth, D, D], F32, "ExternalInput")
    w_up = dram("ffn_w_up", [depth, D, 2 * FF], F32, "ExternalInput")
    w_dn = dram("ffn_w_down", [depth, FF, D], F32, "ExternalInput")
    r_wup = dram("rwkv_w_up", [depth, 64, 512], F32, "ExternalInput")
    r_aup = dram("rwkv_a_up", [depth, 64, 512], F32, "ExternalInput")
    r_gup = dram("rwkv_g_up", [depth, 128, 512], F32, "ExternalInput")
    pcd = dram("pc", [depth, 128, NPC], F32, "ExternalInput")
    gfin = dram("final_norm_g", [D], F32, "ExternalInput")
    cst = dram("cst", [128, 128 * 6], F32, "ExternalInput")
    mk = dram("mk", [64, 4, 8, 128], F32, "ExternalInput")
    y_out = dram("y", [T, D], F32, "ExternalOutput")

    xa = dram("xa", [T, D], F32)
    xb = dram("xb", [T, D], F32)
    zc = dram("zc", [1536, T], BF16)
    zr = dram("zr", [1792, T], BF16)
    qT = dram("qT", [512, T], BF16)
    kT = dram("kT", [512, T], BF16)
    zf = dram("zf", [8, T], F32)
    gT = dram("gT", [3072, T], BF16)
    vtm = dram("vtm", [T, 528], BF16)
    cqk = dram("cqk", [8, 6, T], BF16)
    yaT = dram("yaT", [512, T], BF16)
    ycT = dram("ycT", [512, T], BF16)
    ybn = dram("ybn", [512, T], BF16)
    bsc = dram("bsc", [512, T], BF16)
    gsc = dram("gsc", [512, T], BF16)

    with contextlib.ExitStack() as gst:
        S = Sched(nc, gst)
        pcs = [S.sbuf([128, NPC], F32, f"pc{l}") for l in range(depth)]
        cf = S.sbuf([128, 768], F32, "cstf")
        identb = S.sbuf([128, 128], BF16, "identb")
        trib = S.sbuf([128, 128], BF16, "trib")
        o64b = S.sbuf([64, 64], BF16, "o64b")
        bdmb = S.sbuf([128, 128], BF16, "bdmb")
        omka = [S.sbuf([128, 4], F32, f"omka{l}") for l in range(depth)]
        nfb = [S.sbuf([128, 1], F32, f"nfb{l}") for l in range(depth)]
        with S.phase():
            for l in range(depth):
                S.dma("sp", pcs[l][:], pcd[l], writes=[pcs[l]])
            S.dma("sp", cf[:], cst, writes=[cf])
            S.CP("dve", identb[:], cf[:, 0:128], [cf], [identb])
            S.CP("dve", trib[:], cf[:, 128:256], [cf], [trib])
            S.CP("dve", o64b[:], cf[0:64, 384:448], [cf], [o64b])
            S.TS("dve", bdmb[:], cf[:, 256:384], 1.0 / 64, None, ALU.mult, None, [cf], [bdmb])
            for l in range(depth):
                o = PCO["ka"]
                S.TS("dve", omka[l][:], pcs[l][:, o:o + 4], -1.0, 1.0, ALU.mult, ALU.add, [pcs[l]], [omka[l]])
                o = PCO["fb"]
                S.TS("dve", nfb[l][:], pcs[l][:, o:o + 1], -1.0, None, ALU.mult, None, [pcs[l]], [nfb[l]])
        bdones = cf[:, 256:384]
        ones64 = cf[0:64, 384:448]
        onesf = cf[:, 512:640]

        def pcol(l, name, j=0, n=1):
            o = PCO[name] + j
            return pcs[l][:, o:o + n]

        def load_w_bf16(dst, dst_ap_fn, src_ap_fn, pieces, stages, engs=("act", "dve")):
            for i, pc_ in enumerate(pieces):
                st = stages[i % len(stages)]
                sv = st_view(st, pc_)
                S.dma("sp", sv, src_ap_fn(pc_), writes=[st])
                S.CP(engs[i % len(engs)], dst_ap_fn(pc_), sv, [st], [dst])

        def st_view(st, pc_):
            kc, n = pc_[2], pc_[1]
            return st[:, 0:kc * n].rearrange("p (k n) -> p k n", k=kc)

        def rmsnorm_T(xt, gcol, xnT, s, nb):
            ss, rs, junk, xs, pT = nb
            S.MS("pool", ss[:], 0.0, [ss])
            S.A(junk[:], xt[:], AF.Square, [xt], [junk, ss], accum_out=ss[:])
            S.A(rs[:], ss[:], AF.Sqrt, [ss], [rs], scale=1.0 / D, bias=1e-6)
            S.op("dve", lambda e: e.reciprocal(out=rs[:], in_=rs[:]), [rs], [rs])
            S.TS("dve", xs[:], xt[:], rs[:, 0:1], None, ALU.mult, None, [xt, rs], [xs])
            for kc in range(8):
                S.TR(pT[:, kc, :], xs[:, kc * 128:(kc + 1) * 128], identb[:], [xs, identb], [pT], signal=(kc == 7))
            S.TT("dve", xnT[:, :, s * 128:(s + 1) * 128], pT[:], bc(gcol, 128), ALU.mult, [pT], [xnT])

        def phase_inproj(l, xsrc):
            with S.phase():
                wres = S.sbuf([128, 8, NIN], BF16, "wres")
                stages = [S.sbuf([128, 2048], F32, f"stg{i}") for i in range(2)]
                pieces = [(c0, min(256, NIN - c0), 8) for c0 in range(0, NIN, 256)]
                load_w_bf16(wres, lambda p: wres[:, :, p[0]:p[0] + p[1]],
                            lambda p: w_in[l, :, p[0]:p[0] + p[1]].rearrange("(k q) n -> q k n", q=128),
                            pieces, stages)
                if CUT == 1:
                    return
                xbufs = [S.sbuf([128, D], F32, f"xb{i}") for i in range(2)]
                nbs = [(S.sbuf([128, 1], F32), S.sbuf([128, 1], F32), S.sbuf([128, D], BF16), S.sbuf([128, D], BF16),
                        S.psum([128, 8, 128], BF16)) for _ in range(2)]
                xnTs = [S.sbuf([128, 8, 512], BF16, f"xnT{i}") for i in range(2)]
                obs = [S.sbuf([128, 512], BF16, f"ob{i}") for i in range(4)]
                fsts = [S.sbuf([8, 512], F32, f"fst{i}") for i in range(2)]
                vsts = [S.sbuf([128, 8, 66], BF16, f"vst{i}") for i in range(2)]
                pss = [S.psum([128, 512], F32, f"psm{i}") for i in range(4)]
                for v in vsts:
                    S.MS("pool", v[:], 1.0, [v])
                chunks = []
                for j in range(12):
                    chunks.append((j * 128, 128, zc, j * 128, "copy"))
                for j in range(14):
                    chunks.append((1536 + j * 128, 128, zr, j * 128, "copy"))
                for j in range(4):
                    chunks.append((3328 + j * 128, 128, qT, j * 128, "qscale"))
                for j in range(4):
                    chunks.append((3840 + j * 128, 128, kT, j * 128, "copyA"))
                chunks.append((4864, 8, zf, 0, "f"))
                for j in range(24):
                    chunks.append((4872 + j * 128, 128, gT, j * 128, "gate"))
                cnt = 0
                for i in range(NT):
                    xnT = xnTs[i % 2]
                    for s in range(4):
                        xt = xbufs[(i * 4 + s) % 2]
                        r0 = (i * 4 + s) * 128
                        S.dma("sp", xt[:], xsrc[r0:r0 + 128, :], writes=[xt])
                        rmsnorm_T(xt, pcol(l, "n1g", 0, 8), xnT, s, nbs[(i * 4 + s) % 2])
                    if CUT == 2:
                        continue
                    for (c0, M, dest, row0, mode) in (chunks[:2] if CUT == 3 else chunks):
                        ps = pss[cnt % 4]
                        for kc in range(8):
                            S.MM(ps[0:M, :], wres[:, kc, c0:c0 + M], xnT[:, kc, :], kc == 0, kc == 7,
                                 [wres, xnT], [ps], signal=(kc == 7))
                        if mode == "f":
                            fs = fsts[i % 2]
                            S.CP("dve", fs[:], ps[0:8, :], [ps], [fs])
                            S.dma("pool", zf[:, i * 512:(i + 1) * 512], fs[:], reads=[fs])
                        else:
                            ob = obs[cnt % 4]
                            if mode == "copy":
                                S.CP("dve", ob[:], ps[:], [ps], [ob])
                            elif mode == "copyA":
                                S.CP("act", ob[:], ps[:], [ps], [ob])
                            elif mode == "qscale":
                                S.A(ob[:], ps[:], AF.Copy, [ps], [ob], scale=0.125)
                            else:
                                j = (c0 - 4872) // 128
                                S.A(ob[:], ps[:], AF.Sigmoid, [ps], [ob], bias=pcol(l, "gateb", j))
                            S.dma("pool", dest[row0:row0 + 128, i * 512:(i + 1) * 512], ob[:], reads=[ob])
                        cnt += 1
                    for s in range(4 if CUT not in (3, 4) else 0):
                        ps = pss[cnt % 4]
                        for kc in range(8):
                            S.MM(ps[:], xnT[:, kc, s * 128:(s + 1) * 128], wres[:, kc, 4352:4864], kc == 0, kc == 7,
                                 [wres, xnT], [ps], signal=(kc == 7))
                        vs = vsts[s % 2]
                        S.CP("act", vs[:, :, 0:64], ps[:].rearrange("p (h d) -> p h d", h=8), [ps], [vs])
                        r0 = (i * 4 + s) * 128
                        S.dma("pool", vtm[r0:r0 + 128, :], vs[:].rearrange("p h d -> p (h d)"), reads=[vs])
                        cnt += 1

        def phase_conv(l, own_phase=True):
            TB = min(T, 1024)
            with (S.phase() if own_phase else contextlib.nullcontext()):
                ins = [[S.sbuf([128, TB], BF16) for _ in range(3)] for _ in range(2)]
                hbs = [S.sbuf([128, TB + 2], F32) for _ in range(2)]
                acc = [S.sbuf([128, TB], F32) for _ in range(2)]
                outs = [S.sbuf([128, TB], BF16) for _ in range(2)]
                n = 0
                for j in range(4):
                    for tb in range(T // TB):
                        Bt, Ct, ht = ins[n % 2]
                        hb = hbs[n % 2]
                        ac = acc[n % 2]
                        ot = outs[n % 2]
                        cs = slice(tb * TB, (tb + 1) * TB)
                        S.dma("sp", Bt[:], zc[j * 128:(j + 1) * 128, cs], writes=[Bt])
                        S.dma("sp", Ct[:], zc[512 + j * 128:512 + (j + 1) * 128, cs], writes=[Ct])
                        S.dma("sp", ht[:], zc[1024 + j * 128:1024 + (j + 1) * 128, cs], writes=[ht])
                        if tb == 0:
                            S.MS("pool", hb[:, 0:2], 0.0, [hb])
                        else:
                            hp_ = hbs[(n - 1) % 2]
                            S.CP("pool", hb[:, 0:2], hp_[:, TB:TB + 2], [hp_], [hb])
                        S.TT("dve", hb[:, 2:TB + 2], Ct[:], ht[:], ALU.mult, [Ct, ht], [hb])
                        S.A(ac[:], hb[:, 2:TB + 2], AF.Copy, [hb], [ac], scale=pcol(l, "cmw", j * 3 + 2))
                        S.STT("dve", ac[:], hb[:, 1:TB + 1], pcol(l, "cmw", j * 3 + 1), ac[:], ALU.mult, ALU.add, [hb, ac], [ac])
                        S.STT("dve", ac[:], hb[:, 0:TB], pcol(l, "cmw", j * 3 + 0), ac[:], ALU.mult, ALU.add, [hb, ac], [ac])
                        S.TT("dve", ot[:], ac[:], Bt[:], ALU.mult, [ac, Bt], [ot])
                        S.dma("pool", yaT[j * 128:(j + 1) * 128, cs], ot[:], reads=[ot])
                        n += 1

        def phase_fcum(l):
            with S.phase():
                z = S.sbuf([8, T], F32)
                e1 = S.sbuf([8, T], F32)
                cs_ = S.sbuf([8, T], F32)
                r1 = z
                ob = [S.sbuf([8, T], BF16) for _ in range(6)]
                S.dma("sp", z[:], zf, writes=[z])
                S.A(e1[:], z[:], AF.Exp, [z], [e1], scale=-1.0, bias=nfb[l][0:8, 0:1])
                S.A(z[:], e1[:], AF.Ln, [e1], [z], bias=1.0)
                S.op("dve", lambda e: e.tensor_tensor_scan(out=cs_[:], data0=z[:], data1=z[:], initial=0.0,
                                                            op0=ALU.add, op1=ALU.bypass), [z], [cs_])
                S.A(ob[0][:], cs_[:], AF.Copy, [cs_], [ob[0]], scale=-1.0)
                S.STT("dve", r1[:], cs_[:], -1.0, ob[0][:], ALU.mult, ALU.subtract, [cs_, ob[0]], [r1])
                S.CP("act", ob[1][:], r1[:], [r1], [ob[1]])
                S.TT("dve", e1[:], r1[:], ob[1][:], ALU.subtract, [r1, ob[1]], [e1])
                S.CP("act", ob[2][:], e1[:], [e1], [ob[2]])
                for k in range(3):
                    S.A(ob[3 + k][:], ob[k][:], AF.Copy, [ob[k]], [ob[3 + k]], scale=-1.0)
                for k in range(6):
                    S.dma("pool", cqk[:, k, :], ob[k][:], reads=[ob[k]])

        def phase_attn(l):
            NQB = T // 512
            with S.phase():
                phase_conv(l, own_phase=False)
                vext = S.sbuf([128, NKT, 528], BF16, "vext")
                VS = min(8, NKT)
                for a in range(0, NKT, VS):
                    S.dma("sp", vext[:, a:a + VS, :], vtm[a * 128:(a + VS) * 128, :].rearrange("(n p) c -> p n c", p=128),
                          writes=[vext])
                qas = [S.sbuf([70, T], BF16, f"qa{i}") for i in range(2)]
                kas = [S.sbuf([70, T], BF16, f"ka{i}") for i in range(2)]
                NSL = int(os.environ.get("K_NSL", "5"))
                LOOK = int(os.environ.get("K_LOOK", "3"))
                pts = [S.sbuf([128, 512], BF16, f"pt{i}") for i in range(NSL)]
                osb = [S.sbuf([65, 512], F32, f"osb{i}") for i in range(2)]
                rdn = [S.sbuf([65, 512], F32, f"rdn{i}") for i in range(2)]
                yos = [S.sbuf([64, 512], BF16, f"yo{i}") for i in range(2)]
                psS = [S.psum([128, 512], F32, f"psS{i}") for i in range(NSL)]
                psO = [S.psum([128, 512], F32, f"psO{i}") for i in range(2)]
                psB = [S.psum([128, 512], F32, f"psB{i}") for i in range(1)]
                steps = [(qb, kt) for qb in range(NQB) for kt in range(4 * qb + 4)]
                nst = len(steps)
                gcnt = 0
                nq = 0
                for h in range(8):
                    qa = qas[h % 2]
                    ka = kas[h % 2]
                    S.dma("sp", qa[0:64, :], qT[h * 64:(h + 1) * 64, :], writes=[qa])
                    S.MS("pool", qa[64:70, :], 1.0, [qa])
                    S.dma("sp", qa[64:67, :], cqk[h, 0:3, :], writes=[qa])
                    S.dma("sp", ka[0:64, :], kT[h * 64:(h + 1) * 64, :], writes=[ka])
                    S.MS("pool", ka[64:70, :], 1.0, [ka])
                    S.dma("sp", ka[67:70, :], cqk[h, 3:6, :], writes=[ka])

                    def geom(i):
                        qb, kt = steps[i]
                        j = kt - 4 * qb
                        q0 = j * 128 if j > 0 else 0
                        return qb, kt, j, q0

                    def issue_S(i, qa=qa, ka=ka):
                        qb, kt, j, q0 = geom(i)
                        sp_ = psS[(gcnt + i) % NSL]
                        S.MM(sp_[:, q0:512], ka[:, kt * 128:(kt + 1) * 128], qa[:, qb * 512 + q0:(qb + 1) * 512],
                             True, True, [ka, qa], [sp_])

                    pending = []

                    def epi2(qb, ob, rd, yo, h=h):
                        pb = psB[0]
                        S.MM(pb[0:64, :], cf[64:65, 512:576], rd[64:65, :], True, True, [cf, rd], [pb])
                        S.TT("dve", yo[:], ob[0:64, :], pb[0:64, :], ALU.mult, [ob, pb], [yo])
                        S.dma("pool", ycT[h * 64:(h + 1) * 64, qb * 512:(qb + 1) * 512], yo[:], reads=[yo])

                    for i in range(min(LOOK, nst)):
                        issue_S(i)
                    for i0 in range(0, nst, 2):
                        pair = [i for i in (i0, i0 + 1) if i < nst]
                        for i in pair:
                            qb, kt, j, q0 = geom(i)
                            sp_ = psS[(gcnt + i) % NSL]
                            pt = pts[(gcnt + i) % NSL]
                            S.A(pt[:, q0:512], sp_[:, q0:512], AF.Exp, [sp_], [pt])
                            if j >= 0:
                                S.TT("dve", pt[:, q0:q0 + 128], pt[:, q0:q0 + 128], trib[:], ALU.mult, [pt, trib], [pt])
                        for i in pair:
                            if i + LOOK < nst:
                                issue_S(i + LOOK)
                        for i in pair:
                            qb, kt, j, q0 = geom(i)
                            nkt = 4 * qb + 4
                            pt = pts[(gcnt + i) % NSL]
                            ops_ = psO[(nq + qb) % 2]
                            extra = [pts[(gcnt + k) % NSL] for k in pair]
                            S.MM(ops_[0:65, q0:512], vext[:, kt, h * 66:h * 66 + 65], pt[:, q0:512],
                                 kt == 0, kt == nkt - 1, [vext] + extra, [ops_], signal=(kt == nkt - 1))
                            while pending and pending[0][0] <= i:
                                pending.pop(0)[1]()
                            if kt == nkt - 1:
                                ob = osb[(nq + qb) % 2]
                                rd = rdn[(nq + qb) % 2]
                                yo = yos[(nq + qb) % 2]
                                S.CP("dve", ob[:], ops_[0:65, :], [ops_], [ob])
                                S.op("dve", lambda e, rd=rd, ob=ob: e.reciprocal(out=rd[64:65, :], in_=ob[64:65, :]), [ob], [rd])
                                pending.append((i + 3, lambda qb=qb, ob=ob, rd=rd, yo=yo: epi2(qb, ob, rd, yo)))
                    for _, fn in pending:
                        fn()
                    gcnt += nst
                    nq += NQB

        def phase_rwkv(l):
            NTB = T // 512
            with S.phase(xlat=1.0):
                stg = S.sbuf([128, 512], F32, "rstg")
                waw = S.sbuf([128, 512], BF16, "waw")
                gup = S.sbuf([128, 512], BF16, "gup")
                S.dma("sp", stg[0:64, :], r_wup[l], writes=[stg])
                S.dma("sp", stg[64:128, :], r_aup[l], writes=[stg])
                S.CP("dve", waw[:], stg[:], [stg], [waw])
                S.dma("sp", stg[:], r_gup[l], writes=[stg])
                S.CP("dve", gup[:], stg[:], [stg], [gup])
                mkf = S.sbuf([64, 3, 8, 128], F32, "mkf")
                S.dma("sp", mkf[:], mk[:, 0:3], writes=[mkf])
                mG1 = mkf[:, 0]
                mG2 = mkf[:, 1]
                mL = mkf[:, 2, :, 0:64]
                id8 = mkf[:, 2, :, 64:128]
                St = S.sbuf([64, 8, 64], F32, "St")
                Sb = S.sbuf([64, 8, 64], BF16, "Sb")
                Stmp = S.sbuf([64, 8, 64], F32, "Stmp")
                S.MS("pool", St[:], 0.0, [St])
                S.MS("pool", Sb[:], 0.0, [Sb])
                EQ8 = S.sbuf([64, 8, 8, 2, 64], BF16, "EQ8")
                FB8 = S.sbuf([64, 8, 2, 512], BF16, "FB8")
                pC8 = S.sbuf([64, 8, 8], F32, "pC8")
                EQ = S.sbuf([128, 4, 8, 2, 64], BF16, "EQ")
                FB = S.sbuf([128, 4, 2, 512], BF16, "FB")
                Vb = S.sbuf([128, 4, 512], BF16, "Vb")
                pC = S.sbuf([128, 4, 8], F32, "pC")
                Ftm = S.sbuf([64, 8, 512], BF16, "Ftm")
                nBtm = S.sbuf([64, 8, 512], BF16, "nBtm")
                Vtm = S.sbuf([64, 8, 512], BF16, "Vtm")
                zls = [S.sbuf([128, 514], BF16, f"zl{i}") for i in range(3)]
                f32t = lambda n: S.sbuf([128, 512], F32, n)
                dtmp, twa, sgf, rr, kk_, vv, sig, csf = [f32t(n) for n in ("dtmp", "twa", "sgf", "rr", "kk", "vv", "sig", "csf")]
                csm, pp, pinv, pprev, aa, kap, ksq, rinv, kh, ktl, bet = [f32t(n) for n in (
                    "csm", "pp", "pinv", "pprev", "aa", "kap", "ksq", "rinv", "kh", "ktl", "bet")]
                t1 = dtmp
                rk = ksq
                twab = S.sbuf([128, 512], BF16, "twab")
                sgb = S.sbuf([128, 512], BF16, "sgb")
                offs = S.sbuf([128, 8], F32, "offs")
                gob = [S.sbuf([128, 512], BF16, f"gob{i}") for i in range(2)]
                bob = [S.sbuf([128, 512], BF16, f"bob{i}") for i in range(2)]
                G1bs = [S.sbuf([64, 8, 128], BF16, f"G1b{i}") for i in range(2)]
                G2bs = [S.sbuf([64, 8, 128], BF16, f"G2b{i}") for i in range(2)]
                Pfin = [S.sbuf([64, 8, 64], BF16, f"Pfin{i}") for i in range(2)]
                Ab = [S.sbuf([64, 8, 64], BF16, f"Ab{i}") for i in range(2)]
                Bb = [S.sbuf([64, 8, 64], BF16, f"Bb{i}") for i in range(2)]
                Pb = [S.sbuf([64, 8, 64], BF16, f"Pb{i}") for i in range(2)]
                IBn = S.sbuf([64, 8, 64], BF16, "IBn")
                Xb = S.sbuf([64, 8, 64], BF16, "Xb")
                Ub = S.sbuf([64, 8, 64], BF16, "Ub")
                Ycs = [S.sbuf([64, 512], F32, f"Yc{i}") for i in range(2)]
                ynbs = [S.sbuf([64, 8, 8, 64], BF16, f"ynb{i}") for i in range(2)]
                gsqb = S.sbuf([64, 512], BF16, "gsqb")
                Ycb = S.sbuf([64, 512], BF16, "Ycb")
                gd_ = S.sbuf([64, 512], F32, "gd")
                gm2 = S.sbuf([64, 512], F32, "gm2")
                gmean = S.sbuf([64, 512], F32, "gmean")
                gvar = S.sbuf([64, 512], F32, "gvar")
                QA, QB, QC, RA, RB, RC, W = [S.psum([128, 512], F32, f"rp{i}") for i in range(7)]
                PT = S.psum([64, 8, 128], BF16, "rpT")

                def v3(ap, a):
                    return ap.rearrange("p (a b) -> p a b", a=a)

                nzc = [0]

                def mix(dst, row0, mucol, tb):
                    zl = zls[nzc[0] % 3]
                    nzc[0] += 1
                    c0 = tb * 512
                    if tb == 0:
                        S.MS("pool", zl[:, 0:2], 0.0, [zl])
                        S.dma("sp", zl[:, 2:514], zr[row0:row0 + 128, 0:512], writes=[zl])
                    else:
                        S.dma("sp", zl[:, 1:514], zr[row0:row0 + 128, c0 - 1:c0 + 512], writes=[zl])
                    S.TT("pool", dtmp[:], zl[:, 1:513], zl[:, 2:514], ALU.subtract, [zl], [dtmp])
                    S.STT("dve", dst[:], dtmp[:], mucol, zl[:, 2:514], ALU.mult, ALU.add, [dtmp, zl], [dst])

                def stage1(tb):
                    tcs = slice(tb * 512, (tb + 1) * 512)
                    mix(twa, 1536, pcol(l, "mu", 12), tb)
                    S.A(twab[0:64, :], twa[0:64, :], AF.Tanh, [twa], [twab])
                    S.CP("act", twab[64:128, :], twa[64:128, :], [twa], [twab])
                    yield
                    mix(sgf, 1664, pcol(l, "mu", 13), tb)
                    S.A(sgb[:], sgf[:], AF.Sigmoid, [sgf], [sgb])
                    yield
                    for hp in range(4):
                        hs = slice(hp * 128, (hp + 1) * 128)
                        mix(rr, hp * 128, pcol(l, "mu", hp), tb)
                        yield
                        mix(kk_, 512 + hp * 128, pcol(l, "mu", 4 + hp), tb)
                        yield
                        mix(vv, 1024 + hp * 128, pcol(l, "mu", 8 + hp), tb)
                        yield
                        S.MM(W[:], waw[0:64, hs], twab[0:64, :], True, True, [waw, twab], [W])
                        S.A(sig[:], W[:], AF.Sigmoid, [W], [sig], bias=pcol(l, "w0", hp))
                        yield
                        S.MM(W[:], waw[64:128, hs], twab[64:128, :], True, True, [waw, twab], [W])
                        S.A(aa[:], W[:], AF.Sigmoid, [W], [aa], bias=pcol(l, "a0", hp))
                        yield
                        S.MM(W[:], gup[:, hs], sgb[:], True, True, [gup, sgb], [W])
                        go = gob[(tb * 4 + hp) % 2]
                        S.CP("act", go[:], W[:], [W], [go])
                        S.dma("pool", gsc[hs, tcs], go[:], reads=[go])
                        yield
                        S.op("dve", lambda e: e.tensor_tensor_scan(out=csf[:], data0=sig[:], data1=sig[:], initial=0.0,
                                                                    op0=ALU.add, op1=ALU.bypass), [sig], [csf])
                        S.MS("pool", offs[:, 0:1], 0.0, [offs])
                        yield
                        S.CP("pool", offs[:, 1:8], v3(csf[:], 8)[:, 0:7, 63], [csf], [offs])
                        yield
                        S.TT("dve", v3(csf[:], 8), v3(csf[:], 8), bc(offs[:, 0:8], 64), ALU.subtract, [csf, offs], [csf])
                        yield
                        S.TT("pool", csm[:], csf[:], sig[:], ALU.subtract, [csf, sig], [csm])
                        S.A(pp[:], csf[:], AF.Exp, [csf], [pp], scale=-DS)
                        yield
                        S.A(pinv[:], csf[:], AF.Exp, [csf], [pinv], scale=DS)
                        S.A(pprev[:], csm[:], AF.Exp, [csm], [pprev], scale=-DS)
                        S.CP("pool", pC[:, hp, :], v3(pp[:], 8)[:, :, 63], [pp], [pC])
                        yield
                        S.A(kap[:], kk_[:], AF.Copy, [kk_], [kap], scale=pcol(l, "kk", hp))
                        yield
                        S.A(ksq[:], kap[:], AF.Square, [kap], [ksq])
                        yield
                        S.MM(W[:], bdones, ksq[:], True, True, [cf, ksq], [W])
                        S.A(rinv[:], W[:], AF.Ln, [W], [rinv], bias=1e-12)
                        S.A(rinv[:], rinv[:], AF.Exp, [rinv], [rinv], scale=-0.5)
                        yield
                        S.TS("pool", t1[:], aa[:], pcol(l, "ka", hp), omka[l][:, hp:hp + 1], ALU.mult, ALU.add, [aa, omka[l]], [t1])
                        yield
                        S.TT("dve", kh[:], kap[:], rinv[:], ALU.mult, [kap, rinv], [kh])
                        S.TT("pool", ktl[:], kk_[:], t1[:], ALU.mult, [kk_, t1], [ktl])
                        yield
                        S.TT("pool", bet[:], aa[:], kh[:], ALU.mult, [aa, kh], [bet])
                        S.TT("dve", EQ[:, hp, :, 1, :], v3(rr[:], 8), v3(pp[:], 8), ALU.mult, [rr, pp], [EQ])
                        yield
                        S.TT("pool", EQ[:, hp, :, 0, :], v3(kh[:], 8), v3(pprev[:], 8), ALU.mult, [kh, pprev], [EQ])
                        S.TT("pool", FB[:, hp, 0, :], ktl[:], pinv[:], ALU.mult, [ktl, pinv], [FB])
                        yield
                        S.TT("pool", FB[:, hp, 1, :], bet[:], pinv[:], ALU.mult, [bet, pinv], [FB])
                        S.CP("act", Vb[:, hp, :], vv[:], [vv], [Vb])
                        S.STT("dve", rk[:], rr[:], pcol(l, "rk", hp), ktl[:], ALU.mult, ALU.mult, [rr, ktl], [rk])
                        yield
                        S.MM(W[:], bdones, rk[:], True, True, [cf, rk], [W])
                        bo = bob[(tb * 4 + hp) % 2]
                        S.TT("dve", bo[:], W[:], vv[:], ALU.mult, [W, vv], [bo])
                        S.dma("pool", bsc[hs, tcs], bo[:], reads=[bo])
                        yield

                def stage2():
                    for hp in range(4):
                        hs = slice(hp * 128, (hp + 1) * 128)
                        for (src, dstT, neg) in ((FB[:, hp, 0, :], Ftm, False), (FB[:, hp, 1, :], nBtm, True),
                                                 (Vb[:, hp, :], Vtm, False)):
                            for c in range(8):
                                S.TR(PT[:, c, :], src[:, c * 64:(c + 1) * 64], identb[:], [FB, Vb, identb], [PT],
                                     signal=(c == 7))
                            if neg:
                                S.A(dstT[:, :, hs], PT[:], AF.Copy, [PT], [dstT], scale=-1.0)
                            else:
                                S.CP("dve", dstT[:, :, hs], PT[:], [PT], [dstT])
                    for par in range(2):
                        ps_ = slice(par * 64, par * 64 + 64)
                        S.dma("sp", EQ8[:].rearrange("p (a two) c e t -> p a two (c e t)", two=2)[:, :, par, :],
                              EQ[ps_].rearrange("p a c e t -> p a (c e t)"), reads=[EQ], writes=[EQ8])
                        S.dma("sp", FB8[:].rearrange("p (a two) e t -> p a two (e t)", two=2)[:, :, par, :],
                              FB[ps_].rearrange("p a e t -> p a (e t)"), reads=[FB], writes=[FB8])
                        S.dma("sp", pC8[:].rearrange("p (a two) c -> p a two c", two=2)[:, :, par, :],
                              pC[ps_], reads=[pC], writes=[pC8])

                def streamA(c, par):
                    ccs = slice(c * 64, (c + 1) * 64)
                    G1b, G2b = G1bs[par], G2bs[par]
                    ga = [v3(QA[0:64, :], 4), v3(QB[0:64, :], 4)]
                    bank = [QA, QB]
                    for h in range(8):
                        eq = EQ8[:, h, c, :, :].rearrange("p a b -> p (a b)")
                        S.MM(ga[h // 4][:, h % 4, :], FB8[:, h, 0, ccs], eq, True, True, [FB8, EQ8], [bank[h // 4]],
                             signal=(h % 4 == 3))
                    yield
                    S.TT("dve", G1b[:, 0:4, :], ga[0], mG1[:, 0:4, :], ALU.mult, [QA, mkf], [G1b])
                    S.TT("dve", G1b[:, 4:8, :], ga[1], mG1[:, 4:8, :], ALU.mult, [QB, mkf], [G1b])
                    g3 = v3(QC[0:64, :], 8)
                    for h in range(8):
                        S.MM(g3[:, h, :], EQ8[:, h, c, 0, :], FB8[:, h, 1, ccs], True, True, [FB8, EQ8], [QC], signal=(h == 7))
                    yield
                    for h in range(8):
                        eq = EQ8[:, h, c, :, :].rearrange("p a b -> p (a b)")
                        S.MM(ga[h // 4][:, h % 4, :], FB8[:, h, 1, ccs], eq, True, True, [FB8, EQ8], [bank[h // 4]],
                             signal=(h % 4 == 3))
                    S.TT("dve", Bb[0][:], g3, mL, ALU.mult, [QC, mkf], [Bb[0]])
                    yield
                    S.TT("dve", G2b[:, 0:4, :], ga[0], mG2[:, 0:4, :], ALU.mult, [QA, mkf], [G2b])
                    S.TT("dve", G2b[:, 4:8, :], ga[1], mG2[:, 4:8, :], ALU.mult, [QB, mkf], [G2b])
                    yield
                    S.TT("dve", Pb[0][:], id8, G2b[:, :, 0:64], ALU.subtract, [mkf, G2b], [Pb[0]])
                    yield
                    pa, pb_, pq = v3(QA[0:64, :], 8), v3(QB[0:64, :], 8), v3(QC[0:64, :], 8)
                    for j in range(5):
                        Ac, Bc, Pc = Ab[j % 2], Bb[j % 2], Pb[j % 2]
                        An, Bn = Ab[(j + 1) % 2], Bb[(j + 1) % 2]
                        Pn = Pb[(j + 1) % 2] if j < 4 else Pfin[par]
                        if j == 0:
                            Acb, Acv = G2b, (lambda h: G2b[:, h, 0:64])
                        else:
                            Acb, Acv = Ac, (lambda h, Ac=Ac: Ac[:, h, :])
                        for h in range(8):
                            S.MM(pb_[:, h, :], Acv(h), Bc[:, h, :], True, True, [Acb, Bc], [QB], signal=(h == 7))
                        if j < 4:
                            for h in range(8):
                                S.MM(pa[:, h, :], Bc[:, h, :], Acv(h), True, True, [Acb, Bc], [QA], signal=(h == 7))
                        yield
                        S.CP("act", Bn[:], pb_, [QB], [Bn])
                        if j < 4:
                            S.CP("act", An[:], pa, [QA], [An])
                        yield
                        S.TT("dve", IBn[:], Bn[:], id8, ALU.add, [Bn, mkf], [IBn])
                        yield
                        for h in range(8):
                            S.MM(pq[:, h, :], IBn[:, h, :], Pc[:, h, :], True, True, [IBn, Pc], [QC], signal=(h == 7))
                        yield
                        S.CP("act", Pn[:], pq, [QC], [Pn])
                        yield

                def streamB(c, par, yn):
                    G1b, G2b, Pf = G1bs[par], G2bs[par], Pfin[par]
                    px = v3(RA[0:64, :], 8)
                    for h in range(8):
                        hc = slice(h * 64, (h + 1) * 64)
                        S.MM(px[:, h, :], EQ8[:, h, c, 0, :], Sb[:, h, :], True, False, [EQ8, Sb], [RA], signal=False)
                        S.MM(px[:, h, :], G1b[:, h, 0:64], Vtm[:, c, hc], False, True, [G1b, Vtm], [RA], signal=(h == 7))
                    yield
                    S.CP("act", Xb[:], px, [RA], [Xb])
                    yield
                    pu = v3(RB[0:64, :], 8)
                    for h in range(8):
                        S.MM(pu[:, h, :], Pf[:, h, :], Xb[:, h, :], True, True, [Pf, Xb], [RB], signal=(h == 7))
                    yield
                    S.CP("act", Ub[:], pu, [RB], [Ub])
                    yield
                    pS = v3(RA[0:64, :], 8)
                    for h in range(8):
                        hc = slice(h * 64, (h + 1) * 64)
                        S.MM(pS[:, h, :], Ftm[:, c, hc], Vtm[:, c, hc], True, False, [Ftm, Vtm], [RA], signal=False)
                        S.MM(pS[:, h, :], nBtm[:, c, hc], Ub[:, h, :], False, True, [nBtm, Ub], [RA], signal=(h == 7))
                    py = v3(RC[0:64, :], 8)
                    for h in range(8):
                        hc = slice(h * 64, (h + 1) * 64)
                        S.MM(py[:, h, :], Sb[:, h, :], EQ8[:, h, c, 1, :], True, False, [EQ8, Sb], [RC], signal=False)
                        S.MM(py[:, h, :], Vtm[:, c, hc], G1b[:, h, 64:128], False, False, [G1b, Vtm], [RC], signal=False)
                        S.MM(py[:, h, :], Ub[:, h, :], G2b[:, h, 64:128], False, True, [G2b, Ub], [RC], signal=(h == 7))
                    yield
                    S.TT("dve", Stmp[:], St[:], pS, ALU.add, [St, RA], [Stmp])
                    S.CP("act", yn[:, :, c, :], py, [RC], [yn])
                    yield
                    S.TT("dve", Sb[:], Stmp[:], bc(pC8[:, :, c], 64), ALU.mult, [Stmp, pC8], [Sb])
                    yield
                    S.TT("pool", St[:], Stmp[:], bc(pC8[:, :, c], 64), ALU.mult, [Stmp, pC8], [St])
                    yield

                def drive(gens):
                    gens = [g for g in gens if g is not None]
                    while gens:
                        for g in list(gens):
                            try:
                                next(g)
                            except StopIteration:
                                gens.remove(g)

                drive([stage1(0)])
                stage2()
                gch = 0
                for tb in range(NTB):
                    tcs = slice(tb * 512, (tb + 1) * 512)
                    yn = ynbs[tb % 2]
                    s1 = stage1(tb + 1) if tb + 1 < NTB else None
                    RW = int(os.environ.get("K_RW", "0"))
                    if RW != 1:
                        drive([streamA(0, gch % 2)])
                    for c in range(8 if RW != 1 else 0):
                        ga_ = streamA(c + 1, (gch + 1) % 2) if c < 7 else None
                        gb_ = streamB(c, gch % 2, yn) if RW != 2 else None
                        gens = [g for g in (gb_, ga_) if g is not None]
                        while gens:
                            for g in list(gens):
                                try:
                                    next(g)
                                except StopIteration:
                                    gens.remove(g)
                            if s1 is not None:
                                try:
                                    next(s1)
                                except StopIteration:
                                    s1 = None
                        gch += 1
                    if s1 is not None:
                        drive([s1])
                    S.dma("pool", ybn[:, tcs].rearrange("(h p) t -> p h t", p=64),
                          yn[:].rearrange("p h c t -> p h (c t)"), reads=[yn])
                    if tb + 1 < NTB:
                        stage2()

        def phase_merge(l, xsrc, xdst):
            with S.phase():
                wb = S.sbuf([128, 3, 4, D], BF16, "wb")
                wo = S.sbuf([128, 8, D], BF16, "wo")
                stages = [S.sbuf([128, 2048], F32, f"mstg{i}") for i in range(2)]
                for br in range(3):
                    pieces = [(c0, 256, 4) for c0 in range(0, D, 256)]
                    load_w_bf16(wb, lambda p, br=br: wb[:, br, :, p[0]:p[0] + p[1]],
                                lambda p, br=br: w_br[l, br, :, p[0]:p[0] + p[1]].rearrange("(k q) n -> q k n", q=128),
                                pieces, stages)
                pieces = [(c0, 256, 8) for c0 in range(0, D, 256)]
                load_w_bf16(wo, lambda p: wo[:, :, p[0]:p[0] + p[1]],
                            lambda p: w_o[l, :, p[0]:p[0] + p[1]].rearrange("(k q) n -> q k n", q=128), pieces, stages)
                yas = [S.sbuf([128, 4, 512], BF16) for _ in range(2)]
                ycs = [S.sbuf([128, 4, 512], BF16) for _ in range(2)]
                ybs = [S.sbuf([128, 4, 512], BF16) for _ in range(2)]
                bos = [S.sbuf([128, 4, 512], BF16) for _ in range(2)]
                gos = [S.sbuf([128, 4, 512], BF16) for _ in range(2)]
                ybf = [S.sbuf([128, 4, 512], BF16) for _ in range(2)]
                gsets = [(S.sbuf([128, 512], BF16), S.sbuf([128, 512], F32), S.sbuf([128, 512], F32),
                          S.sbuf([128, 512], F32), S.sbuf([128, 512], F32)) for _ in range(2)]
                Gs = [S.sbuf([128, 24, 512], BF16) for _ in range(2)]
                mT = [S.sbuf([128, 8, 512], BF16) for _ in range(2)]
                m1 = [S.sbuf([128, 512], F32) for _ in range(2)]
                _m2 = S.sbuf([128, 512], F32)
                m2 = [_m2, _m2]
                xbufs = [S.sbuf([128, D], F32) for _ in range(2)]
                pbr = [S.psum([128, 512], F32) for _ in range(3)]
                pso = [S.psum([128, 512], F32) for _ in range(2)]
                pgn = [S.psum([128, 512], F32) for _ in range(3)]
                no = 0
                for i in range(NT):
                    tcs = slice(i * 512, (i + 1) * 512)
                    k = i % 2
                    ld = lambda dst, src: S.dma("sp", dst[:], src[:, tcs].rearrange("(c p) t -> p c t", p=128), writes=[dst])
                    ld(yas[k], yaT); ld(ycs[k], ycT); ld(ybs[k], ybn); ld(bos[k], bsc); ld(gos[k], gsc)
                    S.dma("sp", Gs[k][:], gT[:, tcs].rearrange("(c p) t -> p c t", p=128), writes=[Gs[k]])
                    for hp in range(4):
                        yv = ybs[k][:, hp, :]
                        gsqm, gmn, gdm, gm2m, gvm = gsets[hp % 2]
                        pg0 = pgn[(2 * hp) % 3]
                        pg1 = pgn[(2 * hp + 1) % 3]
                        S.A(gsqm[:], yv, AF.Square, [ybs[k]], [gsqm])
                        S.MM(pg0[:], bdmb[:], yv, True, True, [bdmb, ybs[k]], [pg0])
                        S.MM(pg1[:], bdmb[:], gsqm[:], True, True, [bdmb, gsqm], [pg1])
                        S.CP("dve", gmn[:], pg0[:], [pg0], [gmn])
                        S.TT("pool", gdm[:], yv, gmn[:], ALU.subtract, [ybs[k], gmn], [gdm])
                        S.A(gm2m[:], gmn[:], AF.Square, [gmn], [gm2m])
                        S.TT("dve", gvm[:], pg1[:], gm2m[:], ALU.subtract, [pg1, gm2m], [gvm])
                        S.A(gvm[:], gvm[:], AF.Sqrt, [gvm], [gvm], bias=64e-5)
                        S.op("dve", lambda e, gvm=gvm: e.reciprocal(out=gvm[:], in_=gvm[:]), [gvm], [gvm], dur=0.65)
                        S.TT("pool", gdm[:], gdm[:], gvm[:], ALU.mult, [gdm, gvm], [gdm])
                        S.TS("dve", gdm[:], gdm[:], pcol(l, "gng", hp), pcol(l, "gnb", hp), ALU.mult, ALU.add, [gdm], [gdm])
                        S.TT("pool", gdm[:], gdm[:], bos[k][:, hp, :], ALU.add, [gdm, bos[k]], [gdm])
                        S.TT("dve", ybf[k][:, hp, :], gdm[:], gos[k][:, hp, :], ALU.mult, [gdm, gos[k]], [ybf[k]])
                    ysrc = (yas[k], ybf[k], ycs[k])
                    for oc in range(8):
                        pp3 = [pbr[br] for br in range(3)]
                        for br in range(3):
                            for kc in range(4):
                                S.MM(pp3[br][:], wb[:, br, kc, oc * 128:(oc + 1) * 128], ysrc[br][:, kc, :], kc == 0, kc == 3,
                                     [wb, ysrc[br]], [pp3[br]], signal=(kc == 3))
                        a1, a2 = m1[oc % 2], m2[oc % 2]
                        S.TT("dve", a1[:], pp3[0][:], Gs[k][:, oc, :], ALU.mult, [pp3[0], Gs[k]], [a1])
                        S.TT("dve", a2[:], pp3[1][:], Gs[k][:, 8 + oc, :], ALU.mult, [pp3[1], Gs[k]], [a2])
                        S.TT("pool", a1[:], a1[:], a2[:], ALU.add, [a1, a2], [a1])
                        S.TT("dve", a2[:], pp3[2][:], Gs[k][:, 16 + oc, :], ALU.mult, [pp3[2], Gs[k]], [a2])
                        S.TT("pool", mT[k][:, oc, :], a1[:], a2[:], ALU.add, [a1, a2], [mT[k]])
                    for s in range(4):
                        xt = xbufs[no % 2]
                        r0 = (i * 4 + s) * 128
                        S.dma("sp", xt[:], xsrc[r0:r0 + 128, :], writes=[xt])
                        for half in range(2):
                            ps = pso[half]
                            for kc in range(8):
                                S.MM(ps[:], mT[k][:, kc, s * 128:(s + 1) * 128], wo[:, kc, half * 512:(half + 1) * 512],
                                     kc == 0, kc == 7, [mT[k], wo], [ps], signal=(kc == 7))
                            S.TT("dve", xt[:, half * 512:(half + 1) * 512], xt[:, half * 512:(half + 1) * 512], ps[:], ALU.add,
                                 [xt, ps], [xt])
                        S.dma("pool", xdst[r0:r0 + 128, :], xt[:], reads=[xt])
                        no += 1

        def phase_ffn(l, xsrc, xdst, final):
            with S.phase():
                wup = S.sbuf([128, 8, 2 * FF], BF16, "wup")
                wdn = S.sbuf([128, 22, D], BF16, "wdn")
                xbufs = [S.sbuf([128, D], F32, f"fx{i}") for i in range(3)]
                pieces = [(c0, 128, 8) for c0 in range(0, 2 * FF, 128)]
                load_w_bf16(wup, lambda p: wup[:, :, p[0]:p[0] + p[1]],
                            lambda p: w_up[l, :, p[0]:p[0] + p[1]].rearrange("(k q) n -> q k n", q=128), pieces, xbufs)
                ip = 0
                for kc0 in range(0, 22, 2):
                    for half in range(2):
                        st = xbufs[ip % 3]
                        sv = st[:, 0:1024].rearrange("p (k n) -> p k n", k=2)
                        S.dma("sp", sv, w_dn[l, kc0 * 128:(kc0 + 2) * 128, half * 512:(half + 1) * 512].rearrange(
                            "(k q) n -> q k n", q=128), writes=[st])
                        S.CP(("act", "dve")[ip % 2], wdn[:, kc0:kc0 + 2, half * 512:(half + 1) * 512], sv, [st], [wdn])
                        ip += 1
                _xs = S.sbuf([128, D], BF16)
                nbs = [(S.sbuf([128, 1], F32), S.sbuf([128, 1], F32), _xs, _xs, S.psum([128, 8, 128], BF16))]
                fxnTs = [S.sbuf([128, 8, 512], BF16, f"fxnT{i}") for i in range(2)]
                actT = S.sbuf([128, 22, 512], BF16, "actT")
                hg = [S.sbuf([128, 514], F32) for _ in range(2)]
                _hu = S.sbuf([128, 514], F32)
                hu = [_hu, _hu]
                cg = [S.sbuf([128, 512], F32) for _ in range(2)]
                _cu = S.sbuf([128, 512], F32)
                cu = [_cu, _cu]
                carry = S.sbuf([128, 44, 2], F32, "carry")
                S.MS("pool", carry[:], 0.0, [carry])
                gf = None
                if final:
                    gf = S.sbuf([128, D], F32, "gfin")
                    S.dma("sp", gf[:], gfin.partition_broadcast(128), writes=[gf])
                    fss = S.sbuf([128, 1], F32)
                    frs = S.sbuf([128, 1], F32)
                    fj = _xs
                pg = [S.psum([128, 512], F32) for _ in range(2)]
                pu = [S.psum([128, 512], F32) for _ in range(2)]
                pdn = [S.psum([128, 512], F32) for _ in range(2)]
                nx = 0
                for i in range(NT):
                    xnT = fxnTs[i % 2]
                    for s in range(4):
                        xt = xbufs[nx % 3]; nx += 1
                        r0 = (i * 4 + s) * 128
                        S.dma("sp", xt[:], xsrc[r0:r0 + 128, :], writes=[xt])
                        rmsnorm_T(xt, pcol(l, "n2g", 0, 8), xnT, s, nbs[0])
                    for j in range(22):
                        k = j % 2
                        for kc in range(8):
                            S.MM(pg[k][:], wup[:, kc, j * 128:(j + 1) * 128], xnT[:, kc, :], kc == 0, kc == 7, [wup, xnT], [pg[k]],
                                 signal=(kc == 7))
                        for kc in range(8):
                            S.MM(pu[k][:], wup[:, kc, FF + j * 128:FF + (j + 1) * 128], xnT[:, kc, :], kc == 0, kc == 7,
                                 [wup, xnT], [pu[k]], signal=(kc == 7))
                        for (hb, ps, cc, slot) in ((hg[k], pg[k], cg[k], j), (hu[k], pu[k], cu[k], 22 + j)):
                            S.CP("pool", hb[:, 0:2], carry[:, slot, :], [carry], [hb])
                            S.CP("act", hb[:, 2:514], ps[:], [ps], [hb])
                            S.CP("pool", carry[:, slot, :], hb[:, 512:514], [hb], [carry])
                            S.A(cc[:], hb[:, 2:514], AF.Copy, [hb], [cc], scale=pcol(l, "fcw", slot * 3 + 2))
                            S.STT("dve", cc[:], hb[:, 1:513], pcol(l, "fcw", slot * 3 + 1), cc[:], ALU.mult, ALU.add, [hb, cc], [cc])
                            S.STT("dve", cc[:], hb[:, 0:512], pcol(l, "fcw", slot * 3 + 0), cc[:], ALU.mult, ALU.add, [hb, cc], [cc])
                        S.A(cg[k][:], cg[k][:], AF.Silu, [cg[k]], [cg[k]])
                        S.TT("dve", actT[:, j, :], cg[k][:], cu[k][:], ALU.mult, [cg[k], cu[k]], [actT])
                    for s in range(4):
                        xt = xbufs[nx % 3]; nx += 1
                        r0 = (i * 4 + s) * 128
                        S.dma("sp", xt[:], xsrc[r0:r0 + 128, :], writes=[xt])
                        for half in range(2):
                            ps = pdn[half]
                            for j in range(22):
                                S.MM(ps[:], actT[:, j, s * 128:(s + 1) * 128], wdn[:, j, half * 512:(half + 1) * 512],
                                     j == 0, j == 21, [actT, wdn], [ps], signal=(j == 21))
                            S.TT("dve", xt[:, half * 512:(half + 1) * 512], xt[:, half * 512:(half + 1) * 512], ps[:], ALU.add,
                                 [xt, ps], [xt])
                        if final:
                            S.MS("pool", fss[:], 0.0, [fss])
                            S.A(fj[:], xt[:], AF.Square, [xt], [fj, fss], accum_out=fss[:])
                            S.A(frs[:], fss[:], AF.Sqrt, [fss], [frs], scale=1.0 / D, bias=1e-6)
                            S.op("dve", lambda e: e.reciprocal(out=frs[:], in_=frs[:]), [frs], [frs])
                            S.STT("dve", xt[:], xt[:], frs[:, 0:1], gf[:], ALU.mult, ALU.mult, [xt, frs, gf], [xt])
                        S.dma("pool", xdst[r0:r0 + 128, :], xt[:], reads=[xt])

        src = x_in
        phl = []
        for l in range(depth):
            last = (l == depth - 1)
            phl += [lambda l=l, src=src: phase_inproj(l, src), lambda l=l: phase_fcum(l),
                    lambda l=l: phase_attn(l), lambda l=l: phase_rwkv(l), lambda l=l, src=src: phase_merge(l, src, xa),
                    lambda l=l, last=last: phase_ffn(l, xa, y_out if last else xb, last)]
            src = xb
        for f in phl[:nph]:
            f()
        print("instructions:", S.ninstr)
    return nc


def host_consts():
    cst = np.zeros((128, 768), np.float32)
    cst[:, 0:128] = np.eye(128)
    k = np.arange(128)
    cst[:, 128:256] = (k[:, None] <= k[None, :])
    bd = np.zeros((128, 128), np.float32)
    bd[:64, :64] = 1
    bd[64:, 64:] = 1
    cst[:, 256:384] = bd
    cst[:, 384:512] = 1.0 / 64
    cst[:, 512:640] = 1.0
    s = np.arange(64)
    su = (s[:, None] < s[None, :]).astype(np.float32)
    ui = (s[:, None] <= s[None, :]).astype(np.float32)
    sl = (s[:, None] > s[None, :]).astype(np.float32)
    mk = np.zeros((64, 4, 8, 128), np.float32)
    mk[:, 0, :, 0:64] = su[:, None, :]
    mk[:, 0, :, 64:128] = ui[:, None, :]
    mk[:, 1, :, 0:64] = su[:, None, :]
    mk[:, 1, :, 64:128] = -ui[:, None, :]
    mk[:, 2, :, 0:64] = sl[:, None, :]
    mk[:, 2, :, 64:128] = np.eye(64, dtype=np.float32)[:, None, :]
    return cst, mk


def pack_params(inp, depth):
    pc = np.zeros((depth, 128, NPC), np.float32)

    def col(v):
        return np.ascontiguousarray(v.reshape(-1, 128).T)

    for l in range(depth):
        def put(name, arr):
            o = PCO[name]
            pc[l, :arr.shape[0], o:o + arr.shape[1]] = arr
        put("n1g", col(inp["norm1_g"][l]))
        put("n2g", col(inp["norm2_g"][l]))
        put("gateb", col(inp["gate_b"][l]))
        cm = inp["conv_mix_w"][l]
        put("cmw", np.ascontiguousarray(cm.reshape(3, 4, 128).transpose(2, 1, 0).reshape(128, 12)))
        fc = inp["ffn_conv_w"][l]
        put("fcw", np.ascontiguousarray(fc.reshape(3, 44, 128).transpose(2, 1, 0).reshape(128, 132)))
        put("mu", col(inp["rwkv_mu"][l]))
        put("w0", col(inp["rwkv_w0"][l]))
        put("a0", col(inp["rwkv_a0"][l]))
        put("kk", col(inp["rwkv_k_k"][l]))
        put("ka", col(inp["rwkv_k_a"][l]))
        put("rk", col(inp["rwkv_r_k"][l].reshape(-1)))
        put("fb", inp["attn_forget_b"][l].reshape(8, 1))
        put("gng8", np.ascontiguousarray(inp["rwkv_gn_g"][l].reshape(8, 64).T))
        put("gnb8", np.ascontiguousarray(inp["rwkv_gn_b"][l].reshape(8, 64).T))
        put("gng", col(inp["rwkv_gn_g"][l]))
        put("gnb", col(inp["rwkv_gn_b"][l]))
    return pc


_NC_CACHE = {}


def run(inputs, T, nb, depth=DEPTH, dbg=(), nph=99):
    inputs = {k: np.asarray(v) for k, v in inputs.items()}
    key = (T, depth, tuple(dbg), nph)
    if key not in _NC_CACHE:
        _NC_CACHE[key] = build(T, depth, dbg, nph)
    nc = _NC_CACHE[key]
    cst, mk = host_consts()
    pc = pack_params(inputs, depth)
    shared = {
        "w_in": inputs["w_in"][:depth], "w_branch": inputs["w_branch"][:depth], "w_o": inputs["w_o"][:depth],
        "ffn_w_up": inputs["ffn_w_up"][:depth], "ffn_w_down": inputs["ffn_w_down"][:depth],
        "rwkv_w_up": inputs["rwkv_w_up"][:depth], "rwkv_a_up": inputs["rwkv_a_up"][:depth],
        "rwkv_g_up": inputs["rwkv_g_up"][:depth], "pc": pc, "final_norm_g": inputs["final_norm_g"],
        "cst": cst, "mk": mk,
    }
    shared = {k: np.ascontiguousarray(v, dtype=np.float32) for k, v in shared.items()}
    in_maps = []
    for b in range(nb):
        m = dict(shared)
        m["x"] = np.ascontiguousarray(inputs["x"][b], dtype=np.float32)
        in_maps.append(m)
    res = run_bass_kernel_spmd(nc, in_maps, core_ids=list(range(nb)))
    return res.results


def kernel(**inputs):
    res = run(inputs, SEQ, NB)
    return np.stack([np.asarray(r["y"], dtype=np.float32) for r in res], axis=0)
```

```python
import contextlib
import math
import os
CUT = int(os.environ.get("K_CUT", "0"))
SUB = int(os.environ.get("K_SUB", "0"))
import numpy as np
import concourse.bass as bass
import concourse.mybir as mybir
from concourse.bass_utils import run_bass_kernel_spmd

F32 = mybir.dt.float32
BF16 = mybir.dt.bfloat16
AF = mybir.ActivationFunctionType
ALU = mybir.AluOpType

ENGS = ("pe", "act", "dve", "pool", "sp")

D = 1024
NIN = 7944
FF = 2816
SEQ = 8192
NB = 4
DEPTH = 2
DS = math.exp(-0.5)

PCO = {}
_o = 0
for _n, _w in (("n1g", 8), ("n2g", 8), ("gateb", 24), ("cmw", 12), ("fcw", 132), ("mu", 14),
               ("w0", 4), ("a0", 4), ("kk", 4), ("ka", 4), ("rk", 4), ("fb", 1), ("gng8", 8), ("gnb8", 8), ("gng", 4), ("gnb", 4)):
    PCO[_n] = _o
    _o += _w
NPC = _o


class Buf:
    __slots__ = ("t", "name", "last_w", "readers", "chan", "last_dma")

    def __init__(self, t, name):
        self.t = t
        self.name = name
        self.last_w = []
        self.readers = []
        self.chan = None
        self.last_dma = None

    def __getitem__(self, k):
        return self.t[k]


class Node:
    __slots__ = ("id", "eng", "fns", "deps", "dur", "occ", "kind", "chan", "ev", "open", "kw")

    def __init__(self, id, eng, kind):
        self.id = id
        self.eng = eng
        self.kind = kind
        self.fns = []
        self.deps = set()
        self.dur = 0.0
        self.occ = 0.0
        self.chan = None
        self.ev = None
        self.open = False


def _nfree(ap):
    n = 1
    for d in ap.shape[1:]:
        n *= d
    return n


class Sched:
    NCHAN = 64
    XLAT = float(os.environ.get("K_XLAT", "0.3"))

    def __init__(self, nc, stack):
        self.nc = nc
        self.gstack = stack
        self.stack = stack
        self.q = {e: [] for e in ENGS}
        self.cnt = {e: 0 for e in ENGS}
        self.sems = {}
        for e in ENGS:
            self.sems[e] = stack.enter_context(nc.semaphore("s_" + e))
        self.chan_cnt = {}
        for i in range(self.NCHAN):
            k = f"d{i}"
            self.sems[k] = stack.enter_context(nc.semaphore("s_" + k))
            self.chan_cnt[k] = 0
        self.chan_next = 0
        self.known = {e: {} for e in ENGS}
        self.nbuf = 0
        self.ninstr = 0
        self.nodes = []
        self.pe_open = None
        self.sb_bytes = 0
        self.reorder = True

    def sbuf(self, shape, dtype, name=None):
        self.nbuf += 1
        name = f"{name or 'sb'}_{self.nbuf}"
        nb = int(np.prod(shape[1:])) * (2 if dtype == BF16 else 4)
        self.sb_bytes += ((nb + 31) // 32) * 32
        return Buf(self.stack.enter_context(self.nc.sbuf_tensor(name, list(shape), dtype)), name)

    def psum(self, shape, dtype, name=None):
        self.nbuf += 1
        name = f"{name or 'ps'}_{self.nbuf}"
        return Buf(self.stack.enter_context(self.nc.psum_tensor(name, list(shape), dtype)), name)

    def _chan(self, b):
        if b.chan is None:
            assert self.chan_next < self.NCHAN, "out of dma channels"
            b.chan = f"d{self.chan_next}"
            self.chan_next += 1
        return b.chan

    def _close_pe(self):
        if self.pe_open is not None:
            self.pe_open.open = False
            self.pe_open = None

    def _deps_of(self, reads, writes):
        deps = set()
        for r in reads:
            deps.update(r.last_w)
        for w in writes:
            deps.update(w.last_w)
            deps.update(w.readers)
        return deps

    def _touch(self, node, reads, writes):
        for r in reads:
            if not r.readers or r.readers[-1] is not node:
                r.readers.append(node)
        for w in writes:
            w.last_w = [node]
            w.readers = []

    def op(self, eng, fn, reads=(), writes=(), signal=True, same_ok=False, dur=0.3):
        self.ninstr += 1
        deps = self._deps_of(reads, writes)
        if eng == "pe":
            node = self.pe_open
            if node is None:
                node = Node(len(self.nodes), "pe", "op")
                self.nodes.append(node)
                node.open = True
                self.pe_open = node
            deps.discard(node)
            node.deps |= deps
            node.fns.append(fn)
            node.dur += dur
            node.occ += dur
            if signal:
                self._close_pe()
        else:
            self._close_pe()
            node = Node(len(self.nodes), eng, "op")
            self.nodes.append(node)
            node.deps = deps
            node.fns.append(fn)
            node.dur = dur
            node.occ = dur
        self._touch(node, reads, writes)
        return node

    def dma(self, eng, out_ap, in_ap, reads=(), writes=(), **kw):
        self._close_pe()
        self.ninstr += 1
        cb = writes[0] if writes else reads[0]
        key = self._chan(cb)
        node = Node(len(self.nodes), eng, "dma")
        self.nodes.append(node)
        node.chan = key
        node.deps = self._deps_of(reads, writes)
        if cb.last_dma is not None:
            node.deps.add(cb.last_dma)
        cb.last_dma = node
        nbytes = _nfree(out_ap) * out_ap.shape[0] * (2 if out_ap.dtype == BF16 else 4)
        node.dur = 2.0 + nbytes / 1.0e5
        node.occ = 0.1 if eng == "sp" else 0.6
        node.fns.append((out_ap, in_ap, kw))
        self._touch(node, reads, writes)
        return node

    def _schedule(self):
        import heapq
        nodes = self.nodes
        if not self.reorder:
            return list(nodes)
        succ = {n.id: [] for n in nodes}
        indeg = {}
        alive = {n.id for n in nodes}
        for n in nodes:
            n.deps = {d for d in n.deps if d.id in alive and d.ev is None}
            indeg[n.id] = len(n.deps)
            for d in n.deps:
                succ[d.id].append(n)
        tail = {}
        for n in reversed(nodes):
            t = 0.0
            for s_ in succ[n.id]:
                v = tail[s_.id] + self.XLAT
                if v > t:
                    t = v
            tail[n.id] = t + n.dur
        est = {n.id: 0.0 for n in nodes}
        wait = {e: [] for e in ENGS}
        avail = {e: [] for e in ENGS}
        for n in nodes:
            if indeg[n.id] == 0:
                heapq.heappush(wait[n.eng], (0.0, n.id, n))
        free = {e: 0.0 for e in ENGS}
        order = []
        left = len(nodes)
        while left:
            best = None
            for e in ENGS:
                if avail[e]:
                    tc = free[e]
                elif wait[e]:
                    tc = max(free[e], wait[e][0][0])
                else:
                    continue
                if best is None or tc < best[0]:
                    best = (tc, e)
            assert best is not None, "dependency cycle in schedule"
            tc, e = best
            w = wait[e]
            av = avail[e]
            while w and w[0][0] <= tc:
                _, i_, n_ = heapq.heappop(w)
                heapq.heappush(av, (-tail[i_], i_, n_))
            _, _, n = heapq.heappop(av)
            free[e] = tc + n.occ
            fin = tc + n.dur
            order.append(n)
            left -= 1
            for s_ in succ[n.id]:
                if est[s_.id] < fin + self.XLAT:
                    est[s_.id] = fin + self.XLAT
                indeg[s_.id] -= 1
                if indeg[s_.id] == 0:
                    heapq.heappush(wait[s_.eng], (est[s_.id], s_.id, s_))
        return order

    def _need(self, eng, evs):
        kn = self.known[eng]
        out = {}
        for k, v in evs:
            if kn.get(k, 0) < v and out.get(k, 0) < v:
                out[k] = v
        for k, v in out.items():
            kn[k] = v
        return list(out.items())

    def _emit_waits(self, eng, waits):
        for k, v in waits:
            self.q[eng].append(lambda e, s=self.sems[k], v=v: e.wait_ge(s, v))

    def flush(self):
        self._close_pe()
        order = self._schedule()
        for n in order:
            evs = [d.ev for d in n.deps if d.ev is not None]
            eng = n.eng
            self._emit_waits(eng, self._need(eng, evs))
            if n.kind == "dma":
                key = n.chan
                self.chan_cnt[key] += 16
                n.ev = (key, self.chan_cnt[key])
                o, i, kw = n.fns[0]
                self.q[eng].append(
                    lambda e, o=o, i=i, s=self.sems[key], kw=kw: e.dma_start(out=o, in_=i, **kw).then_inc(s, 16))
            else:
                self.cnt[eng] += 1
                n.ev = (eng, self.cnt[eng])
                s = self.sems[eng]
                for fn in n.fns[:-1]:
                    self.q[eng].append(lambda e, fn=fn: fn(e))
                self.q[eng].append(lambda e, fn=n.fns[-1], s=s: fn(e).then_inc(s, 1))
        self.nodes = []

    def barrier(self):
        self.flush()
        deps = [(e, self.cnt[e]) for e in ENGS if self.cnt[e] > 0]
        deps += [(k, v) for k, v in self.chan_cnt.items() if v > 0]
        for e in ENGS:
            self._emit_waits(e, self._need(e, deps))

    def emit(self):
        nc = self.nc
        q = self.q
        with nc.Block() as block:
            @block.tensor
            def _(e):
                for f in q["pe"]:
                    f(e)

            @block.scalar
            def _(e):
                for f in q["act"]:
                    f(e)

            @block.vector
            def _(e):
                for f in q["dve"]:
                    f(e)

            @block.gpsimd
            def _(e):
                for f in q["pool"]:
                    f(e)

            @block.sync
            def _(e):
                for f in q["sp"]:
                    f(e)
        self.q = {e: [] for e in ENGS}

    @contextlib.contextmanager
    def phase(self, reorder=True, xlat=None):
        with contextlib.ExitStack() as ph:
            self.stack = ph
            self.chan_next = 0
            self.reorder = reorder
            self.XLAT = xlat if xlat is not None else Sched.XLAT
            yield
            self.barrier()
            self.emit()
        self.stack = self.gstack

    @staticmethod
    def _d(eng, ap):
        n = _nfree(ap)
        if eng == "act":
            return 0.22 + n / 1400.0
        if eng == "dve":
            return 0.12 + n / 1000.0
        return 0.3 + n / 600.0

    def A(self, out, in_, func, r, w, eng="act", **kw):
        return self.op("act", lambda e: e.activation(out=out, in_=in_, func=func, **kw), r, w, dur=self._d("act", out))

    def TT(self, eng, out, in0, in1, op, r, w):
        return self.op(eng, lambda e: e.tensor_tensor(out=out, in0=in0, in1=in1, op=op), r, w, dur=self._d(eng, out))

    def TS(self, eng, out, in0, s1, s2, op0, op1, r, w):
        if s2 is None:
            return self.op(eng, lambda e: e.tensor_scalar(out=out, in0=in0, scalar1=s1, scalar2=None, op0=op0), r, w,
                           dur=self._d(eng, out))
        return self.op(eng, lambda e: e.tensor_scalar(out=out, in0=in0, scalar1=s1, scalar2=s2, op0=op0, op1=op1), r, w,
                       dur=self._d(eng, out))

    def STT(self, eng, out, in0, sc, in1, op0, op1, r, w):
        return self.op(eng, lambda e: e.scalar_tensor_tensor(out=out, in0=in0, scalar=sc, in1=in1, op0=op0, op1=op1), r, w,
                       dur=self._d(eng, out))

    def CP(self, eng, out, in_, r, w):
        if eng == "act":
            return self.op("act", lambda e: e.copy(out=out, in_=in_), r, w, dur=self._d("act", out))
        return self.op(eng, lambda e: e.tensor_copy(out=out, in_=in_), r, w, dur=self._d(eng, out))

    def MS(self, eng, ap, val, w):
        return self.op(eng, lambda e: e.memset(ap, val), (), w, dur=self._d(eng, ap))

    def MM(self, out, lhsT, rhs, start, stop, r, w, signal=True):
        n = _nfree(rhs)
        d = 0.03 + n / 2400.0
        if rhs.dtype == F32:
            d *= 4
        return self.op("pe", lambda e: e.matmul(out, lhsT=lhsT, rhs=rhs, start=start, stop=stop), r, w,
                       signal=signal, dur=d)

    def TR(self, out, in_, ident, r, w, signal=True):
        return self.op("pe", lambda e: e.transpose(out=out, in_=in_, identity=ident), r, w, signal=signal, dur=0.1)


def bc(ap2, n):
    return ap2.unsqueeze(2).to_broadcast([ap2.shape[0], ap2.shape[1], n])


def build(T, depth=DEPTH, dbg=(), nph=99):
    nc = bass.Bass("TRN2", target_bir_lowering=False)
    NT = T // 512
    NKT = T // 128

    def dram(name, shape, dt, kind="Internal"):
        if name in dbg:
            kind = "ExternalOutput"
        return nc.dram_tensor(name, list(shape), dt, kind=kind).ap()

    x_in = dram("x", [T, D], F32, "ExternalInput")
    w_in = dram("w_in", [depth, D, NIN], F32, "ExternalInput")
    w_br = dram("w_branch", [depth, 3, 512, D], F32, "ExternalInput")
    w_o = dram("w_o", [depth, D, D], F32, "ExternalInput")
    w_up = dram("ffn_w_up", [depth, D, 2 * FF], F32, "ExternalInput")
    w_dn = dram("ffn_w_down", [depth, FF, D], F32, "ExternalInput")
    r_wup = dram("rwkv_w_up", [depth, 64, 512], F32, "ExternalInput")
    r_aup = dram("rwkv_a_up", [depth, 64, 512], F32, "ExternalInput")
    r_gup = dram("rwkv_g_up", [depth, 128, 512], F32, "ExternalInput")
    pcd = dram("pc", [depth, 128, NPC], F32, "ExternalInput")
    gfin = dram("final_norm_g", [D], F32, "ExternalInput")
    cst = dram("cst", [128, 128 * 6], F32, "ExternalInput")
    mk = dram("mk", [64, 4, 8, 128], F32, "ExternalInput")
    y_out = dram("y", [T, D], F32, "ExternalOutput")

    xa = dram("xa", [T, D], F32)
    xb = dram("xb", [T, D], F32)
    zc = dram("zc", [1536, T], BF16)
    zr = dram("zr", [1792, T], BF16)
    qT = dram("qT", [512, T], BF16)
    kT = dram("kT", [512, T], BF16)
    zf = dram("zf", [8, T], F32)
    gT = dram("gT", [3072, T], BF16)
    vtm = dram("vtm", [T, 528], BF16)
    cqk = dram("cqk", [8, 6, T], BF16)
    yaT = dram("yaT", [512, T], BF16)
    ycT = dram("ycT", [512, T], BF16)
    ybn = dram("ybn", [512, T], BF16)
    bsc = dram("bsc", [512, T], BF16)
    gsc = dram("gsc", [512, T], BF16)

    with contextlib.ExitStack() as gst:
        S = Sched(nc, gst)
        pcs = [S.sbuf([128, NPC], F32, f"pc{l}") for l in range(depth)]
        cf = S.sbuf([128, 768], F32, "cstf")
        identb = S.sbuf([128, 128], BF16, "identb")
        trib = S.sbuf([128, 128], BF16, "trib")
        o64b = S.sbuf([64, 64], BF16, "o64b")
        bdmb = S.sbuf([128, 128], BF16, "bdmb")
        omka = [S.sbuf([128, 4], F32, f"omka{l}") for l in range(depth)]
        nfb = [S.sbuf([128, 1], F32, f"nfb{l}") for l in range(depth)]
        with S.phase():
            for l in range(depth):
                S.dma("sp", pcs[l][:], pcd[l], writes=[pcs[l]])
            S.dma("sp", cf[:], cst, writes=[cf])
            S.CP("dve", identb[:], cf[:, 0:128], [cf], [identb])
            S.CP("dve", trib[:], cf[:, 128:256], [cf], [trib])
            S.CP("dve", o64b[:], cf[0:64, 384:448], [cf], [o64b])
            S.TS("dve", bdmb[:], cf[:, 256:384], 1.0 / 64, None, ALU.mult, None, [cf], [bdmb])
            for l in range(depth):
                o = PCO["ka"]
                S.TS("dve", omka[l][:], pcs[l][:, o:o + 4], -1.0, 1.0, ALU.mult, ALU.add, [pcs[l]], [omka[l]])
                o = PCO["fb"]
                S.TS("dve", nfb[l][:], pcs[l][:, o:o + 1], -1.0, None, ALU.mult, None, [pcs[l]], [nfb[l]])
        bdones = cf[:, 256:384]
        ones64 = cf[0:64, 384:448]
        onesf = cf[:, 512:640]

        def pcol(l, name, j=0, n=1):
            o = PCO[name] + j
            return pcs[l][:, o:o + n]

        def load_w_bf16(dst, dst_ap_fn, src_ap_fn, pieces, stages, engs=("act", "dve")):
            for i, pc_ in enumerate(pieces):
                st = stages[i % len(stages)]
                sv = st_view(st, pc_)
                S.dma("sp", sv, src_ap_fn(pc_), writes=[st])
                S.CP(engs[i % len(engs)], dst_ap_fn(pc_), sv, [st], [dst])

        def st_view(st, pc_):
            kc, n = pc_[2], pc_[1]
            return st[:, 0:kc * n].rearrange("p (k n) -> p k n", k=kc)

        def rmsnorm_T(xt, gcol, xnT, s, nb):
            ss, rs, junk, xs, pT = nb
            S.MS("pool", ss[:], 0.0, [ss])
            S.A(junk[:], xt[:], AF.Square, [xt], [junk, ss], accum_out=ss[:])
            S.A(rs[:], ss[:], AF.Sqrt, [ss], [rs], scale=1.0 / D, bias=1e-6)
            S.op("dve", lambda e: e.reciprocal(out=rs[:], in_=rs[:]), [rs], [rs])
            S.TS("dve", xs[:], xt[:], rs[:, 0:1], None, ALU.mult, None, [xt, rs], [xs])
            for kc in range(8):
                S.TR(pT[:, kc, :], xs[:, kc * 128:(kc + 1) * 128], identb[:], [xs, identb], [pT], signal=(kc == 7))
            S.TT("dve", xnT[:, :, s * 128:(s + 1) * 128], pT[:], bc(gcol, 128), ALU.mult, [pT], [xnT])

        def phase_inproj(l, xsrc):
            with S.phase():
                wres = S.sbuf([128, 8, NIN], BF16, "wres")
                stages = [S.sbuf([128, 2048], F32, f"stg{i}") for i in range(2)]
                pieces = [(c0, min(256, NIN - c0), 8) for c0 in range(0, NIN, 256)]
                load_w_bf16(wres, lambda p: wres[:, :, p[0]:p[0] + p[1]],
                            lambda p: w_in[l, :, p[0]:p[0] + p[1]].rearrange("(k q) n -> q k n", q=128),
                            pieces, stages)
                if CUT == 1:
                    return
                xbufs = [S.sbuf([128, D], F32, f"xb{i}") for i in range(2)]
                nbs = [(S.sbuf([128, 1], F32), S.sbuf([128, 1], F32), S.sbuf([128, D], BF16), S.sbuf([128, D], BF16),
                        S.psum([128, 8, 128], BF16)) for _ in range(2)]
                xnTs = [S.sbuf([128, 8, 512], BF16, f"xnT{i}") for i in range(2)]
                obs = [S.sbuf([128, 512], BF16, f"ob{i}") for i in range(4)]
                fsts = [S.sbuf([8, 512], F32, f"fst{i}") for i in range(2)]
                fe = S.sbuf([8, 512], F32, "fe")
                fcs = S.sbuf([8, 512], F32, "fcs")
                fr1 = S.sbuf([8, 512], F32, "fr1")
                fcar = S.sbuf([8, 1], F32, "fcar")
                fobs = [S.sbuf([8, 6, 512], BF16, f"fob{i}") for i in range(2)]
                S.MS("pool", fcar[:], 0.0, [fcar])
                vsts = [S.sbuf([128, 8, 66], BF16, f"vst{i}") for i in range(2)]
                pss = [S.psum([128, 512], F32, f"psm{i}") for i in range(4)]
                for v in vsts:
                    S.MS("pool", v[:], 1.0, [v])
                chunks = []
                for j in range(12):
                    chunks.append((j * 128, 128, zc, j * 128, "copy"))
                for j in range(14):
                    chunks.append((1536 + j * 128, 128, zr, j * 128, "copy"))
                for j in range(4):
                    chunks.append((3328 + j * 128, 128, qT, j * 128, "qscale"))
                for j in range(4):
                    chunks.append((3840 + j * 128, 128, kT, j * 128, "copyA"))
                chunks.append((4864, 8, zf, 0, "f"))
                for j in range(24):
                    chunks.append((4872 + j * 128, 128, gT, j * 128, "gate"))
                cnt = 0
                for i in range(NT):
                    xnT = xnTs[i % 2]
                    for s in range(4):
                        xt = xbufs[(i * 4 + s) % 2]
                        r0 = (i * 4 + s) * 128
                        S.dma("sp", xt[:], xsrc[r0:r0 + 128, :], writes=[xt])
                        rmsnorm_T(xt, pcol(l, "n1g", 0, 8), xnT, s, nbs[(i * 4 + s) % 2])
                    if CUT == 2:
                        continue
                    for (c0, M, dest, row0, mode) in (chunks[:2] if CUT == 3 else chunks):
                        ps = pss[cnt % 4]
                        for kc in range(8):
                            S.MM(ps[0:M, :], wres[:, kc, c0:c0 + M], xnT[:, kc, :], kc == 0, kc == 7,
                                 [wres, xnT], [ps], signal=(kc == 7))
                        if mode == "f":
                            fs = fsts[i % 2]
                            tsl = slice(i * 512, (i + 1) * 512)
                            S.CP("dve", fs[:], ps[0:8, :], [ps], [fs])
                            S.A(fe[:], fs[:], AF.Exp, [fs], [fe], scale=-1.0, bias=nfb[l][0:8, 0:1])
                            S.A(fe[:], fe[:], AF.Ln, [fe], [fe], bias=1.0)
                            S.op("dve", lambda e: e.tensor_tensor_scan(out=fcs[:], data0=fe[:], data1=fe[:], initial=0.0,
                                                                        op0=ALU.add, op1=ALU.bypass), [fe], [fcs], dur=1.2)
                            S.TS("dve", fcs[:], fcs[:], fcar[:, 0:1], None, ALU.add, None, [fcs, fcar], [fcs])
                            S.CP("dve", fcar[:], fcs[:, 511:512], [fcs], [fcar])
                            fo = fobs[i % 2]
                            S.A(fo[:, 0, :], fcs[:], AF.Copy, [fcs], [fo], scale=-1.0)
                            S.STT("dve", fr1[:], fcs[:], -1.0, fo[:, 0, :], ALU.mult, ALU.subtract, [fcs, fo], [fr1])
                            S.CP("act", fo[:, 1, :], fr1[:], [fr1], [fo])
                            S.TT("dve", fr1[:], fr1[:], fo[:, 1, :], ALU.subtract, [fr1, fo], [fr1])
                            S.CP("act", fo[:, 2, :], fr1[:], [fr1], [fo])
                            S.A(fo[:, 3:6, :], fo[:, 0:3, :], AF.Copy, [fo], [fo], scale=-1.0)
                            S.dma("pool", cqk[:, :, tsl], fo[:], reads=[fo])
                        else:
                            ob = obs[cnt % 4]
                            if mode == "copy":
                                S.CP("dve", ob[:], ps[:], [ps], [ob])
                            elif mode == "copyA":
                                S.CP("act", ob[:], ps[:], [ps], [ob])
                            elif mode == "qscale":
                                S.A(ob[:], ps[:], AF.Copy, [ps], [ob], scale=0.125)
                            else:
                                j = (c0 - 4872) // 128
                                S.A(ob[:], ps[:], AF.Sigmoid, [ps], [ob], bias=pcol(l, "gateb", j))
                            S.dma("pool", dest[row0:row0 + 128, i * 512:(i + 1) * 512], ob[:], reads=[ob])
                        cnt += 1
                    for s in range(4 if CUT not in (3, 4) else 0):
                        ps = pss[cnt % 4]
                        for kc in range(8):
                            S.MM(ps[:], xnT[:, kc, s * 128:(s + 1) * 128], wres[:, kc, 4352:4864], kc == 0, kc == 7,
                                 [wres, xnT], [ps], signal=(kc == 7))
                        vs = vsts[s % 2]
                        S.CP("act", vs[:, :, 0:64], ps[:].rearrange("p (h d) -> p h d", h=8), [ps], [vs])
                        r0 = (i * 4 + s) * 128
                        S.dma("pool", vtm[r0:r0 + 128, :], vs[:].rearrange("p h d -> p (h d)"), reads=[vs])
                        cnt += 1

        def phase_conv(l, own_phase=True):
            TB = min(T, 1024)
            with (S.phase() if own_phase else contextlib.nullcontext()):
                ins = [[S.sbuf([128, TB], BF16) for _ in range(3)] for _ in range(2)]
                hbs = [S.sbuf([128, TB + 2], F32) for _ in range(2)]
                acc = [S.sbuf([128, TB], F32) for _ in range(2)]
                outs = [S.sbuf([128, TB], BF16) for _ in range(2)]
                n = 0
                for j in range(4):
                    for tb in range(T // TB):
                        Bt, Ct, ht = ins[n % 2]
                        hb = hbs[n % 2]
                        ac = acc[n % 2]
                        ot = outs[n % 2]
                        cs = slice(tb * TB, (tb + 1) * TB)
                        S.dma("sp", Bt[:], zc[j * 128:(j + 1) * 128, cs], writes=[Bt])
                        S.dma("sp", Ct[:], zc[512 + j * 128:512 + (j + 1) * 128, cs], writes=[Ct])
                        S.dma("sp", ht[:], zc[1024 + j * 128:1024 + (j + 1) * 128, cs], writes=[ht])
                        if tb == 0:
                            S.MS("pool", hb[:, 0:2], 0.0, [hb])
                        else:
                            hp_ = hbs[(n - 1) % 2]
                            S.CP("pool", hb[:, 0:2], hp_[:, TB:TB + 2], [hp_], [hb])
                        S.TT("dve", hb[:, 2:TB + 2], Ct[:], ht[:], ALU.mult, [Ct, ht], [hb])
                        S.A(ac[:], hb[:, 2:TB + 2], AF.Copy, [hb], [ac], scale=pcol(l, "cmw", j * 3 + 2))
                        S.STT("dve", ac[:], hb[:, 1:TB + 1], pcol(l, "cmw", j * 3 + 1), ac[:], ALU.mult, ALU.add, [hb, ac], [ac])
                        S.STT("dve", ac[:], hb[:, 0:TB], pcol(l, "cmw", j * 3 + 0), ac[:], ALU.mult, ALU.add, [hb, ac], [ac])
                        S.TT("dve", ot[:], ac[:], Bt[:], ALU.mult, [ac, Bt], [ot])
                        S.dma("pool", yaT[j * 128:(j + 1) * 128, cs], ot[:], reads=[ot])
                        n += 1

        def phase_fcum(l):
            with S.phase():
                z = S.sbuf([8, T], F32)
                e1 = S.sbuf([8, T], F32)
                cs_ = S.sbuf([8, T], F32)
                r1 = z
                ob = [S.sbuf([8, T], BF16) for _ in range(6)]
                S.dma("sp", z[:], zf, writes=[z])
                S.A(e1[:], z[:], AF.Exp, [z], [e1], scale=-1.0, bias=nfb[l][0:8, 0:1])
                S.A(z[:], e1[:], AF.Ln, [e1], [z], bias=1.0)
                S.op("dve", lambda e: e.tensor_tensor_scan(out=cs_[:], data0=z[:], data1=z[:], initial=0.0,
                                                            op0=ALU.add, op1=ALU.bypass), [z], [cs_])
                S.A(ob[0][:], cs_[:], AF.Copy, [cs_], [ob[0]], scale=-1.0)
                S.STT("dve", r1[:], cs_[:], -1.0, ob[0][:], ALU.mult, ALU.subtract, [cs_, ob[0]], [r1])
                S.CP("act", ob[1][:], r1[:], [r1], [ob[1]])
                S.TT("dve", e1[:], r1[:], ob[1][:], ALU.subtract, [r1, ob[1]], [e1])
                S.CP("act", ob[2][:], e1[:], [e1], [ob[2]])
                for k in range(3):
                    S.A(ob[3 + k][:], ob[k][:], AF.Copy, [ob[k]], [ob[3 + k]], scale=-1.0)
                for k in range(6):
                    S.dma("pool", cqk[:, k, :], ob[k][:], reads=[ob[k]])

        def phase_attn(l):
            NQB = T // 512
            with S.phase():
                phase_conv(l, own_phase=False)
                vext = S.sbuf([128, NKT, 528], BF16, "vext")
                VS = min(8, NKT)
                for a in range(0, NKT, VS):
                    S.dma("sp", vext[:, a:a + VS, :], vtm[a * 128:(a + VS) * 128, :].rearrange("(n p) c -> p n c", p=128),
                          writes=[vext])
                qas = [S.sbuf([70, T], BF16, f"qa{i}") for i in range(2)]
                kas = [S.sbuf([70, T], BF16, f"ka{i}") for i in range(2)]
                NSL = int(os.environ.get("K_NSL", "5"))
                LOOK = int(os.environ.get("K_LOOK", "3"))
                pts = [S.sbuf([128, 512], BF16, f"pt{i}") for i in range(NSL)]
                osb = [S.sbuf([65, 512], F32, f"osb{i}") for i in range(2)]
                rdn = [S.sbuf([65, 512], F32, f"rdn{i}") for i in range(2)]
                yos = [S.sbuf([64, 512], BF16, f"yo{i}") for i in range(2)]
                psS = [S.psum([128, 512], F32, f"psS{i}") for i in range(NSL)]
                psO = [S.psum([128, 512], F32, f"psO{i}") for i in range(2)]
                psB = [S.psum([128, 512], F32, f"psB{i}") for i in range(1)]
                steps = [(qb, kt) for qb in range(NQB) for kt in range(4 * qb + 4)]
                nst = len(steps)
                gcnt = 0
                nq = 0
                for h in range(8):
                    qa = qas[h % 2]
                    ka = kas[h % 2]
                    S.dma("sp", qa[0:64, :], qT[h * 64:(h + 1) * 64, :], writes=[qa])
                    S.MS("pool", qa[64:70, :], 1.0, [qa])
                    S.dma("sp", qa[64:67, :], cqk[h, 0:3, :], writes=[qa])
                    S.dma("sp", ka[0:64, :], kT[h * 64:(h + 1) * 64, :], writes=[ka])
                    S.MS("pool", ka[64:70, :], 1.0, [ka])
                    S.dma("sp", ka[67:70, :], cqk[h, 3:6, :], writes=[ka])

                    def geom(i):
                        qb, kt = steps[i]
                        j = kt - 4 * qb
                        q0 = j * 128 if j > 0 else 0
                        return qb, kt, j, q0

                    def issue_S(i, qa=qa, ka=ka):
                        qb, kt, j, q0 = geom(i)
                        sp_ = psS[(gcnt + i) % NSL]
                        S.MM(sp_[:, q0:512], ka[:, kt * 128:(kt + 1) * 128], qa[:, qb * 512 + q0:(qb + 1) * 512],
                             True, True, [ka, qa], [sp_])

                    pending = []

                    def epi2(qb, ob, rd, yo, h=h):
                        pb = psB[0]
                        S.MM(pb[0:64, :], cf[64:65, 512:576], rd[64:65, :], True, True, [cf, rd], [pb])
                        S.TT("dve", yo[:], ob[0:64, :], pb[0:64, :], ALU.mult, [ob, pb], [yo])
                        S.dma("pool", ycT[h * 64:(h + 1) * 64, qb * 512:(qb + 1) * 512], yo[:], reads=[yo])

                    for i in range(min(LOOK, nst)):
                        issue_S(i)
                    for i0 in range(0, nst, 2):
                        pair = [i for i in (i0, i0 + 1) if i < nst]
                        for i in pair:
                            qb, kt, j, q0 = geom(i)
                            sp_ = psS[(gcnt + i) % NSL]
                            pt = pts[(gcnt + i) % NSL]
                            S.A(pt[:, q0:512], sp_[:, q0:512], AF.Exp, [sp_], [pt])
                            if j >= 0:
                                S.TT("dve", pt[:, q0:q0 + 128], pt[:, q0:q0 + 128], trib[:], ALU.mult, [pt, trib], [pt])
                        for i in pair:
                            if i + LOOK < nst:
                                issue_S(i + LOOK)
                        for i in pair:
                            qb, kt, j, q0 = geom(i)
                            nkt = 4 * qb + 4
                            pt = pts[(gcnt + i) % NSL]
                            ops_ = psO[(nq + qb) % 2]
                            extra = [pts[(gcnt + k) % NSL] for k in pair]
                            S.MM(ops_[0:65, q0:512], vext[:, kt, h * 66:h * 66 + 65], pt[:, q0:512],
                                 kt == 0, kt == nkt - 1, [vext] + extra, [ops_], signal=(kt == nkt - 1))
                            while pending and pending[0][0] <= i:
                                pending.pop(0)[1]()
                            if kt == nkt - 1:
                                ob = osb[(nq + qb) % 2]
                                rd = rdn[(nq + qb) % 2]
                                yo = yos[(nq + qb) % 2]
                                S.CP("dve", ob[:], ops_[0:65, :], [ops_], [ob])
                                S.op("dve", lambda e, rd=rd, ob=ob: e.reciprocal(out=rd[64:65, :], in_=ob[64:65, :]), [ob], [rd])
                                pending.append((i + 3, lambda qb=qb, ob=ob, rd=rd, yo=yo: epi2(qb, ob, rd, yo)))
                    for _, fn in pending:
                        fn()
                    gcnt += nst
                    nq += NQB

        def phase_rwkv(l):
            NTB = T // 512
            with S.phase(xlat=1.0):
                stg = S.sbuf([128, 512], F32, "rstg")
                waw = S.sbuf([128, 512], BF16, "waw")
                gup = S.sbuf([128, 512], BF16, "gup")
                S.dma("sp", stg[0:64, :], r_wup[l], writes=[stg])
                S.dma("sp", stg[64:128, :], r_aup[l], writes=[stg])
                S.CP("dve", waw[:], stg[:], [stg], [waw])
                S.dma("sp", stg[:], r_gup[l], writes=[stg])
                S.CP("dve", gup[:], stg[:], [stg], [gup])
                mkf = S.sbuf([64, 3, 8, 128], F32, "mkf")
                S.dma("sp", mkf[:], mk[:, 0:3], writes=[mkf])
                mG1 = mkf[:, 0]
                mG2 = mkf[:, 1]
                mL = mkf[:, 2, :, 0:64]
                id8 = mkf[:, 2, :, 64:128]
                St = S.sbuf([64, 8, 64], F32, "St")
                Sb = S.sbuf([64, 8, 64], BF16, "Sb")
                Stmp = S.sbuf([64, 8, 64], F32, "Stmp")
                S.MS("pool", St[:], 0.0, [St])
                S.MS("pool", Sb[:], 0.0, [Sb])
                EQ8 = S.sbuf([64, 8, 8, 2, 64], BF16, "EQ8")
                FB8 = S.sbuf([64, 8, 2, 512], BF16, "FB8")
                pC8 = S.sbuf([64, 8, 8], F32, "pC8")
                EQ = S.sbuf([128, 4, 8, 2, 64], BF16, "EQ")
                FB = S.sbuf([128, 4, 2, 512], BF16, "FB")
                Vb = S.sbuf([128, 4, 512], BF16, "Vb")
                pC = S.sbuf([128, 4, 8], F32, "pC")
                Ftm = S.sbuf([64, 8, 512], BF16, "Ftm")
                nBtm = S.sbuf([64, 8, 512], BF16, "nBtm")
                Vtm = S.sbuf([64, 8, 512], BF16, "Vtm")
                zls = [S.sbuf([128, 514], BF16, f"zl{i}") for i in range(3)]
                f32t = lambda n: S.sbuf([128, 512], F32, n)
                dtmp, twa, sgf, rr, kk_, vv, sig, csf = [f32t(n) for n in ("dtmp", "twa", "sgf", "rr", "kk", "vv", "sig", "csf")]
                csm, pp, pinv, pprev, aa, kap, ksq, rinv, kh, ktl, bet = [f32t(n) for n in (
                    "csm", "pp", "pinv", "pprev", "aa", "kap", "ksq", "rinv", "kh", "ktl", "bet")]
                t1 = dtmp
                rk = ksq
                twab = S.sbuf([128, 512], BF16, "twab")
                sgb = S.sbuf([128, 512], BF16, "sgb")
                offs = S.sbuf([128, 8], F32, "offs")
                gob = [S.sbuf([128, 512], BF16, f"gob{i}") for i in range(2)]
                bob = [S.sbuf([128, 512], BF16, f"bob{i}") for i in range(2)]
                G1bs = [S.sbuf([64, 8, 128], BF16, f"G1b{i}") for i in range(2)]
                G2bs = [S.sbuf([64, 8, 128], BF16, f"G2b{i}") for i in range(2)]
                Pfin = [S.sbuf([64, 8, 64], BF16, f"Pfin{i}") for i in range(2)]
                Ab = [S.sbuf([64, 8, 64], BF16, f"Ab{i}") for i in range(2)]
                Bb = [S.sbuf([64, 8, 64], BF16, f"Bb{i}") for i in range(2)]
                Pb = [S.sbuf([64, 8, 64], BF16, f"Pb{i}") for i in range(2)]
                IBn = S.sbuf([64, 8, 64], BF16, "IBn")
                Xb = S.sbuf([64, 8, 64], BF16, "Xb")
                Ub = S.sbuf([64, 8, 64], BF16, "Ub")
                Ycs = [S.sbuf([64, 512], F32, f"Yc{i}") for i in range(2)]
                ynbs = [S.sbuf([64, 8, 8, 64], BF16, f"ynb{i}") for i in range(2)]
                gsqb = S.sbuf([64, 512], BF16, "gsqb")
                Ycb = S.sbuf([64, 512], BF16, "Ycb")
                gd_ = S.sbuf([64, 512], F32, "gd")
                gm2 = S.sbuf([64, 512], F32, "gm2")
                gmean = S.sbuf([64, 512], F32, "gmean")
                gvar = S.sbuf([64, 512], F32, "gvar")
                QA, QB, QC, RA, RB, RC, W = [S.psum([128, 512], F32, f"rp{i}") for i in range(7)]
                PT = S.psum([64, 8, 128], BF16, "rpT")

                def v3(ap, a):
                    return ap.rearrange("p (a b) -> p a b", a=a)

                nzc = [0]

                def mix(dst, row0, mucol, tb):
                    zl = zls[nzc[0] % 3]
                    nzc[0] += 1
                    c0 = tb * 512
                    if tb == 0:
                        S.MS("pool", zl[:, 0:2], 0.0, [zl])
                        S.dma("sp", zl[:, 2:514], zr[row0:row0 + 128, 0:512], writes=[zl])
                    else:
                        S.dma("sp", zl[:, 1:514], zr[row0:row0 + 128, c0 - 1:c0 + 512], writes=[zl])
                    S.TT("pool", dtmp[:], zl[:, 1:513], zl[:, 2:514], ALU.subtract, [zl], [dtmp])
                    S.STT("dve", dst[:], dtmp[:], mucol, zl[:, 2:514], ALU.mult, ALU.add, [dtmp, zl], [dst])

                def stage1(tb):
                    tcs = slice(tb * 512, (tb + 1) * 512)
                    mix(twa, 1536, pcol(l, "mu", 12), tb)
                    S.A(twab[0:64, :], twa[0:64, :], AF.Tanh, [twa], [twab])
                    S.CP("act", twab[64:128, :], twa[64:128, :], [twa], [twab])
                    yield
                    mix(sgf, 1664, pcol(l, "mu", 13), tb)
                    S.A(sgb[:], sgf[:], AF.Sigmoid, [sgf], [sgb])
                    yield
                    for hp in range(4):
                        hs = slice(hp * 128, (hp + 1) * 128)
                        mix(rr, hp * 128, pcol(l, "mu", hp), tb)
                        yield
                        mix(kk_, 512 + hp * 128, pcol(l, "mu", 4 + hp), tb)
                        yield
                        mix(vv, 1024 + hp * 128, pcol(l, "mu", 8 + hp), tb)
                        yield
                        S.MM(W[:], waw[0:64, hs], twab[0:64, :], True, True, [waw, twab], [W])
                        S.A(sig[:], W[:], AF.Sigmoid, [W], [sig], bias=pcol(l, "w0", hp))
                        yield
                        S.MM(W[:], waw[64:128, hs], twab[64:128, :], True, True, [waw, twab], [W])
                        S.A(aa[:], W[:], AF.Sigmoid, [W], [aa], bias=pcol(l, "a0", hp))
                        yield
                        S.MM(W[:], gup[:, hs], sgb[:], True, True, [gup, sgb], [W])
                        go = gob[(tb * 4 + hp) % 2]
                        S.CP("act", go[:], W[:], [W], [go])
                        S.dma("pool", gsc[hs, tcs], go[:], reads=[go])
                        yield
                        S.op("dve", lambda e: e.tensor_tensor_scan(out=csf[:], data0=sig[:], data1=sig[:], initial=0.0,
                                                                    op0=ALU.add, op1=ALU.bypass), [sig], [csf])
                        S.MS("pool", offs[:, 0:1], 0.0, [offs])
                        yield
                        S.CP("pool", offs[:, 1:8], v3(csf[:], 8)[:, 0:7, 63], [csf], [offs])
                        yield
                        S.TT("dve", v3(csf[:], 8), v3(csf[:], 8), bc(offs[:, 0:8], 64), ALU.subtract, [csf, offs], [csf])
                        yield
                        S.TT("pool", csm[:], csf[:], sig[:], ALU.subtract, [csf, sig], [csm])
                        S.A(pp[:], csf[:], AF.Exp, [csf], [pp], scale=-DS)
                        yield
                        S.A(pinv[:], csf[:], AF.Exp, [csf], [pinv], scale=DS)
                        S.A(pprev[:], csm[:], AF.Exp, [csm], [pprev], scale=-DS)
                        S.CP("pool", pC[:, hp, :], v3(pp[:], 8)[:, :, 63], [pp], [pC])
                        yield
                        S.A(kap[:], kk_[:], AF.Copy, [kk_], [kap], scale=pcol(l, "kk", hp))
                        yield
                        S.A(ksq[:], kap[:], AF.Square, [kap], [ksq])
                        yield
                        S.MM(W[:], bdones, ksq[:], True, True, [cf, ksq], [W])
                        S.A(rinv[:], W[:], AF.Ln, [W], [rinv], bias=1e-12)
                        S.A(rinv[:], rinv[:], AF.Exp, [rinv], [rinv], scale=-0.5)
                        yield
                        S.TS("pool", t1[:], aa[:], pcol(l, "ka", hp), omka[l][:, hp:hp + 1], ALU.mult, ALU.add, [aa, omka[l]], [t1])
                        yield
                        S.TT("dve", kh[:], kap[:], rinv[:], ALU.mult, [kap, rinv], [kh])
                        S.TT("pool", ktl[:], kk_[:], t1[:], ALU.mult, [kk_, t1], [ktl])
                        yield
                        S.TT("pool", bet[:], aa[:], kh[:], ALU.mult, [aa, kh], [bet])
                        S.TT("dve", EQ[:, hp, :, 1, :], v3(rr[:], 8), v3(pp[:], 8), ALU.mult, [rr, pp], [EQ])
                        yield
                        S.TT("pool", EQ[:, hp, :, 0, :], v3(kh[:], 8), v3(pprev[:], 8), ALU.mult, [kh, pprev], [EQ])
                        S.TT("pool", FB[:, hp, 0, :], ktl[:], pinv[:], ALU.mult, [ktl, pinv], [FB])
                        yield
                        S.TT("pool", FB[:, hp, 1, :], bet[:], pinv[:], ALU.mult, [bet, pinv], [FB])
                        S.CP("act", Vb[:, hp, :], vv[:], [vv], [Vb])
                        S.STT("dve", rk[:], rr[:], pcol(l, "rk", hp), ktl[:], ALU.mult, ALU.mult, [rr, ktl], [rk])
                        yield
                        S.MM(W[:], bdones, rk[:], True, True, [cf, rk], [W])
                        bo = bob[(tb * 4 + hp) % 2]
                        S.TT("dve", bo[:], W[:], vv[:], ALU.mult, [W, vv], [bo])
                        S.dma("pool", bsc[hs, tcs], bo[:], reads=[bo])
                        yield

                def stage2():
                    for hp in range(4):
                        hs = slice(hp * 128, (hp + 1) * 128)
                        for (src, dstT, neg) in ((FB[:, hp, 0, :], Ftm, False), (FB[:, hp, 1, :], nBtm, True),
                                                 (Vb[:, hp, :], Vtm, False)):
                            for c in range(8):
                                S.TR(PT[:, c, :], src[:, c * 64:(c + 1) * 64], identb[:], [FB, Vb, identb], [PT],
                                     signal=(c == 7))
                            if neg:
                                S.A(dstT[:, :, hs], PT[:], AF.Copy, [PT], [dstT], scale=-1.0)
                            else:
                                S.CP("dve", dstT[:, :, hs], PT[:], [PT], [dstT])
                    for par in range(2):
                        ps_ = slice(par * 64, par * 64 + 64)
                        S.dma("sp", EQ8[:].rearrange("p (a two) c e t -> p a two (c e t)", two=2)[:, :, par, :],
                              EQ[ps_].rearrange("p a c e t -> p a (c e t)"), reads=[EQ], writes=[EQ8])
                        S.dma("sp", FB8[:].rearrange("p (a two) e t -> p a two (e t)", two=2)[:, :, par, :],
                              FB[ps_].rearrange("p a e t -> p a (e t)"), reads=[FB], writes=[FB8])
                        S.dma("sp", pC8[:].rearrange("p (a two) c -> p a two c", two=2)[:, :, par, :],
                              pC[ps_], reads=[pC], writes=[pC8])

                def streamA(c, par):
                    ccs = slice(c * 64, (c + 1) * 64)
                    G1b, G2b = G1bs[par], G2bs[par]
                    ga = [v3(QA[0:64, :], 4), v3(QB[0:64, :], 4)]
                    bank = [QA, QB]
                    for h in range(8):
                        eq = EQ8[:, h, c, :, :].rearrange("p a b -> p (a b)")
                        S.MM(ga[h // 4][:, h % 4, :], FB8[:, h, 0, ccs], eq, True, True, [FB8, EQ8], [bank[h // 4]],
                             signal=(h % 4 == 3))
                    yield
                    S.TT("dve", G1b[:, 0:4, :], ga[0], mG1[:, 0:4, :], ALU.mult, [QA, mkf], [G1b])
                    S.TT("dve", G1b[:, 4:8, :], ga[1], mG1[:, 4:8, :], ALU.mult, [QB, mkf], [G1b])
                    g3 = v3(QC[0:64, :], 8)
                    for h in range(8):
                        S.MM(g3[:, h, :], EQ8[:, h, c, 0, :], FB8[:, h, 1, ccs], True, True, [FB8, EQ8], [QC], signal=(h == 7))
                    yield
                    for h in range(8):
                        eq = EQ8[:, h, c, :, :].rearrange("p a b -> p (a b)")
                        S.MM(ga[h // 4][:, h % 4, :], FB8[:, h, 1, ccs], eq, True, True, [FB8, EQ8], [bank[h // 4]],
                             signal=(h % 4 == 3))
                    S.TT("dve", Bb[0][:], g3, mL, ALU.mult, [QC, mkf], [Bb[0]])
                    yield
                    S.TT("dve", G2b[:, 0:4, :], ga[0], mG2[:, 0:4, :], ALU.mult, [QA, mkf], [G2b])
                    S.TT("dve", G2b[:, 4:8, :], ga[1], mG2[:, 4:8, :], ALU.mult, [QB, mkf], [G2b])
                    yield
                    S.TT("dve", Pb[0][:], id8, G2b[:, :, 0:64], ALU.subtract, [mkf, G2b], [Pb[0]])
                    yield
                    pa, pb_, pq = v3(QA[0:64, :], 8), v3(QB[0:64, :], 8), v3(QC[0:64, :], 8)
                    for j in range(5):
                        Ac, Bc, Pc = Ab[j % 2], Bb[j % 2], Pb[j % 2]
                        An, Bn = Ab[(j + 1) % 2], Bb[(j + 1) % 2]
                        Pn = Pb[(j + 1) % 2] if j < 4 else Pfin[par]
                        if j == 0:
                            Acb, Acv = G2b, (lambda h: G2b[:, h, 0:64])
                        else:
                            Acb, Acv = Ac, (lambda h, Ac=Ac: Ac[:, h, :])
                        for h in range(8):
                            S.MM(pb_[:, h, :], Acv(h), Bc[:, h, :], True, True, [Acb, Bc], [QB], signal=(h == 7))
                        if j < 4:
                            for h in range(8):
                                S.MM(pa[:, h, :], Bc[:, h, :], Acv(h), True, True, [Acb, Bc], [QA], signal=(h == 7))
                        yield
                        S.CP("act", Bn[:], pb_, [QB], [Bn])
                        if j < 4:
                            S.CP("act", An[:], pa, [QA], [An])
                        yield
                        S.TT("dve", IBn[:], Bn[:], id8, ALU.add, [Bn, mkf], [IBn])
                        yield
                        for h in range(8):
                            S.MM(pq[:, h, :], IBn[:, h, :], Pc[:, h, :], True, True, [IBn, Pc], [QC], signal=(h == 7))
                        yield
                        S.CP("act", Pn[:], pq, [QC], [Pn])
                        yield

                def streamB(c, par, yn):
                    G1b, G2b, Pf = G1bs[par], G2bs[par], Pfin[par]
                    px = v3(RA[0:64, :], 8)
                    for h in range(8):
                        hc = slice(h * 64, (h + 1) * 64)
                        S.MM(px[:, h, :], EQ8[:, h, c, 0, :], Sb[:, h, :], True, False, [EQ8, Sb], [RA], signal=False)
                        S.MM(px[:, h, :], G1b[:, h, 0:64], Vtm[:, c, hc], False, True, [G1b, Vtm], [RA], signal=(h == 7))
                    yield
                    S.CP("act", Xb[:], px, [RA], [Xb])
                    yield
                    pu = v3(RB[0:64, :], 8)
                    for h in range(8):
                        S.MM(pu[:, h, :], Pf[:, h, :], Xb[:, h, :], True, True, [Pf, Xb], [RB], signal=(h == 7))
                    yield
                    S.CP("act", Ub[:], pu, [RB], [Ub])
                    yield
                    pS = v3(RA[0:64, :], 8)
                    for h in range(8):
                        hc = slice(h * 64, (h + 1) * 64)
                        S.MM(pS[:, h, :], Ftm[:, c, hc], Vtm[:, c, hc], True, False, [Ftm, Vtm], [RA], signal=False)
                        S.MM(pS[:, h, :], nBtm[:, c, hc], Ub[:, h, :], False, True, [nBtm, Ub], [RA], signal=(h == 7))
                    py = v3(RC[0:64, :], 8)
                    for h in range(8):
                        hc = slice(h * 64, (h + 1) * 64)
                        S.MM(py[:, h, :], Sb[:, h, :], EQ8[:, h, c, 1, :], True, False, [EQ8, Sb], [RC], signal=False)
                        S.MM(py[:, h, :], Vtm[:, c, hc], G1b[:, h, 64:128], False, False, [G1b, Vtm], [RC], signal=False)
                        S.MM(py[:, h, :], Ub[:, h, :], G2b[:, h, 64:128], False, True, [G2b, Ub], [RC], signal=(h == 7))
                    yield
                    S.TT("dve", Stmp[:], St[:], pS, ALU.add, [St, RA], [Stmp])
                    S.CP("act", yn[:, :, c, :], py, [RC], [yn])
                    yield
                    S.TT("dve", Sb[:], Stmp[:], bc(pC8[:, :, c], 64), ALU.mult, [Stmp, pC8], [Sb])
                    yield
                    S.TT("pool", St[:], Stmp[:], bc(pC8[:, :, c], 64), ALU.mult, [Stmp, pC8], [St])
                    yield

                def drive(gens):
                    gens = [g for g in gens if g is not None]
                    while gens:
                        for g in list(gens):
                            try:
                                next(g)
                            except StopIteration:
                                gens.remove(g)

                drive([stage1(0)])
                stage2()
                gch = 0
                for tb in range(NTB):
                    tcs = slice(tb * 512, (tb + 1) * 512)
                    yn = ynbs[tb % 2]
                    s1 = stage1(tb + 1) if tb + 1 < NTB else None
                    RW = int(os.environ.get("K_RW", "0"))
                    if RW != 1:
                        drive([streamA(0, gch % 2)])
                    for c in range(8 if RW != 1 else 0):
                        ga_ = streamA(c + 1, (gch + 1) % 2) if c < 7 else None
                        gb_ = streamB(c, gch % 2, yn) if RW != 2 else None
                        gens = [g for g in (gb_, ga_) if g is not None]
                        while gens:
                            for g in list(gens):
                                try:
                                    next(g)
                                except StopIteration:
                                    gens.remove(g)
                            if s1 is not None:
                                try:
                                    next(s1)
                                except StopIteration:
                                    s1 = None
                        gch += 1
                    if s1 is not None:
                        drive([s1])
                    S.dma("pool", ybn[:, tcs].rearrange("(h p) t -> p h t", p=64),
                          yn[:].rearrange("p h c t -> p h (c t)"), reads=[yn])
                    if tb + 1 < NTB:
                        stage2()

        def phase_merge(l, xsrc, xdst):
            with S.phase():
                wb = S.sbuf([128, 3, 4, D], BF16, "wb")
                wo = S.sbuf([128, 8, D], BF16, "wo")
                stages = [S.sbuf([128, 2048], F32, f"mstg{i}") for i in range(2)]
                for br in range(3):
                    pieces = [(c0, 256, 4) for c0 in range(0, D, 256)]
                    load_w_bf16(wb, lambda p, br=br: wb[:, br, :, p[0]:p[0] + p[1]],
                                lambda p, br=br: w_br[l, br, :, p[0]:p[0] + p[1]].rearrange("(k q) n -> q k n", q=128),
                                pieces, stages)
                pieces = [(c0, 256, 8) for c0 in range(0, D, 256)]
                load_w_bf16(wo, lambda p: wo[:, :, p[0]:p[0] + p[1]],
                            lambda p: w_o[l, :, p[0]:p[0] + p[1]].rearrange("(k q) n -> q k n", q=128), pieces, stages)
                yas = [S.sbuf([128, 4, 512], BF16) for _ in range(2)]
                ycs = [S.sbuf([128, 4, 512], BF16) for _ in range(2)]
                ybs = [S.sbuf([128, 4, 512], BF16) for _ in range(2)]
                bos = [S.sbuf([128, 4, 512], BF16) for _ in range(2)]
                gos = [S.sbuf([128, 4, 512], BF16) for _ in range(2)]
                ybf = [S.sbuf([128, 4, 512], BF16) for _ in range(2)]
                gsets = [(S.sbuf([128, 512], BF16), S.sbuf([128, 512], F32), S.sbuf([128, 512], F32),
                          S.sbuf([128, 512], F32), S.sbuf([128, 512], F32)) for _ in range(2)]
                Gs = [S.sbuf([128, 24, 512], BF16) for _ in range(2)]
                mT = [S.sbuf([128, 8, 512], BF16) for _ in range(2)]
                m1 = [S.sbuf([128, 512], F32) for _ in range(2)]
                _m2 = S.sbuf([128, 512], F32)
                m2 = [_m2, _m2]
                xbufs = [S.sbuf([128, D], F32) for _ in range(2)]
                pbr = [S.psum([128, 512], F32) for _ in range(3)]
                pso = [S.psum([128, 512], F32) for _ in range(2)]
                pgn = [S.psum([128, 512], F32) for _ in range(3)]
                no = 0
                for i in range(NT):
                    tcs = slice(i * 512, (i + 1) * 512)
                    k = i % 2
                    ld = lambda dst, src: S.dma("sp", dst[:], src[:, tcs].rearrange("(c p) t -> p c t", p=128), writes=[dst])
                    ld(yas[k], yaT); ld(ycs[k], ycT); ld(ybs[k], ybn); ld(bos[k], bsc); ld(gos[k], gsc)
                    S.dma("sp", Gs[k][:], gT[:, tcs].rearrange("(c p) t -> p c t", p=128), writes=[Gs[k]])
                    for hp in range(4):
                        yv = ybs[k][:, hp, :]
                        gsqm, gmn, gdm, gm2m, gvm = gsets[hp % 2]
                        pg0 = pgn[(2 * hp) % 3]
                        pg1 = pgn[(2 * hp + 1) % 3]
                        S.A(gsqm[:], yv, AF.Square, [ybs[k]], [gsqm])
                        S.MM(pg0[:], bdmb[:], yv, True, True, [bdmb, ybs[k]], [pg0])
                        S.MM(pg1[:], bdmb[:], gsqm[:], True, True, [bdmb, gsqm], [pg1])
                        S.CP("dve", gmn[:], pg0[:], [pg0], [gmn])
                        S.TT("pool", gdm[:], yv, gmn[:], ALU.subtract, [ybs[k], gmn], [gdm])
                        S.A(gm2m[:], gmn[:], AF.Square, [gmn], [gm2m])
                        S.TT("dve", gvm[:], pg1[:], gm2m[:], ALU.subtract, [pg1, gm2m], [gvm])
                        S.A(gvm[:], gvm[:], AF.Sqrt, [gvm], [gvm], bias=64e-5)
                        S.op("dve", lambda e, gvm=gvm: e.reciprocal(out=gvm[:], in_=gvm[:]), [gvm], [gvm], dur=0.65)
                        S.TT("pool", gdm[:], gdm[:], gvm[:], ALU.mult, [gdm, gvm], [gdm])
                        S.TS("dve", gdm[:], gdm[:], pcol(l, "gng", hp), pcol(l, "gnb", hp), ALU.mult, ALU.add, [gdm], [gdm])
                        S.TT("pool", gdm[:], gdm[:], bos[k][:, hp, :], ALU.add, [gdm, bos[k]], [gdm])
                        S.TT("dve", ybf[k][:, hp, :], gdm[:], gos[k][:, hp, :], ALU.mult, [gdm, gos[k]], [ybf[k]])
                    ysrc = (yas[k], ybf[k], ycs[k])
                    for oc in range(8):
                        pp3 = [pbr[br] for br in range(3)]
                        for br in range(3):
                            for kc in range(4):
                                S.MM(pp3[br][:], wb[:, br, kc, oc * 128:(oc + 1) * 128], ysrc[br][:, kc, :], kc == 0, kc == 3,
                                     [wb, ysrc[br]], [pp3[br]], signal=(kc == 3))
                        a1, a2 = m1[oc % 2], m2[oc % 2]
                        S.TT("dve", a1[:], pp3[0][:], Gs[k][:, oc, :], ALU.mult, [pp3[0], Gs[k]], [a1])
                        S.TT("dve", a2[:], pp3[1][:], Gs[k][:, 8 + oc, :], ALU.mult, [pp3[1], Gs[k]], [a2])
                        S.TT("pool", a1[:], a1[:], a2[:], ALU.add, [a1, a2], [a1])
                        S.TT("dve", a2[:], pp3[2][:], Gs[k][:, 16 + oc, :], ALU.mult, [pp3[2], Gs[k]], [a2])
                        S.TT("pool", mT[k][:, oc, :], a1[:], a2[:], ALU.add, [a1, a2], [mT[k]])
                    for s in range(4):
                        xt = xbufs[no % 2]
                        r0 = (i * 4 + s) * 128
                        S.dma("sp", xt[:], xsrc[r0:r0 + 128, :], writes=[xt])
                        for half in range(2):
                            ps = pso[half]
                            for kc in range(8):
                                S.MM(ps[:], mT[k][:, kc, s * 128:(s + 1) * 128], wo[:, kc, half * 512:(half + 1) * 512],
                                     kc == 0, kc == 7, [mT[k], wo], [ps], signal=(kc == 7))
                            S.TT("dve", xt[:, half * 512:(half + 1) * 512], xt[:, half * 512:(half + 1) * 512], ps[:], ALU.add,
                                 [xt, ps], [xt])
                        S.dma("pool", xdst[r0:r0 + 128, :], xt[:], reads=[xt])
                        no += 1

        def phase_ffn(l, xsrc, xdst, final):
            with S.phase():
                wup = S.sbuf([128, 8, 2 * FF], BF16, "wup")
                wdn = S.sbuf([128, 22, D], BF16, "wdn")
                xbufs = [S.sbuf([128, D], F32, f"fx{i}") for i in range(3)]
                pieces = [(c0, 128, 8) for c0 in range(0, 2 * FF, 128)]
                load_w_bf16(wup, lambda p: wup[:, :, p[0]:p[0] + p[1]],
                            lambda p: w_up[l, :, p[0]:p[0] + p[1]].rearrange("(k q) n -> q k n", q=128), pieces, xbufs)
                ip = 0
                for kc0 in range(0, 22, 2):
                    for half in range(2):
                        st = xbufs[ip % 3]
                        sv = st[:, 0:1024].rearrange("p (k n) -> p k n", k=2)
                        S.dma("sp", sv, w_dn[l, kc0 * 128:(kc0 + 2) * 128, half * 512:(half + 1) * 512].rearrange(
                            "(k q) n -> q k n", q=128), writes=[st])
                        S.CP(("act", "dve")[ip % 2], wdn[:, kc0:kc0 + 2, half * 512:(half + 1) * 512], sv, [st], [wdn])
                        ip += 1
                _xs = S.sbuf([128, D], BF16)
                nbs = [(S.sbuf([128, 1], F32), S.sbuf([128, 1], F32), _xs, _xs, S.psum([128, 8, 128], BF16))]
                fxnTs = [S.sbuf([128, 8, 512], BF16, f"fxnT{i}") for i in range(2)]
                actT = S.sbuf([128, 22, 512], BF16, "actT")
                hg = [S.sbuf([128, 514], F32) for _ in range(2)]
                _hu = S.sbuf([128, 514], F32)
                hu = [_hu, _hu]
                cg = [S.sbuf([128, 512], F32) for _ in range(2)]
                _cu = S.sbuf([128, 512], F32)
                cu = [_cu, _cu]
                carry = S.sbuf([128, 44, 2], F32, "carry")
                S.MS("pool", carry[:], 0.0, [carry])
                gf = None
                if final:
                    gf = S.sbuf([128, D], F32, "gfin")
                    S.dma("sp", gf[:], gfin.partition_broadcast(128), writes=[gf])
                    fss = S.sbuf([128, 1], F32)
                    frs = S.sbuf([128, 1], F32)
                    fj = _xs
                pg = [S.psum([128, 512], F32) for _ in range(2)]
                pu = [S.psum([128, 512], F32) for _ in range(2)]
                pdn = [S.psum([128, 512], F32) for _ in range(2)]
                nx = 0
                for i in range(NT):
                    xnT = fxnTs[i % 2]
                    for s in range(4):
                        xt = xbufs[nx % 3]; nx += 1
                        r0 = (i * 4 + s) * 128
                        S.dma("sp", xt[:], xsrc[r0:r0 + 128, :], writes=[xt])
                        rmsnorm_T(xt, pcol(l, "n2g", 0, 8), xnT, s, nbs[0])
                    for j in range(22):
                        k = j % 2
                        for kc in range(8):
                            S.MM(pg[k][:], wup[:, kc, j * 128:(j + 1) * 128], xnT[:, kc, :], kc == 0, kc == 7, [wup, xnT], [pg[k]],
                                 signal=(kc == 7))
                        for kc in range(8):
                            S.MM(pu[k][:], wup[:, kc, FF + j * 128:FF + (j + 1) * 128], xnT[:, kc, :], kc == 0, kc == 7,
                                 [wup, xnT], [pu[k]], signal=(kc == 7))
                        for (hb, ps, cc, slot) in ((hg[k], pg[k], cg[k], j), (hu[k], pu[k], cu[k], 22 + j)):
                            S.CP("pool", hb[:, 0:2], carry[:, slot, :], [carry], [hb])
                            S.CP("act", hb[:, 2:514], ps[:], [ps], [hb])
                            S.CP("pool", carry[:, slot, :], hb[:, 512:514], [hb], [carry])
                            S.A(cc[:], hb[:, 2:514], AF.Copy, [hb], [cc], scale=pcol(l, "fcw", slot * 3 + 2))
                            S.STT("dve", cc[:], hb[:, 1:513], pcol(l, "fcw", slot * 3 + 1), cc[:], ALU.mult, ALU.add, [hb, cc], [cc])
                            S.STT("dve", cc[:], hb[:, 0:512], pcol(l, "fcw", slot * 3 + 0), cc[:], ALU.mult, ALU.add, [hb, cc], [cc])
                        S.A(cg[k][:], cg[k][:], AF.Silu, [cg[k]], [cg[k]])
                        S.TT("dve", actT[:, j, :], cg[k][:], cu[k][:], ALU.mult, [cg[k], cu[k]], [actT])
                    for s in range(4):
                        xt = xbufs[nx % 3]; nx += 1
                        r0 = (i * 4 + s) * 128
                        S.dma("sp", xt[:], xsrc[r0:r0 + 128, :], writes=[xt])
                        for half in range(2):
                            ps = pdn[half]
                            for j in range(22):
                                S.MM(ps[:], actT[:, j, s * 128:(s + 1) * 128], wdn[:, j, half * 512:(half + 1) * 512],
                                     j == 0, j == 21, [actT, wdn], [ps], signal=(j == 21))
                            S.TT("dve", xt[:, half * 512:(half + 1) * 512], xt[:, half * 512:(half + 1) * 512], ps[:], ALU.add,
                                 [xt, ps], [xt])
                        if final:
                            S.MS("pool", fss[:], 0.0, [fss])
                            S.A(fj[:], xt[:], AF.Square, [xt], [fj, fss], accum_out=fss[:])
                            S.A(frs[:], fss[:], AF.Sqrt, [fss], [frs], scale=1.0 / D, bias=1e-6)
                            S.op("dve", lambda e: e.reciprocal(out=frs[:], in_=frs[:]), [frs], [frs])
                            S.STT("dve", xt[:], xt[:], frs[:, 0:1], gf[:], ALU.mult, ALU.mult, [xt, frs, gf], [xt])
                        S.dma("pool", xdst[r0:r0 + 128, :], xt[:], reads=[xt])

        src = x_in
        phl = []
        for l in range(depth):
            last = (l == depth - 1)
            phl += [lambda l=l, src=src: phase_inproj(l, src),
                    lambda l=l: phase_attn(l), lambda l=l: phase_rwkv(l), lambda l=l, src=src: phase_merge(l, src, xa),
                    lambda l=l, last=last: phase_ffn(l, xa, y_out if last else xb, last)]
            src = xb
        for f in phl[:nph]:
            f()
        print("instructions:", S.ninstr)
    return nc


def host_consts():
    cst = np.zeros((128, 768), np.float32)
    cst[:, 0:128] = np.eye(128)
    k = np.arange(128)
    cst[:, 128:256] = (k[:, None] <= k[None, :])
    bd = np.zeros((128, 128), np.float32)
    bd[:64, :64] = 1
    bd[64:, 64:] = 1
    cst[:, 256:384] = bd
    cst[:, 384:512] = 1.0 / 64
    cst[:, 512:640] = 1.0
    s = np.arange(64)
    su = (s[:, None] < s[None, :]).astype(np.float32)
    ui = (s[:, None] <= s[None, :]).astype(np.float32)
    sl = (s[:, None] > s[None, :]).astype(np.float32)
    mk = np.zeros((64, 4, 8, 128), np.float32)
    mk[:, 0, :, 0:64] = su[:, None, :]
    mk[:, 0, :, 64:128] = ui[:, None, :]
    mk[:, 1, :, 0:64] = su[:, None, :]
    mk[:, 1, :, 64:128] = -ui[:, None, :]
    mk[:, 2, :, 0:64] = sl[:, None, :]
    mk[:, 2, :, 64:128] = np.eye(64, dtype=np.float32)[:, None, :]
    return cst, mk


def pack_params(inp, depth):
    pc = np.zeros((depth, 128, NPC), np.float32)

    def col(v):
        return np.ascontiguousarray(v.reshape(-1, 128).T)

    for l in range(depth):
        def put(name, arr):
            o = PCO[name]
            pc[l, :arr.shape[0], o:o + arr.shape[1]] = arr
        put("n1g", col(inp["norm1_g"][l]))
        put("n2g", col(inp["norm2_g"][l]))
        put("gateb", col(inp["gate_b"][l]))
        cm = inp["conv_mix_w"][l]
        put("cmw", np.ascontiguousarray(cm.reshape(3, 4, 128).transpose(2, 1, 0).reshape(128, 12)))
        fc = inp["ffn_conv_w"][l]
        put("fcw", np.ascontiguousarray(fc.reshape(3, 44, 128).transpose(2, 1, 0).reshape(128, 132)))
        put("mu", col(inp["rwkv_mu"][l]))
        put("w0", col(inp["rwkv_w0"][l]))
        put("a0", col(inp["rwkv_a0"][l]))
        put("kk", col(inp["rwkv_k_k"][l]))
        put("ka", col(inp["rwkv_k_a"][l]))
        put("rk", col(inp["rwkv_r_k"][l].reshape(-1)))
        put("fb", inp["attn_forget_b"][l].reshape(8, 1))
        put("gng8", np.ascontiguousarray(inp["rwkv_gn_g"][l].reshape(8, 64).T))
        put("gnb8", np.ascontiguousarray(inp["rwkv_gn_b"][l].reshape(8, 64).T))
        put("gng", col(inp["rwkv_gn_g"][l]))
        put("gnb", col(inp["rwkv_gn_b"][l]))
    return pc


_NC_CACHE = {}


def run(inputs, T, nb, depth=DEPTH, dbg=(), nph=99):
    inputs = {k: np.asarray(v) for k, v in inputs.items()}
    key = (T, depth, tuple(dbg), nph)
    if key not in _NC_CACHE:
        _NC_CACHE[key] = build(T, depth, dbg, nph)
    nc = _NC_CACHE[key]
    cst, mk = host_consts()
    pc = pack_params(inputs, depth)
    shared = {
        "w_in": inputs["w_in"][:depth], "w_branch": inputs["w_branch"][:depth], "w_o": inputs["w_o"][:depth],
        "ffn_w_up": inputs["ffn_w_up"][:depth], "ffn_w_down": inputs["ffn_w_down"][:depth],
        "rwkv_w_up": inputs["rwkv_w_up"][:depth], "rwkv_a_up": inputs["rwkv_a_up"][:depth],
        "rwkv_g_up": inputs["rwkv_g_up"][:depth], "pc": pc, "final_norm_g": inputs["final_norm_g"],
        "cst": cst, "mk": mk,
    }
    shared = {k: np.ascontiguousarray(v, dtype=np.float32) for k, v in shared.items()}
    in_maps = []
    for b in range(nb):
        m = dict(shared)
        m["x"] = np.ascontiguousarray(inputs["x"][b], dtype=np.float32)
        in_maps.append(m)
    res = run_bass_kernel_spmd(nc, in_maps, core_ids=list(range(nb)))
    return res.results


def kernel(**inputs):
    res = run(inputs, SEQ, NB)
    return np.stack([np.asarray(r["y"], dtype=np.float32) for r in res], axis=0)
```

```python
import contextlib
import math
import os
CUT = int(os.environ.get("K_CUT", "0"))
SUB = int(os.environ.get("K_SUB", "0"))
import numpy as np
import concourse.bass as bass
import concourse.mybir as mybir
from concourse.bass_utils import run_bass_kernel_spmd

F32 = mybir.dt.float32
BF16 = mybir.dt.bfloat16
AF = mybir.ActivationFunctionType
ALU = mybir.AluOpType

ENGS = ("pe", "act", "dve", "pool", "sp")

D = 1024
NIN = 7944
FF = 2816
SEQ = 8192
NB = 4
DEPTH = 2
DS = math.exp(-0.5)

PCO = {}
_o = 0
for _n, _w in (("n1g", 8), ("n2g", 8), ("gateb", 24), ("cmw", 12), ("fcw", 132), ("mu", 14),
               ("w0", 4), ("a0", 4), ("kk", 4), ("ka", 4), ("rk", 4), ("fb", 1), ("gng8", 8), ("gnb8", 8), ("gng", 4), ("gnb", 4)):
    PCO[_n] = _o
    _o += _w
NPC = _o


class Buf:
    __slots__ = ("t", "name", "last_w", "readers", "chan", "last_dma")

    def __init__(self, t, name):
        self.t = t
        self.name = name
        self.last_w = []
        self.readers = []
        self.chan = None
        self.last_dma = None

    def __getitem__(self, k):
        return self.t[k]


class Node:
    __slots__ = ("id", "eng", "fns", "deps", "dur", "occ", "kind", "chan", "ev", "open", "kw")

    def __init__(self, id, eng, kind):
        self.id = id
        self.eng = eng
        self.kind = kind
        self.fns = []
        self.deps = set()
        self.dur = 0.0
        self.occ = 0.0
        self.chan = None
        self.ev = None
        self.open = False


def _nfree(ap):
    n = 1
    for d in ap.shape[1:]:
        n *= d
    return n


class Sched:
    NCHAN = 64
    XLAT = float(os.environ.get("K_XLAT", "0.3"))

    def __init__(self, nc, stack):
        self.nc = nc
        self.gstack = stack
        self.stack = stack
        self.q = {e: [] for e in ENGS}
        self.cnt = {e: 0 for e in ENGS}
        self.sems = {}
        for e in ENGS:
            self.sems[e] = stack.enter_context(nc.semaphore("s_" + e))
        self.chan_cnt = {}
        for i in range(self.NCHAN):
            k = f"d{i}"
            self.sems[k] = stack.enter_context(nc.semaphore("s_" + k))
            self.chan_cnt[k] = 0
        self.chan_next = 0
        self.known = {e: {} for e in ENGS}
        self.nbuf = 0
        self.ninstr = 0
        self.nodes = []
        self.pe_open = None
        self.sb_bytes = 0
        self.reorder = True

    def sbuf(self, shape, dtype, name=None):
        self.nbuf += 1
        name = f"{name or 'sb'}_{self.nbuf}"
        nb = int(np.prod(shape[1:])) * (2 if dtype == BF16 else 4)
        self.sb_bytes += ((nb + 31) // 32) * 32
        return Buf(self.stack.enter_context(self.nc.sbuf_tensor(name, list(shape), dtype)), name)

    def psum(self, shape, dtype, name=None):
        self.nbuf += 1
        name = f"{name or 'ps'}_{self.nbuf}"
        return Buf(self.stack.enter_context(self.nc.psum_tensor(name, list(shape), dtype)), name)

    def _chan(self, b):
        if b.chan is None:
            assert self.chan_next < self.NCHAN, "out of dma channels"
            b.chan = f"d{self.chan_next}"
            self.chan_next += 1
        return b.chan

    def _close_pe(self):
        if self.pe_open is not None:
            self.pe_open.open = False
            self.pe_open = None

    def _deps_of(self, reads, writes):
        deps = set()
        for r in reads:
            deps.update(r.last_w)
        for w in writes:
            deps.update(w.last_w)
            deps.update(w.readers)
        return deps

    def _touch(self, node, reads, writes):
        for r in reads:
            if not r.readers or r.readers[-1] is not node:
                r.readers.append(node)
        for w in writes:
            w.last_w = [node]
            w.readers = []

    def op(self, eng, fn, reads=(), writes=(), signal=True, same_ok=False, dur=0.3):
        self.ninstr += 1
        deps = self._deps_of(reads, writes)
        if eng == "pe":
            node = self.pe_open
            if node is None:
                node = Node(len(self.nodes), "pe", "op")
                self.nodes.append(node)
                node.open = True
                self.pe_open = node
            deps.discard(node)
            node.deps |= deps
            node.fns.append(fn)
            node.dur += dur
            node.occ += dur
            if signal:
                self._close_pe()
        else:
            self._close_pe()
            node = Node(len(self.nodes), eng, "op")
            self.nodes.append(node)
            node.deps = deps
            node.fns.append(fn)
            node.dur = dur
            node.occ = dur
        self._touch(node, reads, writes)
        return node

    def dma(self, eng, out_ap, in_ap, reads=(), writes=(), **kw):
        self._close_pe()
        self.ninstr += 1
        cb = writes[0] if writes else reads[0]
        key = self._chan(cb)
        node = Node(len(self.nodes), eng, "dma")
        self.nodes.append(node)
        node.chan = key
        node.deps = self._deps_of(reads, writes)
        if cb.last_dma is not None:
            node.deps.add(cb.last_dma)
        cb.last_dma = node
        nbytes = _nfree(out_ap) * out_ap.shape[0] * (2 if out_ap.dtype == BF16 else 4)
        node.dur = 2.0 + nbytes / 1.0e5
        node.occ = 0.1 if eng == "sp" else 0.6
        node.fns.append((out_ap, in_ap, kw))
        self._touch(node, reads, writes)
        return node

    def _schedule(self):
        import heapq
        nodes = self.nodes
        if not self.reorder:
            return list(nodes)
        succ = {n.id: [] for n in nodes}
        indeg = {}
        alive = {n.id for n in nodes}
        for n in nodes:
            n.deps = {d for d in n.deps if d.id in alive and d.ev is None}
            indeg[n.id] = len(n.deps)
            for d in n.deps:
                succ[d.id].append(n)
        tail = {}
        for n in reversed(nodes):
            t = 0.0
            for s_ in succ[n.id]:
                v = tail[s_.id] + self.XLAT
                if v > t:
                    t = v
            tail[n.id] = t + n.dur
        est = {n.id: 0.0 for n in nodes}
        wait = {e: [] for e in ENGS}
        avail = {e: [] for e in ENGS}
        for n in nodes:
            if indeg[n.id] == 0:
                heapq.heappush(wait[n.eng], (0.0, n.id, n))
        free = {e: 0.0 for e in ENGS}
        order = []
        left = len(nodes)
        while left:
            best = None
            for e in ENGS:
                if avail[e]:
                    tc = free[e]
                elif wait[e]:
                    tc = max(free[e], wait[e][0][0])
                else:
                    continue
                if best is None or tc < best[0]:
                    best = (tc, e)
            assert best is not None, "dependency cycle in schedule"
            tc, e = best
            w = wait[e]
            av = avail[e]
            while w and w[0][0] <= tc:
                _, i_, n_ = heapq.heappop(w)
                heapq.heappush(av, (-tail[i_], i_, n_))
            _, _, n = heapq.heappop(av)
            free[e] = tc + n.occ
            fin = tc + n.dur
            order.append(n)
            left -= 1
            for s_ in succ[n.id]:
                if est[s_.id] < fin + self.XLAT:
                    est[s_.id] = fin + self.XLAT
                indeg[s_.id] -= 1
                if indeg[s_.id] == 0:
                    heapq.heappush(wait[s_.eng], (est[s_.id], s_.id, s_))
        return order

    def _need(self, eng, evs):
        kn = self.known[eng]
        out = {}
        for k, v in evs:
            if kn.get(k, 0) < v and out.get(k, 0) < v:
                out[k] = v
        for k, v in out.items():
            kn[k] = v
        return list(out.items())

    def _emit_waits(self, eng, waits):
        for k, v in waits:
            self.q[eng].append(lambda e, s=self.sems[k], v=v: e.wait_ge(s, v))

    def flush(self):
        self._close_pe()
        order = self._schedule()
        for n in order:
            evs = [d.ev for d in n.deps if d.ev is not None]
            eng = n.eng
            self._emit_waits(eng, self._need(eng, evs))
            if n.kind == "dma":
                key = n.chan
                self.chan_cnt[key] += 16
                n.ev = (key, self.chan_cnt[key])
                o, i, kw = n.fns[0]
                self.q[eng].append(
                    lambda e, o=o, i=i, s=self.sems[key], kw=kw: e.dma_start(out=o, in_=i, **kw).then_inc(s, 16))
            else:
                self.cnt[eng] += 1
                n.ev = (eng, self.cnt[eng])
                s = self.sems[eng]
                for fn in n.fns[:-1]:
                    self.q[eng].append(lambda e, fn=fn: fn(e))
                self.q[eng].append(lambda e, fn=n.fns[-1], s=s: fn(e).then_inc(s, 1))
        self.nodes = []

    def barrier(self):
        self.flush()
        deps = [(e, self.cnt[e]) for e in ENGS if self.cnt[e] > 0]
        deps += [(k, v) for k, v in self.chan_cnt.items() if v > 0]
        for e in ENGS:
            self._emit_waits(e, self._need(e, deps))

    def emit(self):
        nc = self.nc
        q = self.q
        with nc.Block() as block:
            @block.tensor
            def _(e):
                for f in q["pe"]:
                    f(e)

            @block.scalar
            def _(e):
                for f in q["act"]:
                    f(e)

            @block.vector
            def _(e):
                for f in q["dve"]:
                    f(e)

            @block.gpsimd
            def _(e):
                for f in q["pool"]:
                    f(e)

            @block.sync
            def _(e):
                for f in q["sp"]:
                    f(e)
        self.q = {e: [] for e in ENGS}

    @contextlib.contextmanager
    def phase(self, reorder=True, xlat=None):
        with contextlib.ExitStack() as ph:
            self.stack = ph
            self.chan_next = 0
            self.reorder = reorder
            self.XLAT = xlat if xlat is not None else Sched.XLAT
            yield
            self.barrier()
            self.emit()
        self.stack = self.gstack

    @staticmethod
    def _d(eng, ap):
        n = _nfree(ap)
        if eng == "act":
            return 0.22 + n / 1400.0
        if eng == "dve":
            return 0.12 + n / 1000.0
        return 0.3 + n / 600.0

    def A(self, out, in_, func, r, w, eng="act", **kw):
        return self.op("act", lambda e: e.activation(out=out, in_=in_, func=func, **kw), r, w, dur=self._d("act", out))

    def TT(self, eng, out, in0, in1, op, r, w):
        return self.op(eng, lambda e: e.tensor_tensor(out=out, in0=in0, in1=in1, op=op), r, w, dur=self._d(eng, out))

    def TS(self, eng, out, in0, s1, s2, op0, op1, r, w):
        if s2 is None:
            return self.op(eng, lambda e: e.tensor_scalar(out=out, in0=in0, scalar1=s1, scalar2=None, op0=op0), r, w,
                           dur=self._d(eng, out))
        return self.op(eng, lambda e: e.tensor_scalar(out=out, in0=in0, scalar1=s1, scalar2=s2, op0=op0, op1=op1), r, w,
                       dur=self._d(eng, out))

    def STT(self, eng, out, in0, sc, in1, op0, op1, r, w):
        return self.op(eng, lambda e: e.scalar_tensor_tensor(out=out, in0=in0, scalar=sc, in1=in1, op0=op0, op1=op1), r, w,
                       dur=self._d(eng, out))

    def CP(self, eng, out, in_, r, w):
        if eng == "act":
            return self.op("act", lambda e: e.copy(out=out, in_=in_), r, w, dur=self._d("act", out))
        return self.op(eng, lambda e: e.tensor_copy(out=out, in_=in_), r, w, dur=self._d(eng, out))

    def MS(self, eng, ap, val, w):
        return self.op(eng, lambda e: e.memset(ap, val), (), w, dur=self._d(eng, ap))

    def MM(self, out, lhsT, rhs, start, stop, r, w, signal=True):
        n = _nfree(rhs)
        d = 0.03 + n / 2400.0
        if rhs.dtype == F32:
            d *= 4
        return self.op("pe", lambda e: e.matmul(out, lhsT=lhsT, rhs=rhs, start=start, stop=stop), r, w,
                       signal=signal, dur=d)

    def TR(self, out, in_, ident, r, w, signal=True):
        return self.op("pe", lambda e: e.transpose(out=out, in_=in_, identity=ident), r, w, signal=signal, dur=0.1)


def bc(ap2, n):
    return ap2.unsqueeze(2).to_broadcast([ap2.shape[0], ap2.shape[1], n])


def build(T, depth=DEPTH, dbg=(), nph=99):
    nc = bass.Bass("TRN2", target_bir_lowering=False)
    NT = T // 512
    NKT = T // 128

    def dram(name, shape, dt, kind="Internal"):
        if name in dbg:
            kind = "ExternalOutput"
        return nc.dram_tensor(name, list(shape), dt, kind=kind).ap()

    x_in = dram("x", [T, D], F32, "ExternalInput")
    w_in = dram("w_in", [depth, D, NIN], F32, "ExternalInput")
    w_br = dram("w_branch", [depth, 3, 512, D], F32, "ExternalInput")
    w_o = dram("w_o", [depth, D, D], F32, "ExternalInput")
    w_up = dram("ffn_w_up", [depth, D, 2 * FF], F32, "ExternalInput")
    w_dn = dram("ffn_w_down", [depth, FF, D], F32, "ExternalInput")
    r_wup = dram("rwkv_w_up", [depth, 64, 512], F32, "ExternalInput")
    r_aup = dram("rwkv_a_up", [depth, 64, 512], F32, "ExternalInput")
    r_gup = dram("rwkv_g_up", [depth, 128, 512], F32, "ExternalInput")
    pcd = dram("pc", [depth, 128, NPC], F32, "ExternalInput")
    gfin = dram("final_norm_g", [D], F32, "ExternalInput")
    cst = dram("cst", [128, 128 * 6], F32, "ExternalInput")
    mk = dram("mk", [64, 4, 8, 128], F32, "ExternalInput")
    y_out = dram("y", [T, D], F32, "ExternalOutput")

    xa = dram("xa", [T, D], F32)
    xb = dram("xb", [T, D], F32)
    zc = dram("zc", [1536, T], BF16)
    zr = dram("zr", [1792, T], BF16)
    qT = dram("qT", [512, T], BF16)
    kT = dram("kT", [512, T], BF16)
    zf = dram("zf", [8, T], F32)
    gT = dram("gT", [3072, T], BF16)
    vtm = dram("vtm", [T, 528], BF16)
    cqk = dram("cqk", [8, 6, T], BF16)
    yaT = dram("yaT", [512, T], BF16)
    ycT = dram("ycT", [512, T], BF16)
    ybn = dram("ybn", [512, T], BF16)
    bsc = dram("bsc", [512, T], BF16)
    gsc = dram("gsc", [512, T], BF16)

    with contextlib.ExitStack() as gst:
        S = Sched(nc, gst)
        pcs = [S.sbuf([128, NPC], F32, f"pc{l}") for l in range(depth)]
        cf = S.sbuf([128, 768], F32, "cstf")
        identb = S.sbuf([128, 128], BF16, "identb")
        trib = S.sbuf([128, 128], BF16, "trib")
        o64b = S.sbuf([64, 64], BF16, "o64b")
        bdmb = S.sbuf([128, 128], BF16, "bdmb")
        omka = [S.sbuf([128, 4], F32, f"omka{l}") for l in range(depth)]
        nfb = [S.sbuf([128, 1], F32, f"nfb{l}") for l in range(depth)]
        with S.phase():
            for l in range(depth):
                S.dma("sp", pcs[l][:], pcd[l], writes=[pcs[l]])
            S.dma("sp", cf[:], cst, writes=[cf])
            S.CP("dve", identb[:], cf[:, 0:128], [cf], [identb])
            S.CP("dve", trib[:], cf[:, 128:256], [cf], [trib])
            S.CP("dve", o64b[:], cf[0:64, 384:448], [cf], [o64b])
            S.TS("dve", bdmb[:], cf[:, 256:384], 1.0 / 64, None, ALU.mult, None, [cf], [bdmb])
            for l in range(depth):
                o = PCO["ka"]
                S.TS("dve", omka[l][:], pcs[l][:, o:o + 4], -1.0, 1.0, ALU.mult, ALU.add, [pcs[l]], [omka[l]])
                o = PCO["fb"]
                S.TS("dve", nfb[l][:], pcs[l][:, o:o + 1], -1.0, None, ALU.mult, None, [pcs[l]], [nfb[l]])
        bdones = cf[:, 256:384]
        ones64 = cf[0:64, 384:448]
        onesf = cf[:, 512:640]

        def pcol(l, name, j=0, n=1):
            o = PCO[name] + j
            return pcs[l][:, o:o + n]

        def load_w_bf16(dst, dst_ap_fn, src_ap_fn, pieces, stages, engs=("act", "dve")):
            for i, pc_ in enumerate(pieces):
                st = stages[i % len(stages)]
                sv = st_view(st, pc_)
                S.dma("sp", sv, src_ap_fn(pc_), writes=[st])
                S.CP(engs[i % len(engs)], dst_ap_fn(pc_), sv, [st], [dst])

        def st_view(st, pc_):
            kc, n = pc_[2], pc_[1]
            return st[:, 0:kc * n].rearrange("p (k n) -> p k n", k=kc)

        def rmsnorm_T(xt, gcol, xnT, s, nb):
            ss, rs, junk, xs, pT = nb
            S.MS("pool", ss[:], 0.0, [ss])
            S.A(junk[:], xt[:], AF.Square, [xt], [junk, ss], accum_out=ss[:])
            S.A(rs[:], ss[:], AF.Sqrt, [ss], [rs], scale=1.0 / D, bias=1e-6)
            S.op("dve", lambda e: e.reciprocal(out=rs[:], in_=rs[:]), [rs], [rs])
            S.TS("dve", xs[:], xt[:], rs[:, 0:1], None, ALU.mult, None, [xt, rs], [xs])
            for kc in range(8):
                S.TR(pT[:, kc, :], xs[:, kc * 128:(kc + 1) * 128], identb[:], [xs, identb], [pT], signal=(kc == 7))
            S.TT("dve", xnT[:, :, s * 128:(s + 1) * 128], pT[:], bc(gcol, 128), ALU.mult, [pT], [xnT])

        def phase_inproj(l, xsrc):
            with S.phase():
                wres = S.sbuf([128, 8, NIN], BF16, "wres")
                stages = [S.sbuf([128, 2048], F32, f"stg{i}") for i in range(2)]
                pieces = [(c0, min(256, NIN - c0), 8) for c0 in range(0, NIN, 256)]
                load_w_bf16(wres, lambda p: wres[:, :, p[0]:p[0] + p[1]],
                            lambda p: w_in[l, :, p[0]:p[0] + p[1]].rearrange("(k q) n -> q k n", q=128),
                            pieces, stages)
                if CUT == 1:
                    return
                xbufs = [S.sbuf([128, D], F32, f"xb{i}") for i in range(2)]
                nbs = [(S.sbuf([128, 1], F32), S.sbuf([128, 1], F32), S.sbuf([128, D], BF16), S.sbuf([128, D], BF16),
                        S.psum([128, 8, 128], BF16)) for _ in range(2)]
                xnTs = [S.sbuf([128, 8, 512], BF16, f"xnT{i}") for i in range(2)]
                obs = [S.sbuf([128, 512], BF16, f"ob{i}") for i in range(4)]
                fsts = [S.sbuf([8, 512], F32, f"fst{i}") for i in range(2)]
                fe = S.sbuf([8, 512], F32, "fe")
                fcs = S.sbuf([8, 512], F32, "fcs")
                fr1 = S.sbuf([8, 512], F32, "fr1")
                fcar = S.sbuf([8, 1], F32, "fcar")
                fobs = [S.sbuf([8, 6, 512], BF16, f"fob{i}") for i in range(2)]
                S.MS("pool", fcar[:], 0.0, [fcar])
                vsts = [S.sbuf([128, 8, 66], BF16, f"vst{i}") for i in range(2)]
                pss = [S.psum([128, 512], F32, f"psm{i}") for i in range(4)]
                for v in vsts:
                    S.MS("pool", v[:], 1.0, [v])
                chunks = []
                for j in range(12):
                    chunks.append((j * 128, 128, zc, j * 128, "copy"))
                for j in range(14):
                    chunks.append((1536 + j * 128, 128, zr, j * 128, "copy"))
                for j in range(4):
                    chunks.append((3328 + j * 128, 128, qT, j * 128, "qscale"))
                for j in range(4):
                    chunks.append((3840 + j * 128, 128, kT, j * 128, "copyA"))
                chunks.append((4864, 8, zf, 0, "f"))
                for j in range(24):
                    chunks.append((4872 + j * 128, 128, gT, j * 128, "gate"))
                cnt = 0
                for i in range(NT):
                    xnT = xnTs[i % 2]
                    for s in range(4):
                        xt = xbufs[(i * 4 + s) % 2]
                        r0 = (i * 4 + s) * 128
                        S.dma("sp", xt[:], xsrc[r0:r0 + 128, :], writes=[xt])
                        rmsnorm_T(xt, pcol(l, "n1g", 0, 8), xnT, s, nbs[(i * 4 + s) % 2])
                    if CUT == 2:
                        continue
                    for (c0, M, dest, row0, mode) in (chunks[:2] if CUT == 3 else chunks):
                        ps = pss[cnt % 4]
                        for kc in range(8):
                            S.MM(ps[0:M, :], wres[:, kc, c0:c0 + M], xnT[:, kc, :], kc == 0, kc == 7,
                                 [wres, xnT], [ps], signal=(kc == 7))
                        if mode == "f":
                            fs = fsts[i % 2]
                            tsl = slice(i * 512, (i + 1) * 512)
                            S.CP("dve", fs[:], ps[0:8, :], [ps], [fs])
                            S.A(fe[:], fs[:], AF.Exp, [fs], [fe], scale=-1.0, bias=nfb[l][0:8, 0:1])
                            S.A(fe[:], fe[:], AF.Ln, [fe], [fe], bias=1.0)
                            S.op("dve", lambda e: e.tensor_tensor_scan(out=fcs[:], data0=fe[:], data1=fe[:], initial=0.0,
                                                                        op0=ALU.add, op1=ALU.bypass), [fe], [fcs], dur=1.2)
                            S.TS("dve", fcs[:], fcs[:], fcar[:, 0:1], None, ALU.add, None, [fcs, fcar], [fcs])
                            S.CP("dve", fcar[:], fcs[:, 511:512], [fcs], [fcar])
                            fo = fobs[i % 2]
                            S.A(fo[:, 0, :], fcs[:], AF.Copy, [fcs], [fo], scale=-1.0)
                            S.STT("dve", fr1[:], fcs[:], -1.0, fo[:, 0, :], ALU.mult, ALU.subtract, [fcs, fo], [fr1])
                            S.CP("act", fo[:, 1, :], fr1[:], [fr1], [fo])
                            S.TT("dve", fr1[:], fr1[:], fo[:, 1, :], ALU.subtract, [fr1, fo], [fr1])
                            S.CP("act", fo[:, 2, :], fr1[:], [fr1], [fo])
                            S.A(fo[:, 3:6, :], fo[:, 0:3, :], AF.Copy, [fo], [fo], scale=-1.0)
                            S.dma("pool", cqk[:, :, tsl], fo[:], reads=[fo])
                        else:
                            ob = obs[cnt % 4]
                            if mode == "copy":
                                S.CP("dve", ob[:], ps[:], [ps], [ob])
                            elif mode == "copyA":
                                S.CP("act", ob[:], ps[:], [ps], [ob])
                            elif mode == "qscale":
                                S.A(ob[:], ps[:], AF.Copy, [ps], [ob], scale=0.125)
                            else:
                                j = (c0 - 4872) // 128
                                S.A(ob[:], ps[:], AF.Sigmoid, [ps], [ob], bias=pcol(l, "gateb", j))
                            S.dma("pool", dest[row0:row0 + 128, i * 512:(i + 1) * 512], ob[:], reads=[ob])
                        cnt += 1
                    for s in range(4 if CUT not in (3, 4) else 0):
                        ps = pss[cnt % 4]
                        for kc in range(8):
                            S.MM(ps[:], xnT[:, kc, s * 128:(s + 1) * 128], wres[:, kc, 4352:4864], kc == 0, kc == 7,
                                 [wres, xnT], [ps], signal=(kc == 7))
                        vs = vsts[s % 2]
                        S.CP("act", vs[:, :, 0:64], ps[:].rearrange("p (h d) -> p h d", h=8), [ps], [vs])
                        r0 = (i * 4 + s) * 128
                        S.dma("pool", vtm[r0:r0 + 128, :], vs[:].rearrange("p h d -> p (h d)"), reads=[vs])
                        cnt += 1

        def phase_conv(l, own_phase=True):
            TB = min(T, 1024)
            with (S.phase() if own_phase else contextlib.nullcontext()):
                ins = [[S.sbuf([128, TB], BF16) for _ in range(3)] for _ in range(2)]
                hbs = [S.sbuf([128, TB + 2], F32) for _ in range(2)]
                acc = [S.sbuf([128, TB], F32) for _ in range(2)]
                outs = [S.sbuf([128, TB], BF16) for _ in range(2)]
                n = 0
                for j in range(4):
                    for tb in range(T // TB):
                        Bt, Ct, ht = ins[n % 2]
                        hb = hbs[n % 2]
                        ac = acc[n % 2]
                        ot = outs[n % 2]
                        cs = slice(tb * TB, (tb + 1) * TB)
                        S.dma("sp", Bt[:], zc[j * 128:(j + 1) * 128, cs], writes=[Bt])
                        S.dma("sp", Ct[:], zc[512 + j * 128:512 + (j + 1) * 128, cs], writes=[Ct])
                        S.dma("sp", ht[:], zc[1024 + j * 128:1024 + (j + 1) * 128, cs], writes=[ht])
                        if tb == 0:
                            S.MS("pool", hb[:, 0:2], 0.0, [hb])
                        else:
                            hp_ = hbs[(n - 1) % 2]
                            S.CP("pool", hb[:, 0:2], hp_[:, TB:TB + 2], [hp_], [hb])
                        S.TT("dve", hb[:, 2:TB + 2], Ct[:], ht[:], ALU.mult, [Ct, ht], [hb])
                        S.A(ac[:], hb[:, 2:TB + 2], AF.Copy, [hb], [ac], scale=pcol(l, "cmw", j * 3 + 2))
                        S.STT("dve", ac[:], hb[:, 1:TB + 1], pcol(l, "cmw", j * 3 + 1), ac[:], ALU.mult, ALU.add, [hb, ac], [ac])
                        S.STT("dve", ac[:], hb[:, 0:TB], pcol(l, "cmw", j * 3 + 0), ac[:], ALU.mult, ALU.add, [hb, ac], [ac])
                        S.TT("dve", ot[:], ac[:], Bt[:], ALU.mult, [ac, Bt], [ot])
                        S.dma("pool", yaT[j * 128:(j + 1) * 128, cs], ot[:], reads=[ot])
                        n += 1

        def phase_fcum(l):
            with S.phase():
                z = S.sbuf([8, T], F32)
                e1 = S.sbuf([8, T], F32)
                cs_ = S.sbuf([8, T], F32)
                r1 = z
                ob = [S.sbuf([8, T], BF16) for _ in range(6)]
                S.dma("sp", z[:], zf, writes=[z])
                S.A(e1[:], z[:], AF.Exp, [z], [e1], scale=-1.0, bias=nfb[l][0:8, 0:1])
                S.A(z[:], e1[:], AF.Ln, [e1], [z], bias=1.0)
                S.op("dve", lambda e: e.tensor_tensor_scan(out=cs_[:], data0=z[:], data1=z[:], initial=0.0,
                                                            op0=ALU.add, op1=ALU.bypass), [z], [cs_])
                S.A(ob[0][:], cs_[:], AF.Copy, [cs_], [ob[0]], scale=-1.0)
                S.STT("dve", r1[:], cs_[:], -1.0, ob[0][:], ALU.mult, ALU.subtract, [cs_, ob[0]], [r1])
                S.CP("act", ob[1][:], r1[:], [r1], [ob[1]])
                S.TT("dve", e1[:], r1[:], ob[1][:], ALU.subtract, [r1, ob[1]], [e1])
                S.CP("act", ob[2][:], e1[:], [e1], [ob[2]])
                for k in range(3):
                    S.A(ob[3 + k][:], ob[k][:], AF.Copy, [ob[k]], [ob[3 + k]], scale=-1.0)
                for k in range(6):
                    S.dma("pool", cqk[:, k, :], ob[k][:], reads=[ob[k]])

        def phase_attn(l):
            NQB = T // 512
            with S.phase():
                phase_conv(l, own_phase=False)
                vext = S.sbuf([128, NKT, 528], BF16, "vext")
                VS = min(8, NKT)
                for a in range(0, NKT, VS):
                    S.dma("sp", vext[:, a:a + VS, :], vtm[a * 128:(a + VS) * 128, :].rearrange("(n p) c -> p n c", p=128),
                          writes=[vext])
                qas = [S.sbuf([70, T], BF16, f"qa{i}") for i in range(2)]
                kas = [S.sbuf([70, T], BF16, f"ka{i}") for i in range(2)]
                NSL = int(os.environ.get("K_NSL", "5"))
                LOOK = int(os.environ.get("K_LOOK", "3"))
                pts = [S.sbuf([128, 512], BF16, f"pt{i}") for i in range(NSL)]
                osb = [S.sbuf([65, 512], F32, f"osb{i}") for i in range(2)]
                rdn = [S.sbuf([65, 512], F32, f"rdn{i}") for i in range(2)]
                yos = [S.sbuf([64, 512], BF16, f"yo{i}") for i in range(2)]
                psS = [S.psum([128, 512], F32, f"psS{i}") for i in range(NSL)]
                psO = [S.psum([128, 512], F32, f"psO{i}") for i in range(2)]
                psB = [S.psum([128, 512], F32, f"psB{i}") for i in range(1)]
                steps = [(qb, kt) for qb in range(NQB) for kt in range(4 * qb + 4)]
                nst = len(steps)
                gcnt = 0
                nq = 0
                for h in range(8):
                    qa = qas[h % 2]
                    ka = kas[h % 2]
                    S.dma("sp", qa[0:64, :], qT[h * 64:(h + 1) * 64, :], writes=[qa])
                    S.MS("pool", qa[64:70, :], 1.0, [qa])
                    S.dma("sp", qa[64:67, :], cqk[h, 0:3, :], writes=[qa])
                    S.dma("sp", ka[0:64, :], kT[h * 64:(h + 1) * 64, :], writes=[ka])
                    S.MS("pool", ka[64:70, :], 1.0, [ka])
                    S.dma("sp", ka[67:70, :], cqk[h, 3:6, :], writes=[ka])

                    def geom(i):
                        qb, kt = steps[i]
                        j = kt - 4 * qb
                        q0 = j * 128 if j > 0 else 0
                        return qb, kt, j, q0

                    def issue_S(i, qa=qa, ka=ka):
                        qb, kt, j, q0 = geom(i)
                        sp_ = psS[(gcnt + i) % NSL]
                        S.MM(sp_[:, q0:512], ka[:, kt * 128:(kt + 1) * 128], qa[:, qb * 512 + q0:(qb + 1) * 512],
                             True, True, [ka, qa], [sp_])

                    pending = []

                    def epi2(qb, ob, rd, yo, h=h):
                        pb = psB[0]
                        S.MM(pb[0:64, :], cf[64:65, 512:576], rd[64:65, :], True, True, [cf, rd], [pb])
                        S.TT("dve", yo[:], ob[0:64, :], pb[0:64, :], ALU.mult, [ob, pb], [yo])
                        S.dma("pool", ycT[h * 64:(h + 1) * 64, qb * 512:(qb + 1) * 512], yo[:], reads=[yo])

                    for i in range(min(LOOK, nst)):
                        issue_S(i)
                    for i0 in range(0, nst, 2):
                        pair = [i for i in (i0, i0 + 1) if i < nst]
                        for i in pair:
                            qb, kt, j, q0 = geom(i)
                            sp_ = psS[(gcnt + i) % NSL]
                            pt = pts[(gcnt + i) % NSL]
                            S.A(pt[:, q0:512], sp_[:, q0:512], AF.Exp, [sp_], [pt])
                            if j >= 0:
                                S.TT("dve", pt[:, q0:q0 + 128], pt[:, q0:q0 + 128], trib[:], ALU.mult, [pt, trib], [pt])
                        for i in pair:
                            if i + LOOK < nst:
                                issue_S(i + LOOK)
                        for i in pair:
                            qb, kt, j, q0 = geom(i)
                            nkt = 4 * qb + 4
                            pt = pts[(gcnt + i) % NSL]
                            ops_ = psO[(nq + qb) % 2]
                            extra = [pts[(gcnt + k) % NSL] for k in pair]
                            S.MM(ops_[0:65, q0:512], vext[:, kt, h * 66:h * 66 + 65], pt[:, q0:512],
                                 kt == 0, kt == nkt - 1, [vext] + extra, [ops_], signal=(kt == nkt - 1))
                            while pending and pending[0][0] <= i:
                                pending.pop(0)[1]()
                            if kt == nkt - 1:
                                ob = osb[(nq + qb) % 2]
                                rd = rdn[(nq + qb) % 2]
                                yo = yos[(nq + qb) % 2]
                                S.CP("dve", ob[:], ops_[0:65, :], [ops_], [ob])
                                S.op("dve", lambda e, rd=rd, ob=ob: e.reciprocal(out=rd[64:65, :], in_=ob[64:65, :]), [ob], [rd])
                                pending.append((i + 3, lambda qb=qb, ob=ob, rd=rd, yo=yo: epi2(qb, ob, rd, yo)))
                    for _, fn in pending:
                        fn()
                    gcnt += nst
                    nq += NQB

        def phase_rwkv(l):
            NTB = T // 512
            with S.phase(xlat=1.0):
                stg = S.sbuf([128, 512], F32, "rstg")
                waw = S.sbuf([128, 512], BF16, "waw")
                gup = S.sbuf([128, 512], BF16, "gup")
                S.dma("sp", stg[0:64, :], r_wup[l], writes=[stg])
                S.dma("sp", stg[64:128, :], r_aup[l], writes=[stg])
                S.CP("dve", waw[:], stg[:], [stg], [waw])
                S.dma("sp", stg[:], r_gup[l], writes=[stg])
                S.CP("dve", gup[:], stg[:], [stg], [gup])
                mkf = S.sbuf([64, 3, 8, 128], F32, "mkf")
                S.dma("sp", mkf[:], mk[:, 0:3], writes=[mkf])
                mG1 = mkf[:, 0]
                mG2 = mkf[:, 1]
                mL = mkf[:, 2, :, 0:64]
                id8 = mkf[:, 2, :, 64:128]
                St = S.sbuf([64, 8, 64], F32, "St")
                Sb = S.sbuf([64, 8, 64], BF16, "Sb")
                Stmp = S.sbuf([64, 8, 64], F32, "Stmp")
                S.MS("pool", St[:], 0.0, [St])
                S.MS("pool", Sb[:], 0.0, [Sb])
                EQ8 = S.sbuf([64, 8, 8, 2, 64], BF16, "EQ8")
                FB8 = S.sbuf([64, 8, 2, 512], BF16, "FB8")
                pC8 = S.sbuf([64, 8, 8], F32, "pC8")
                EQ = S.sbuf([128, 4, 8, 2, 64], BF16, "EQ")
                FB = S.sbuf([128, 4, 2, 512], BF16, "FB")
                Vb = S.sbuf([128, 4, 512], BF16, "Vb")
                pC = S.sbuf([128, 4, 8], F32, "pC")
                Ftm = S.sbuf([64, 8, 512], BF16, "Ftm")
                nBtm = S.sbuf([64, 8, 512], BF16, "nBtm")
                Vtm = S.sbuf([64, 8, 512], BF16, "Vtm")
                zls = [S.sbuf([128, 514], BF16, f"zl{i}") for i in range(3)]
                f32t = lambda n: S.sbuf([128, 512], F32, n)
                dtmp, twa, sgf, rr, kk_, vv, sig, csf = [f32t(n) for n in ("dtmp", "twa", "sgf", "rr", "kk", "vv", "sig", "csf")]
                csm, pp, pinv, pprev, aa, kap, ksq, rinv, kh, ktl, bet = [f32t(n) for n in (
                    "csm", "pp", "pinv", "pprev", "aa", "kap", "ksq", "rinv", "kh", "ktl", "bet")]
                t1 = dtmp
                rk = ksq
                twab = S.sbuf([128, 512], BF16, "twab")
                sgb = S.sbuf([128, 512], BF16, "sgb")
                offs = S.sbuf([128, 8], F32, "offs")
                gob = [S.sbuf([128, 512], BF16, f"gob{i}") for i in range(2)]
                bob = [S.sbuf([128, 512], BF16, f"bob{i}") for i in range(2)]
                G1bs = [S.sbuf([64, 8, 128], BF16, f"G1b{i}") for i in range(2)]
                G2bs = [S.sbuf([64, 8, 128], BF16, f"G2b{i}") for i in range(2)]
                Pfin = [S.sbuf([64, 8, 64], BF16, f"Pfin{i}") for i in range(2)]
                Ab = [S.sbuf([64, 8, 64], BF16, f"Ab{i}") for i in range(2)]
                Bb = [S.sbuf([64, 8, 64], BF16, f"Bb{i}") for i in range(2)]
                Pb = [S.sbuf([64, 8, 64], BF16, f"Pb{i}") for i in range(2)]
                IBn = S.sbuf([64, 8, 64], BF16, "IBn")
                Xb = S.sbuf([64, 8, 64], BF16, "Xb")
                Ub = S.sbuf([64, 8, 64], BF16, "Ub")
                Ycs = [S.sbuf([64, 512], F32, f"Yc{i}") for i in range(2)]
                ynbs = [S.sbuf([64, 8, 8, 64], BF16, f"ynb{i}") for i in range(2)]
                gsqb = S.sbuf([64, 512], BF16, "gsqb")
                Ycb = S.sbuf([64, 512], BF16, "Ycb")
                gd_ = S.sbuf([64, 512], F32, "gd")
                gm2 = S.sbuf([64, 512], F32, "gm2")
                gmean = S.sbuf([64, 512], F32, "gmean")
                gvar = S.sbuf([64, 512], F32, "gvar")
                QA, QB, QC, RA, RB, RC, W = [S.psum([128, 512], F32, f"rp{i}") for i in range(7)]
                PT = S.psum([64, 8, 128], BF16, "rpT")

                def v3(ap, a):
                    return ap.rearrange("p (a b) -> p a b", a=a)

                nzc = [0]

                def mix(dst, row0, mucol, tb):
                    zl = zls[nzc[0] % 3]
                    nzc[0] += 1
                    c0 = tb * 512
                    if tb == 0:
                        S.MS("pool", zl[:, 0:2], 0.0, [zl])
                        S.dma("sp", zl[:, 2:514], zr[row0:row0 + 128, 0:512], writes=[zl])
                    else:
                        S.dma("sp", zl[:, 1:514], zr[row0:row0 + 128, c0 - 1:c0 + 512], writes=[zl])
                    S.TT("pool", dtmp[:], zl[:, 1:513], zl[:, 2:514], ALU.subtract, [zl], [dtmp])
                    S.STT("dve", dst[:], dtmp[:], mucol, zl[:, 2:514], ALU.mult, ALU.add, [dtmp, zl], [dst])

                def stage1(tb):
                    tcs = slice(tb * 512, (tb + 1) * 512)
                    mix(twa, 1536, pcol(l, "mu", 12), tb)
                    S.A(twab[0:64, :], twa[0:64, :], AF.Tanh, [twa], [twab])
                    S.CP("act", twab[64:128, :], twa[64:128, :], [twa], [twab])
                    yield
                    mix(sgf, 1664, pcol(l, "mu", 13), tb)
                    S.A(sgb[:], sgf[:], AF.Sigmoid, [sgf], [sgb])
                    yield
                    for hp in range(4):
                        hs = slice(hp * 128, (hp + 1) * 128)
                        mix(rr, hp * 128, pcol(l, "mu", hp), tb)
                        yield
                        mix(kk_, 512 + hp * 128, pcol(l, "mu", 4 + hp), tb)
                        yield
                        mix(vv, 1024 + hp * 128, pcol(l, "mu", 8 + hp), tb)
                        yield
                        S.MM(W[:], waw[0:64, hs], twab[0:64, :], True, True, [waw, twab], [W])
                        S.A(sig[:], W[:], AF.Sigmoid, [W], [sig], bias=pcol(l, "w0", hp))
                        yield
                        S.MM(W[:], waw[64:128, hs], twab[64:128, :], True, True, [waw, twab], [W])
                        S.A(aa[:], W[:], AF.Sigmoid, [W], [aa], bias=pcol(l, "a0", hp))
                        yield
                        S.MM(W[:], gup[:, hs], sgb[:], True, True, [gup, sgb], [W])
                        go = gob[(tb * 4 + hp) % 2]
                        S.CP("act", go[:], W[:], [W], [go])
                        S.dma("pool", gsc[hs, tcs], go[:], reads=[go])
                        yield
                        S.op("dve", lambda e: e.tensor_tensor_scan(out=csf[:], data0=sig[:], data1=sig[:], initial=0.0,
                                                                    op0=ALU.add, op1=ALU.bypass), [sig], [csf])
                        S.MS("pool", offs[:, 0:1], 0.0, [offs])
                        yield
                        S.CP("pool", offs[:, 1:8], v3(csf[:], 8)[:, 0:7, 63], [csf], [offs])
                        yield
                        S.TT("dve", v3(csf[:], 8), v3(csf[:], 8), bc(offs[:, 0:8], 64), ALU.subtract, [csf, offs], [csf])
                        yield
                        S.TT("pool", csm[:], csf[:], sig[:], ALU.subtract, [csf, sig], [csm])
                        S.A(pp[:], csf[:], AF.Exp, [csf], [pp], scale=-DS)
                        yield
                        S.A(pinv[:], csf[:], AF.Exp, [csf], [pinv], scale=DS)
                        S.A(pprev[:], csm[:], AF.Exp, [csm], [pprev], scale=-DS)
                        S.CP("pool", pC[:, hp, :], v3(pp[:], 8)[:, :, 63], [pp], [pC])
                        yield
                        S.A(kap[:], kk_[:], AF.Copy, [kk_], [kap], scale=pcol(l, "kk", hp))
                        yield
                        S.A(ksq[:], kap[:], AF.Square, [kap], [ksq])
                        yield
                        S.MM(W[:], bdones, ksq[:], True, True, [cf, ksq], [W])
                        S.A(rinv[:], W[:], AF.Ln, [W], [rinv], bias=1e-12)
                        S.A(rinv[:], rinv[:], AF.Exp, [rinv], [rinv], scale=-0.5)
                        yield
                        S.TS("pool", t1[:], aa[:], pcol(l, "ka", hp), omka[l][:, hp:hp + 1], ALU.mult, ALU.add, [aa, omka[l]], [t1])
                        yield
                        S.TT("dve", kh[:], kap[:], rinv[:], ALU.mult, [kap, rinv], [kh])
                        S.TT("pool", ktl[:], kk_[:], t1[:], ALU.mult, [kk_, t1], [ktl])
                        yield
                        S.TT("pool", bet[:], aa[:], kh[:], ALU.mult, [aa, kh], [bet])
                        S.TT("dve", EQ[:, hp, :, 1, :], v3(rr[:], 8), v3(pp[:], 8), ALU.mult, [rr, pp], [EQ])
                        yield
                        S.TT("pool", EQ[:, hp, :, 0, :], v3(kh[:], 8), v3(pprev[:], 8), ALU.mult, [kh, pprev], [EQ])
                        S.TT("pool", FB[:, hp, 0, :], ktl[:], pinv[:], ALU.mult, [ktl, pinv], [FB])
                        yield
                        S.TT("pool", FB[:, hp, 1, :], bet[:], pinv[:], ALU.mult, [bet, pinv], [FB])
                        S.CP("act", Vb[:, hp, :], vv[:], [vv], [Vb])
                        S.STT("dve", rk[:], rr[:], pcol(l, "rk", hp), ktl[:], ALU.mult, ALU.mult, [rr, ktl], [rk])
                        yield
                        S.MM(W[:], bdones, rk[:], True, True, [cf, rk], [W])
                        bo = bob[(tb * 4 + hp) % 2]
                        S.TT("dve", bo[:], W[:], vv[:], ALU.mult, [W, vv], [bo])
                        S.dma("pool", bsc[hs, tcs], bo[:], reads=[bo])
                        yield

                def stage2():
                    for hp in range(4):
                        hs = slice(hp * 128, (hp + 1) * 128)
                        for (src, dstT, neg) in ((FB[:, hp, 0, :], Ftm, False), (FB[:, hp, 1, :], nBtm, True),
                                                 (Vb[:, hp, :], Vtm, False)):
                            for c in range(8):
                                S.TR(PT[:, c, :], src[:, c * 64:(c + 1) * 64], identb[:], [FB, Vb, identb], [PT],
                                     signal=(c == 7))
                            if neg:
                                S.A(dstT[:, :, hs], PT[:], AF.Copy, [PT], [dstT], scale=-1.0)
                            else:
                                S.CP("dve", dstT[:, :, hs], PT[:], [PT], [dstT])
                    for par in range(2):
                        ps_ = slice(par * 64, par * 64 + 64)
                        S.dma("sp", EQ8[:].rearrange("p (a two) c e t -> p a two (c e t)", two=2)[:, :, par, :],
                              EQ[ps_].rearrange("p a c e t -> p a (c e t)"), reads=[EQ], writes=[EQ8])
                        S.dma("sp", FB8[:].rearrange("p (a two) e t -> p a two (e t)", two=2)[:, :, par, :],
                              FB[ps_].rearrange("p a e t -> p a (e t)"), reads=[FB], writes=[FB8])
                        S.dma("sp", pC8[:].rearrange("p (a two) c -> p a two c", two=2)[:, :, par, :],
                              pC[ps_], reads=[pC], writes=[pC8])

                def streamA(c, par):
                    ccs = slice(c * 64, (c + 1) * 64)
                    G1b, G2b = G1bs[par], G2bs[par]
                    ga = [v3(QA[0:64, :], 4), v3(QB[0:64, :], 4)]
                    bank = [QA, QB]
                    for h in range(8):
                        eq = EQ8[:, h, c, :, :].rearrange("p a b -> p (a b)")
                        S.MM(ga[h // 4][:, h % 4, :], FB8[:, h, 0, ccs], eq, True, True, [FB8, EQ8], [bank[h // 4]],
                             signal=(h % 4 == 3))
                    yield
                    S.TT("dve", G1b[:, 0:4, :], ga[0], mG1[:, 0:4, :], ALU.mult, [QA, mkf], [G1b])
                    S.TT("dve", G1b[:, 4:8, :], ga[1], mG1[:, 4:8, :], ALU.mult, [QB, mkf], [G1b])
                    g3 = v3(QC[0:64, :], 8)
                    for h in range(8):
                        S.MM(g3[:, h, :], EQ8[:, h, c, 0, :], FB8[:, h, 1, ccs], True, True, [FB8, EQ8], [QC], signal=(h == 7))
                    yield
                    for h in range(8):
                        eq = EQ8[:, h, c, :, :].rearrange("p a b -> p (a b)")
                        S.MM(ga[h // 4][:, h % 4, :], FB8[:, h, 1, ccs], eq, True, True, [FB8, EQ8], [bank[h // 4]],
                             signal=(h % 4 == 3))
                    S.TT("dve", Bb[0][:], g3, mL, ALU.mult, [QC, mkf], [Bb[0]])
                    yield
                    S.TT("dve", G2b[:, 0:4, :], ga[0], mG2[:, 0:4, :], ALU.mult, [QA, mkf], [G2b])
                    S.TT("dve", G2b[:, 4:8, :], ga[1], mG2[:, 4:8, :], ALU.mult, [QB, mkf], [G2b])
                    yield
                    S.TT("dve", Pb[0][:], id8, G2b[:, :, 0:64], ALU.subtract, [mkf, G2b], [Pb[0]])
                    yield
                    pa, pb_, pq = v3(QA[0:64, :], 8), v3(QB[0:64, :], 8), v3(QC[0:64, :], 8)
                    for j in range(5):
                        Ac, Bc, Pc = Ab[j % 2], Bb[j % 2], Pb[j % 2]
                        An, Bn = Ab[(j + 1) % 2], Bb[(j + 1) % 2]
                        Pn = Pb[(j + 1) % 2] if j < 4 else Pfin[par]
                        if j == 0:
                            Acb, Acv = G2b, (lambda h: G2b[:, h, 0:64])
                        else:
                            Acb, Acv = Ac, (lambda h, Ac=Ac: Ac[:, h, :])
                        for h in range(8):
                            S.MM(pb_[:, h, :], Acv(h), Bc[:, h, :], True, True, [Acb, Bc], [QB], signal=(h == 7))
                        if j < 4:
                            for h in range(8):
                                S.MM(pa[:, h, :], Bc[:, h, :], Acv(h), True, True, [Acb, Bc], [QA], signal=(h == 7))
                        yield
                        S.CP("act", Bn[:], pb_, [QB], [Bn])
                        if j < 4:
                            S.CP("act", An[:], pa, [QA], [An])
                        yield
                        S.TT("dve", IBn[:], Bn[:], id8, ALU.add, [Bn, mkf], [IBn])
                        yield
                        for h in range(8):
                            S.MM(pq[:, h, :], IBn[:, h, :], Pc[:, h, :], True, True, [IBn, Pc], [QC], signal=(h == 7))
                        yield
                        S.CP("act", Pn[:], pq, [QC], [Pn])
                        yield

                def streamB(c, par, yn):
                    G1b, G2b, Pf = G1bs[par], G2bs[par], Pfin[par]
                    px = v3(RA[0:64, :], 8)
                    for h in range(8):
                        hc = slice(h * 64, (h + 1) * 64)
                        S.MM(px[:, h, :], EQ8[:, h, c, 0, :], Sb[:, h, :], True, False, [EQ8, Sb], [RA], signal=False)
                        S.MM(px[:, h, :], G1b[:, h, 0:64], Vtm[:, c, hc], False, True, [G1b, Vtm], [RA], signal=(h == 7))
                    yield
                    S.CP("act", Xb[:], px, [RA], [Xb])
                    yield
                    pu = v3(RB[0:64, :], 8)
                    for h in range(8):
                        S.MM(pu[:, h, :], Pf[:, h, :], Xb[:, h, :], True, True, [Pf, Xb], [RB], signal=(h == 7))
                    yield
                    S.CP("act", Ub[:], pu, [RB], [Ub])
                    yield
                    pS = v3(RA[0:64, :], 8)
                    for h in range(8):
                        hc = slice(h * 64, (h + 1) * 64)
                        S.MM(pS[:, h, :], Ftm[:, c, hc], Vtm[:, c, hc], True, False, [Ftm, Vtm], [RA], signal=False)
                        S.MM(pS[:, h, :], nBtm[:, c, hc], Ub[:, h, :], False, True, [nBtm, Ub], [RA], signal=(h == 7))
                    py = v3(RC[0:64, :], 8)
                    for h in range(8):
                        hc = slice(h * 64, (h + 1) * 64)
                        S.MM(py[:, h, :], Sb[:, h, :], EQ8[:, h, c, 1, :], True, False, [EQ8, Sb], [RC], signal=False)
                        S.MM(py[:, h, :], Vtm[:, c, hc], G1b[:, h, 64:128], False, False, [G1b, Vtm], [RC], signal=False)
                        S.MM(py[:, h, :], Ub[:, h, :], G2b[:, h, 64:128], False, True, [G2b, Ub], [RC], signal=(h == 7))
                    yield
                    S.TT("dve", Stmp[:], St[:], pS, ALU.add, [St, RA], [Stmp])
                    S.CP("act", yn[:, :, c, :], py, [RC], [yn])
                    yield
                    S.TT("dve", Sb[:], Stmp[:], bc(pC8[:, :, c], 64), ALU.mult, [Stmp, pC8], [Sb])
                    yield
                    S.TT("pool", St[:], Stmp[:], bc(pC8[:, :, c], 64), ALU.mult, [Stmp, pC8], [St])
                    yield

                def drive(gens):
                    gens = [g for g in gens if g is not None]
                    while gens:
                        for g in list(gens):
                            try:
                                next(g)
                            except StopIteration:
                                gens.remove(g)

                drive([stage1(0)])
                stage2()
                gch = 0
                for tb in range(NTB):
                    tcs = slice(tb * 512, (tb + 1) * 512)
                    yn = ynbs[tb % 2]
                    s1 = stage1(tb + 1) if tb + 1 < NTB else None
                    RW = int(os.environ.get("K_RW", "0"))
                    if RW != 1:
                        drive([streamA(0, gch % 2)])
                    for c in range(8 if RW != 1 else 0):
                        ga_ = streamA(c + 1, (gch + 1) % 2) if c < 7 else None
                        gb_ = streamB(c, gch % 2, yn) if RW != 2 else None
                        gens = [g for g in (gb_, ga_) if g is not None]
                        while gens:
                            for g in list(gens):
                                try:
                                    next(g)
                                except StopIteration:
                                    gens.remove(g)
                            if s1 is not None:
                                try:
                                    next(s1)
                                except StopIteration:
                                    s1 = None
                        gch += 1
                    if s1 is not None:
                        drive([s1])
                    S.dma("pool", ybn[:, tcs].rearrange("(h p) t -> p h t", p=64),
                          yn[:].rearrange("p h c t -> p h (c t)"), reads=[yn])
                    if tb + 1 < NTB:
                        stage2()

        def phase_merge(l, xsrc, xdst):
            with S.phase():
                wb = S.sbuf([128, 3, 4, D], BF16, "wb")
                wo = S.sbuf([128, 8, D], BF16, "wo")
                stages = [S.sbuf([128, 2048], F32, f"mstg{i}") for i in range(2)]
                for br in range(3):
                    pieces = [(c0, 256, 4) for c0 in range(0, D, 256)]
                    load_w_bf16(wb, lambda p, br=br: wb[:, br, :, p[0]:p[0] + p[1]],
                                lambda p, br=br: w_br[l, br, :, p[0]:p[0] + p[1]].rearrange("(k q) n -> q k n", q=128),
                                pieces, stages)
                pieces = [(c0, 256, 8) for c0 in range(0, D, 256)]
                load_w_bf16(wo, lambda p: wo[:, :, p[0]:p[0] + p[1]],
                            lambda p: w_o[l, :, p[0]:p[0] + p[1]].rearrange("(k q) n -> q k n", q=128), pieces, stages)
                yas = [S.sbuf([128, 4, 512], BF16) for _ in range(2)]
                ycs = [S.sbuf([128, 4, 512], BF16) for _ in range(2)]
                ybs = [S.sbuf([128, 4, 512], BF16) for _ in range(2)]
                bos = [S.sbuf([128, 4, 512], BF16) for _ in range(2)]
                gos = [S.sbuf([128, 4, 512], BF16) for _ in range(2)]
                ybf = [S.sbuf([128, 4, 512], BF16) for _ in range(2)]
                gsets = [(S.sbuf([128, 512], BF16), S.sbuf([128, 512], F32), S.sbuf([128, 512], F32),
                          S.sbuf([128, 512], F32), S.sbuf([128, 512], F32)) for _ in range(2)]
                Gs = [S.sbuf([128, 24, 512], BF16) for _ in range(2)]
                mT = [S.sbuf([128, 8, 512], BF16) for _ in range(2)]
                m1 = [S.sbuf([128, 512], F32) for _ in range(2)]
                _m2 = S.sbuf([128, 512], F32)
                m2 = [_m2, _m2]
                xbufs = [S.sbuf([128, D], F32) for _ in range(2)]
                pbr = [S.psum([128, 512], F32) for _ in range(3)]
                pso = [S.psum([128, 512], F32) for _ in range(2)]
                pgn = [S.psum([128, 512], F32) for _ in range(3)]
                no = 0
                for i in range(NT):
                    tcs = slice(i * 512, (i + 1) * 512)
                    k = i % 2
                    ld = lambda dst, src: S.dma("sp", dst[:], src[:, tcs].rearrange("(c p) t -> p c t", p=128), writes=[dst])
                    ld(yas[k], yaT); ld(ycs[k], ycT); ld(ybs[k], ybn); ld(bos[k], bsc); ld(gos[k], gsc)
                    S.dma("sp", Gs[k][:], gT[:, tcs].rearrange("(c p) t -> p c t", p=128), writes=[Gs[k]])
                    for hp in range(4):
                        yv = ybs[k][:, hp, :]
                        gsqm, gmn, gdm, gm2m, gvm = gsets[hp % 2]
                        pg0 = pgn[(2 * hp) % 3]
                        pg1 = pgn[(2 * hp + 1) % 3]
                        S.A(gsqm[:], yv, AF.Square, [ybs[k]], [gsqm])
                        S.MM(pg0[:], bdmb[:], yv, True, True, [bdmb, ybs[k]], [pg0])
                        S.MM(pg1[:], bdmb[:], gsqm[:], True, True, [bdmb, gsqm], [pg1])
                        S.CP("dve", gmn[:], pg0[:], [pg0], [gmn])
                        S.TT("pool", gdm[:], yv, gmn[:], ALU.subtract, [ybs[k], gmn], [gdm])
                        S.A(gm2m[:], gmn[:], AF.Square, [gmn], [gm2m])
                        S.TT("dve", gvm[:], pg1[:], gm2m[:], ALU.subtract, [pg1, gm2m], [gvm])
                        S.A(gvm[:], gvm[:], AF.Sqrt, [gvm], [gvm], bias=64e-5)
                        S.op("dve", lambda e, gvm=gvm: e.reciprocal(out=gvm[:], in_=gvm[:]), [gvm], [gvm], dur=0.65)
                        S.TT("pool", gdm[:], gdm[:], gvm[:], ALU.mult, [gdm, gvm], [gdm])
                        S.TS("dve", gdm[:], gdm[:], pcol(l, "gng", hp), pcol(l, "gnb", hp), ALU.mult, ALU.add, [gdm], [gdm])
                        S.TT("pool", gdm[:], gdm[:], bos[k][:, hp, :], ALU.add, [gdm, bos[k]], [gdm])
                        S.TT("dve", ybf[k][:, hp, :], gdm[:], gos[k][:, hp, :], ALU.mult, [gdm, gos[k]], [ybf[k]])
                    ysrc = (yas[k], ybf[k], ycs[k])
                    for oc in range(8):
                        pp3 = [pbr[br] for br in range(3)]
                        for br in range(3):
                            for kc in range(4):
                                S.MM(pp3[br][:], wb[:, br, kc, oc * 128:(oc + 1) * 128], ysrc[br][:, kc, :], kc == 0, kc == 3,
                                     [wb, ysrc[br]], [pp3[br]], signal=(kc == 3))
                        a1, a2 = m1[oc % 2], m2[oc % 2]
                        S.TT("dve", a1[:], pp3[0][:], Gs[k][:, oc, :], ALU.mult, [pp3[0], Gs[k]], [a1])
                        S.TT("dve", a2[:], pp3[1][:], Gs[k][:, 8 + oc, :], ALU.mult, [pp3[1], Gs[k]], [a2])
                        S.TT("pool", a1[:], a1[:], a2[:], ALU.add, [a1, a2], [a1])
                        S.TT("dve", a2[:], pp3[2][:], Gs[k][:, 16 + oc, :], ALU.mult, [pp3[2], Gs[k]], [a2])
                        S.TT("pool", mT[k][:, oc, :], a1[:], a2[:], ALU.add, [a1, a2], [mT[k]])
                    for s in range(4):
                        xt = xbufs[no % 2]
                        r0 = (i * 4 + s) * 128
                        S.dma("sp", xt[:], xsrc[r0:r0 + 128, :], writes=[xt])
                        for half in range(2):
                            ps = pso[half]
                            for kc in range(8):
                                S.MM(ps[:], mT[k][:, kc, s * 128:(s + 1) * 128], wo[:, kc, half * 512:(half + 1) * 512],
                                     kc == 0, kc == 7, [mT[k], wo], [ps], signal=(kc == 7))
                            S.TT("dve", xt[:, half * 512:(half + 1) * 512], xt[:, half * 512:(half + 1) * 512], ps[:], ALU.add,
                                 [xt, ps], [xt])
                        S.dma("pool", xdst[r0:r0 + 128, :], xt[:], reads=[xt])
                        no += 1

        def phase_ffn(l, xsrc, xdst, final):
            with S.phase():
                wup = S.sbuf([128, 8, 2 * FF], BF16, "wup")
                wdn = S.sbuf([128, 22, D], BF16, "wdn")
                xbufs = [S.sbuf([128, D], F32, f"fx{i}") for i in range(3)]
                pieces = [(c0, 128, 8) for c0 in range(0, 2 * FF, 128)]
                load_w_bf16(wup, lambda p: wup[:, :, p[0]:p[0] + p[1]],
                            lambda p: w_up[l, :, p[0]:p[0] + p[1]].rearrange("(k q) n -> q k n", q=128), pieces, xbufs)
                ip = 0
                for kc0 in range(0, 22, 2):
                    for half in range(2):
                        st = xbufs[ip % 3]
                        sv = st[:, 0:1024].rearrange("p (k n) -> p k n", k=2)
                        S.dma("sp", sv, w_dn[l, kc0 * 128:(kc0 + 2) * 128, half * 512:(half + 1) * 512].rearrange(
                            "(k q) n -> q k n", q=128), writes=[st])
                        S.CP(("act", "dve")[ip % 2], wdn[:, kc0:kc0 + 2, half * 512:(half + 1) * 512], sv, [st], [wdn])
                        ip += 1
                _xs = S.sbuf([128, D], BF16)
                nbs = [(S.sbuf([128, 1], F32), S.sbuf([128, 1], F32), _xs, _xs, S.psum([128, 8, 128], BF16))]
                fxnTs = [S.sbuf([128, 8, 512], BF16, f"fxnT{i}") for i in range(2)]
                actT = S.sbuf([128, 22, 512], BF16, "actT")
                hg = [S.sbuf([128, 514], F32) for _ in range(2)]
                hu = [S.sbuf([128, 514], F32) for _ in range(1 if final else 2)] * (2 if final else 1)
                cg = [S.sbuf([128, 512], F32) for _ in range(2)]
                cu = [S.sbuf([128, 512], F32) for _ in range(1 if final else 2)] * (2 if final else 1)
                carry = S.sbuf([128, 44, 2], F32, "carry")
                S.MS("pool", carry[:], 0.0, [carry])
                gf = None
                if final:
                    gf = S.sbuf([128, D], F32, "gfin")
                    S.dma("sp", gf[:], gfin.partition_broadcast(128), writes=[gf])
                    fss = S.sbuf([128, 1], F32)
                    frs = S.sbuf([128, 1], F32)
                    fj = _xs
                pg = [S.psum([128, 512], F32) for _ in range(2)]
                pu = [S.psum([128, 512], F32) for _ in range(2)]
                pdn = [S.psum([128, 512], F32) for _ in range(2)]
                nx = 0
                for i in range(NT):
                    xnT = fxnTs[i % 2]
                    for s in range(4):
                        xt = xbufs[nx % 3]; nx += 1
                        r0 = (i * 4 + s) * 128
                        S.dma("sp", xt[:], xsrc[r0:r0 + 128, :], writes=[xt])
                        rmsnorm_T(xt, pcol(l, "n2g", 0, 8), xnT, s, nbs[0])
                    for j in range(22):
                        k = j % 2
                        for kc in range(8):
                            S.MM(pg[k][:], wup[:, kc, j * 128:(j + 1) * 128], xnT[:, kc, :], kc == 0, kc == 7, [wup, xnT], [pg[k]],
                                 signal=(kc == 7))
                        for kc in range(8):
                            S.MM(pu[k][:], wup[:, kc, FF + j * 128:FF + (j + 1) * 128], xnT[:, kc, :], kc == 0, kc == 7,
                                 [wup, xnT], [pu[k]], signal=(kc == 7))
                        for (hb, ps, cc, slot) in ((hg[k], pg[k], cg[k], j), (hu[k], pu[k], cu[k], 22 + j)):
                            S.CP("pool", hb[:, 0:2], carry[:, slot, :], [carry], [hb])
                            S.CP("act", hb[:, 2:514], ps[:], [ps], [hb])
                            S.CP("pool", carry[:, slot, :], hb[:, 512:514], [hb], [carry])
                            S.A(cc[:], hb[:, 2:514], AF.Copy, [hb], [cc], scale=pcol(l, "fcw", slot * 3 + 2))
                            S.STT("dve", cc[:], hb[:, 1:513], pcol(l, "fcw", slot * 3 + 1), cc[:], ALU.mult, ALU.add, [hb, cc], [cc])
                            S.STT("dve", cc[:], hb[:, 0:512], pcol(l, "fcw", slot * 3 + 0), cc[:], ALU.mult, ALU.add, [hb, cc], [cc])
                        S.A(cg[k][:], cg[k][:], AF.Silu, [cg[k]], [cg[k]])
                        S.TT("dve", actT[:, j, :], cg[k][:], cu[k][:], ALU.mult, [cg[k], cu[k]], [actT])
                    for s in range(4):
                        xt = xbufs[nx % 3]; nx += 1
                        r0 = (i * 4 + s) * 128
                        S.dma("sp", xt[:], xsrc[r0:r0 + 128, :], writes=[xt])
                        for half in range(2):
                            ps = pdn[half]
                            for j in range(22):
                                S.MM(ps[:], actT[:, j, s * 128:(s + 1) * 128], wdn[:, j, half * 512:(half + 1) * 512],
                                     j == 0, j == 21, [actT, wdn], [ps], signal=(j == 21))
                            S.TT("dve", xt[:, half * 512:(half + 1) * 512], xt[:, half * 512:(half + 1) * 512], ps[:], ALU.add,
                                 [xt, ps], [xt])
                        if final:
                            S.MS("pool", fss[:], 0.0, [fss])
                            S.A(fj[:], xt[:], AF.Square, [xt], [fj, fss], accum_out=fss[:])
                            S.A(frs[:], fss[:], AF.Sqrt, [fss], [frs], scale=1.0 / D, bias=1e-6)
                            S.op("dve", lambda e: e.reciprocal(out=frs[:], in_=frs[:]), [frs], [frs])
                            S.STT("dve", xt[:], xt[:], frs[:, 0:1], gf[:], ALU.mult, ALU.mult, [xt, frs, gf], [xt])
                        S.dma("pool", xdst[r0:r0 + 128, :], xt[:], reads=[xt])

        src = x_in
        phl = []
        for l in range(depth):
            last = (l == depth - 1)
            phl += [lambda l=l, src=src: phase_inproj(l, src),
                    lambda l=l: phase_attn(l), lambda l=l: phase_rwkv(l), lambda l=l, src=src: phase_merge(l, src, xa),
                    lambda l=l, last=last: phase_ffn(l, xa, y_out if last else xb, last)]
            src = xb
        for f in phl[:nph]:
            f()
        print("instructions:", S.ninstr)
    return nc


def host_consts():
    cst = np.zeros((128, 768), np.float32)
    cst[:, 0:128] = np.eye(128)
    k = np.arange(128)
    cst[:, 128:256] = (k[:, None] <= k[None, :])
    bd = np.zeros((128, 128), np.float32)
    bd[:64, :64] = 1
    bd[64:, 64:] = 1
    cst[:, 256:384] = bd
    cst[:, 384:512] = 1.0 / 64
    cst[:, 512:640] = 1.0
    s = np.arange(64)
    su = (s[:, None] < s[None, :]).astype(np.float32)
    ui = (s[:, None] <= s[None, :]).astype(np.float32)
    sl = (s[:, None] > s[None, :]).astype(np.float32)
    mk = np.zeros((64, 4, 8, 128), np.float32)
    mk[:, 0, :, 0:64] = su[:, None, :]
    mk[:, 0, :, 64:128] = ui[:, None, :]
    mk[:, 1, :, 0:64] = su[:, None, :]
    mk[:, 1, :, 64:128] = -ui[:, None, :]
    mk[:, 2, :, 0:64] = sl[:, None, :]
    mk[:, 2, :, 64:128] = np.eye(64, dtype=np.float32)[:, None, :]
    return cst, mk


def pack_params(inp, depth):
    pc = np.zeros((depth, 128, NPC), np.float32)

    def col(v):
        return np.ascontiguousarray(v.reshape(-1, 128).T)

    for l in range(depth):
        def put(name, arr):
            o = PCO[name]
            pc[l, :arr.shape[0], o:o + arr.shape[1]] = arr
        put("n1g", col(inp["norm1_g"][l]))
        put("n2g", col(inp["norm2_g"][l]))
        put("gateb", col(inp["gate_b"][l]))
        cm = inp["conv_mix_w"][l]
        put("cmw", np.ascontiguousarray(cm.reshape(3, 4, 128).transpose(2, 1, 0).reshape(128, 12)))
        fc = inp["ffn_conv_w"][l]
        put("fcw", np.ascontiguousarray(fc.reshape(3, 44, 128).transpose(2, 1, 0).reshape(128, 132)))
        put("mu", col(inp["rwkv_mu"][l]))
        put("w0", col(inp["rwkv_w0"][l]))
        put("a0", col(inp["rwkv_a0"][l]))
        put("kk", col(inp["rwkv_k_k"][l]))
        put("ka", col(inp["rwkv_k_a"][l]))
        put("rk", col(inp["rwkv_r_k"][l].reshape(-1)))
        put("fb", inp["attn_forget_b"][l].reshape(8, 1))
        put("gng8", np.ascontiguousarray(inp["rwkv_gn_g"][l].reshape(8, 64).T))
        put("gnb8", np.ascontiguousarray(inp["rwkv_gn_b"][l].reshape(8, 64).T))
        put("gng", col(inp["rwkv_gn_g"][l]))
        put("gnb", col(inp["rwkv_gn_b"][l]))
    return pc


_NC_CACHE = {}


def run(inputs, T, nb, depth=DEPTH, dbg=(), nph=99):
    inputs = {k: np.asarray(v) for k, v in inputs.items()}
    key = (T, depth, tuple(dbg), nph)
    if key not in _NC_CACHE:
        _NC_CACHE[key] = build(T, depth, dbg, nph)
    nc = _NC_CACHE[key]
    cst, mk = host_consts()
    pc = pack_params(inputs, depth)
    shared = {
        "w_in": inputs["w_in"][:depth], "w_branch": inputs["w_branch"][:depth], "w_o": inputs["w_o"][:depth],
        "ffn_w_up": inputs["ffn_w_up"][:depth], "ffn_w_down": inputs["ffn_w_down"][:depth],
        "rwkv_w_up": inputs["rwkv_w_up"][:depth], "rwkv_a_up": inputs["rwkv_a_up"][:depth],
        "rwkv_g_up": inputs["rwkv_g_up"][:depth], "pc": pc, "final_norm_g": inputs["final_norm_g"],
        "cst": cst, "mk": mk,
    }
    shared = {k: np.ascontiguousarray(v, dtype=np.float32) for k, v in shared.items()}
    in_maps = []
    for b in range(nb):
        m = dict(shared)
        m["x"] = np.ascontiguousarray(inputs["x"][b], dtype=np.float32)
        in_maps.append(m)
    res = run_bass_kernel_spmd(nc, in_maps, core_ids=list(range(nb)))
    return res.results


def kernel(**inputs):
    res = run(inputs, SEQ, NB)
    return np.stack([np.asarray(r["y"], dtype=np.float32) for r in res], axis=0)
```
